# Optimizing a Trainium2 kernel written in Bass

```python
import jax, jax.numpy as jnp
from jax import lax
import numpy as np

D_MODEL = 2048
BATCH = 4
SEQ = 4096
DEPTH = 4

N_META = 16
NORM_EPS = 1e-6
ROPE_BASE = 10000.0

RW_HEADS = 16
RW_HEAD_DIM = 64
RW_WIDTH = RW_HEADS * RW_HEAD_DIM
RW_DECAY_LORA = 96
RW_A_LORA = 96
RW_GATE_LORA = 256
RW_COLS = 3 * RW_WIDTH + RW_DECAY_LORA + RW_A_LORA + RW_GATE_LORA
RW_GN_EPS = RW_HEAD_DIM * 1e-5

RET_HEADS = 8
RET_HEAD_DIM = 128
RET_WIDTH = RET_HEADS * RET_HEAD_DIM
RET_CHUNK = 128
RET_COLS = 4 * RET_WIDTH

MLA_HEADS = 8
MLA_NOPE = 128
MLA_ROPE = 64
MLA_V = 128
MLA_Q_RANK = 512
MLA_KV_RANK = 256
MLA_WIDTH = MLA_HEADS * MLA_V
MLA_COLS = MLA_Q_RANK + MLA_KV_RANK + MLA_ROPE
ATTN_BLOCK = 128

N_BRANCH = 3
GATE_COLS = N_BRANCH * D_MODEL
IN_COLS = RW_COLS + RET_COLS + MLA_COLS + GATE_COLS

FFN_HIDDEN = -(-8 * D_MODEL // (3 * 256)) * 256

kernel_name = 'hybrid_rwkv7_retention_mla_trunk'

F32 = jnp.float32


def split_last(x, sizes):
    idx, acc = [], 0
    for s in sizes[:-1]:
        acc += s
        idx.append(acc)
    return jnp.split(x, idx, axis=-1)


def rms_norm(x, g, eps=NORM_EPS):
    xf = x.astype(F32)
    y = xf * lax.rsqrt(jnp.mean(xf * xf, axis=-1, keepdims=True) + eps)
    return (y * g.astype(F32)).astype(x.dtype)


def rope_tables(positions, dim):
    inv = ROPE_BASE ** (-jnp.arange(0, dim, 2, dtype=F32) / dim)
    ang = positions.astype(F32)[:, None] * inv[None, :]
    return jnp.cos(ang), jnp.sin(ang)


def apply_rope(x, cos, sin):
    xf = x.astype(F32)
    half = x.shape[-1] // 2
    x1, x2 = xf[..., :half], xf[..., half:]
    return jnp.concatenate([x1 * cos - x2 * sin, x2 * cos + x1 * sin], axis=-1).astype(x.dtype)


def rwkv7_mix(z, mu, w0, w_up, a0, a_up, g_up, k_k, k_a, r_k, ln_w, ln_b):
    B, L, _ = z.shape
    H, N = RW_HEADS, RW_HEAD_DIM
    z_prev = jnp.pad(z, ((0, 0), (1, 0), (0, 0)))[:, :-1]
    z = z + (z_prev - z) * mu
    r, k, v, wd, ad, gd = split_last(z, [RW_WIDTH, RW_WIDTH, RW_WIDTH, RW_DECAY_LORA, RW_A_LORA, RW_GATE_LORA])
    w_log = -jax.nn.softplus(-(w0 + jnp.tanh(wd) @ w_up).astype(F32)) - 0.5
    decay = jnp.exp(-jnp.exp(w_log))
    a = jax.nn.sigmoid(a0 + ad @ a_up)
    g = jax.nn.sigmoid(gd) @ g_up
    kk = (k * k_k).astype(F32).reshape(B, L, H, N)
    kk = kk / jnp.maximum(jnp.sqrt(jnp.sum(kk * kk, axis=-1, keepdims=True)), 1e-12)
    k = k * (1.0 + (a - 1.0) * k_a)

    def heads_t(t):
        return jnp.moveaxis(t.astype(F32).reshape(B, L, H, N), 1, 0)

    xs = (heads_t(r), heads_t(decay), heads_t(k), heads_t(v), jnp.moveaxis(kk, 1, 0), heads_t(a))

    def step(S, inp):
        r_t, w_t, k_t, v_t, kk_t, a_t = inp
        s_kk = jnp.einsum('bhvk,bhk->bhv', S, kk_t)
        S = (S * w_t[:, :, None, :] - s_kk[..., None] * (kk_t * a_t)[:, :, None, :]
             + v_t[..., None] * k_t[:, :, None, :])
        return S, jnp.einsum('bhvk,bhk->bhv', S, r_t)

    S0 = jnp.zeros((B, H, N, N), F32)
    _, y = lax.scan(step, S0, xs)
    y = jnp.moveaxis(y, 0, 1)
    mean = jnp.mean(y, axis=-1, keepdims=True)
    var = jnp.mean(jnp.square(y - mean), axis=-1, keepdims=True)
    y = (y - mean) * lax.rsqrt(var + RW_GN_EPS)
    y = y * ln_w.astype(F32).reshape(H, N) + ln_b.astype(F32).reshape(H, N)
    rh = r.astype(F32).reshape(B, L, H, N)
    kh = k.astype(F32).reshape(B, L, H, N)
    vh = v.astype(F32).reshape(B, L, H, N)
    bonus = jnp.sum(rh * kh * r_k.astype(F32).reshape(H, N), axis=-1, keepdims=True) * vh
    y = (y + bonus).reshape(B, L, RW_WIDTH).astype(z.dtype)
    return y * g


def _ret_chunk(R, q, k, v, log_gamma):
    C = q.shape[1]
    idx = jnp.arange(C, dtype=F32)
    diff = idx[:, None] - idx[None, :]
    causal = diff >= 0
    dmat = jnp.where(causal[None], jnp.exp(log_gamma[:, None, None] * jnp.maximum(diff, 0.0)[None]), 0.0)
    s = jnp.einsum('bihd,bjhd->bhij', q, k) * dmat[None]
    o = jnp.einsum('bhij,bjhe->bihe', s, v)
    cross_decay = jnp.exp(log_gamma[None, :] * (idx[:, None] + 1.0))
    o = o + jnp.einsum('bihd,bhde->bihe', q, R) * cross_decay[None, :, :, None]
    k_decay = jnp.exp(log_gamma[None, :] * (C - 1.0 - idx)[:, None])
    R = (R * jnp.exp(log_gamma * C)[None, :, None, None]
         + jnp.einsum('bjhd,bjhe->bhde', k * k_decay[None, :, :, None], v))
    return R, o


def retention_mix(z, cos, sin):
    B, L, _ = z.shape
    H, d = RET_HEADS, RET_HEAD_DIM
    q, k, v, g = split_last(z, [RET_WIDTH] * 4)
    q = apply_rope(q.reshape(B, L, H, d), cos, sin).astype(F32)
    k = apply_rope(k.reshape(B, L, H, d), cos, sin).astype(F32) * (d ** -0.5)
    v = v.reshape(B, L, H, d).astype(F32)
    log_gamma = jnp.log1p(-jnp.exp2(-5.0 - jnp.arange(H, dtype=F32)))
    R0 = jnp.zeros((B, H, d, d), F32)
    R, o_meta = _ret_chunk(R0, q[:, :N_META], k[:, :N_META], v[:, :N_META], log_gamma)
    n_chunks = (L - N_META) // RET_CHUNK

    def chunks(t):
        return jnp.moveaxis(t[:, N_META:].reshape(B, n_chunks, RET_CHUNK, H, d), 1, 0)

    _, o_real = lax.scan(lambda c, xs: _ret_chunk(c, xs[0], xs[1], xs[2], log_gamma), R,
                         (chunks(q), chunks(k), chunks(v)))
    o_real = jnp.moveaxis(o_real, 0, 1).reshape(B, L - N_META, H, d)
    o = jnp.concatenate([o_meta, o_real], axis=1)
    o = o * lax.rsqrt(jnp.mean(o * o, axis=-1, keepdims=True) + NORM_EPS)
    o = o.reshape(B, L, RET_WIDTH).astype(z.dtype)
    return jax.nn.silu(g) * o


def _attend(qb, k, v, q_pos, scale):
    s = jnp.einsum('bqhd,bkhd->bhqk', qb, k).astype(F32) * scale
    mask = jnp.arange(k.shape[1])[None, :] <= q_pos[:, None]
    s = jnp.where(mask[None, None], s, -jnp.inf)
    p = jax.nn.softmax(s, axis=-1).astype(v.dtype)
    return jnp.einsum('bhqk,bkhd->bqhd', p, v)


def mla_mix(qd, kvd, krd, norm_q, norm_kv, w_uq, w_ukv, cos, sin):
    B, L, _ = qd.shape
    H = MLA_HEADS
    q = (rms_norm(qd, norm_q) @ w_uq).reshape(B, L, H, MLA_NOPE + MLA_ROPE)
    q = jnp.concatenate([q[..., :MLA_NOPE], apply_rope(q[..., MLA_NOPE:], cos[:, None], sin[:, None])], axis=-1)
    kv = (rms_norm(kvd, norm_kv) @ w_ukv).reshape(B, L, H, MLA_NOPE + MLA_V)
    k_nope, v = kv[..., :MLA_NOPE], kv[..., MLA_NOPE:]
    k_rope = apply_rope(krd, cos, sin)
    k = jnp.concatenate([k_nope, jnp.broadcast_to(k_rope[:, :, None, :], (B, L, H, MLA_ROPE))], axis=-1)
    scale = (MLA_NOPE + MLA_ROPE) ** -0.5
    o_meta = _attend(q[:, :N_META], k[:, :N_META], v[:, :N_META], jnp.arange(N_META), scale)
    n_blocks = (L - N_META) // ATTN_BLOCK
    q_blocks = jnp.moveaxis(q[:, N_META:].reshape(B, n_blocks, ATTN_BLOCK, H, MLA_NOPE + MLA_ROPE), 1, 0)
    o_real = lax.map(lambda a: _attend(a[0], k, v, N_META + a[1] * ATTN_BLOCK + jnp.arange(ATTN_BLOCK), scale),
                     (q_blocks, jnp.arange(n_blocks)))
    o_real = jnp.moveaxis(o_real, 0, 1).reshape(B, L - N_META, H, MLA_V)
    return jnp.concatenate([o_meta, o_real], axis=1).reshape(B, L, MLA_WIDTH)


def hybrid_layer(h, cos_ret, sin_ret, cos_mla, sin_mla, norm_mix, w_in, rw_mu, rw_w0, rw_w_up, rw_a0,
                 rw_a_up, rw_g_up, rw_k_k, rw_k_a, rw_r_k, rw_ln_w, rw_ln_b, mla_norm_q, mla_norm_kv,
                 mla_w_uq, mla_w_ukv, w_br_rwkv, w_br_ret, w_br_mla, w_out, norm_ffn, w_gate_up, w_down):
    u = rms_norm(h, norm_mix)
    p = u @ w_in
    z_rw, z_ret, qd, kvd, krd, gates = split_last(p, [RW_COLS, RET_COLS, MLA_Q_RANK, MLA_KV_RANK, MLA_ROPE, GATE_COLS])
    y_a = rwkv7_mix(z_rw, rw_mu, rw_w0, rw_w_up, rw_a0, rw_a_up, rw_g_up, rw_k_k, rw_k_a, rw_r_k, rw_ln_w, rw_ln_b)
    y_b = retention_mix(z_ret, cos_ret, sin_ret)
    y_c = mla_mix(qd, kvd, krd, mla_norm_q, mla_norm_kv, mla_w_uq, mla_w_ukv, cos_mla, sin_mla)
    g_a, g_b, g_c = jnp.split(jax.nn.sigmoid(gates), N_BRANCH, axis=-1)
    merged = g_a * (y_a @ w_br_rwkv) + g_b * (y_b @ w_br_ret) + g_c * (y_c @ w_br_mla)
    h = h + merged @ w_out
    u = rms_norm(h, norm_ffn)
    hg, hu = jnp.split(u @ w_gate_up, 2, axis=-1)
    return h + (jax.nn.silu(hg) * hu) @ w_down


def setup_inputs(seed: int = 0) -> dict:
    key = jax.random.key(seed)
    ks = iter(jax.random.split(key, 40))

    def nrm(shape, scale):
        return jax.random.normal(next(ks), shape, F32) * scale

    def gain(shape):
        return 1.0 + nrm(shape, 0.02)

    L_ = DEPTH
    return {
        'x': nrm((BATCH, SEQ, D_MODEL), 1.0),
        'meta_tokens': nrm((N_META, D_MODEL), 1.0),
        'norm_mix': gain((L_, D_MODEL)),
        'w_in': nrm((L_, D_MODEL, IN_COLS), D_MODEL ** -0.5),
        'rw_mu': jax.random.uniform(next(ks), (L_, RW_COLS), F32),
        'rw_w0': jax.random.uniform(next(ks), (L_, RW_WIDTH), F32, minval=-6.0, maxval=1.0),
        'rw_w_up': nrm((L_, RW_DECAY_LORA, RW_WIDTH), RW_DECAY_LORA ** -0.5),
        'rw_a0': nrm((L_, RW_WIDTH), 0.1),
        'rw_a_up': nrm((L_, RW_A_LORA, RW_WIDTH), RW_A_LORA ** -0.5),
        'rw_g_up': nrm((L_, RW_GATE_LORA, RW_WIDTH), RW_GATE_LORA ** -0.5),
        'rw_k_k': 0.85 + nrm((L_, RW_WIDTH), 0.02),
        'rw_k_a': 1.0 + nrm((L_, RW_WIDTH), 0.02),
        'rw_r_k': nrm((L_, RW_WIDTH), 0.1),
        'rw_ln_w': gain((L_, RW_WIDTH)),
        'rw_ln_b': nrm((L_, RW_WIDTH), 0.02),
        'mla_norm_q': gain((L_, MLA_Q_RANK)),
        'mla_norm_kv': gain((L_, MLA_KV_RANK)),
        'mla_w_uq': nrm((L_, MLA_Q_RANK, MLA_HEADS * (MLA_NOPE + MLA_ROPE)), MLA_Q_RANK ** -0.5),
        'mla_w_ukv': nrm((L_, MLA_KV_RANK, MLA_HEADS * (MLA_NOPE + MLA_V)), MLA_KV_RANK ** -0.5),
        'w_br_rwkv': nrm((L_, RW_WIDTH, D_MODEL), RW_WIDTH ** -0.5),
        'w_br_ret': nrm((L_, RET_WIDTH, D_MODEL), RET_WIDTH ** -0.5),
        'w_br_mla': nrm((L_, MLA_WIDTH, D_MODEL), MLA_WIDTH ** -0.5),
        'w_out': nrm((L_, D_MODEL, D_MODEL), D_MODEL ** -0.5),
        'norm_ffn': gain((L_, D_MODEL)),
        'w_gate_up': nrm((L_, D_MODEL, 2 * FFN_HIDDEN), D_MODEL ** -0.5),
        'w_down': nrm((L_, FFN_HIDDEN, D_MODEL), FFN_HIDDEN ** -0.5),
        'final_norm': gain((D_MODEL,)),
    }


def reference(x, meta_tokens, norm_mix, w_in, rw_mu, rw_w0, rw_w_up, rw_a0, rw_a_up, rw_g_up, rw_k_k,
              rw_k_a, rw_r_k, rw_ln_w, rw_ln_b, mla_norm_q, mla_norm_kv, mla_w_uq, mla_w_ukv, w_br_rwkv,
              w_br_ret, w_br_mla, w_out, norm_ffn, w_gate_up, w_down, final_norm):
    B = x.shape[0]
    meta = jnp.broadcast_to(meta_tokens[None].astype(x.dtype), (B, N_META, x.shape[-1]))
    h = jnp.concatenate([meta, x], axis=1)
    pos = jnp.arange(h.shape[1])
    cos_ret, sin_ret = rope_tables(pos, RET_HEAD_DIM)
    cos_ret, sin_ret = cos_ret[:, None, :], sin_ret[:, None, :]
    cos_mla, sin_mla = rope_tables(pos, MLA_ROPE)
    for l in range(DEPTH):
        h = hybrid_layer(h, cos_ret, sin_ret, cos_mla, sin_mla, norm_mix[l], w_in[l], rw_mu[l], rw_w0[l],
                         rw_w_up[l], rw_a0[l], rw_a_up[l], rw_g_up[l], rw_k_k[l], rw_k_a[l], rw_r_k[l],
                         rw_ln_w[l], rw_ln_b[l], mla_norm_q[l], mla_norm_kv[l], mla_w_uq[l], mla_w_ukv[l],
                         w_br_rwkv[l], w_br_ret[l], w_br_mla[l], w_out[l], norm_ffn[l], w_gate_up[l], w_down[l])
    return rms_norm(h[:, N_META:], final_norm)
```

```python
import math
from contextlib import ExitStack, contextmanager
import numpy as np
import ml_dtypes
import concourse.bass as bass
import concourse.mybir as mybir
from concourse.bass_utils import run_bass_kernel_spmd

F32 = mybir.dt.float32
BF16 = mybir.dt.bfloat16
AF = mybir.ActivationFunctionType
ALU = mybir.AluOpType
AX = mybir.AxisListType

D = 2048
NMETA = 16
RW_H, RW_N = 16, 64
IN_COLS = 14592
FFN_H = 5632
EPS = 1e-6
GN_EPS = 64 * 1e-5
C0 = math.exp(-0.5)
T = 384

ENG = ["pe", "act", "dve", "pool", "sp"]


class Prog:
    def __init__(self, nc):
        self.nc = nc
        self.es = ExitStack()
        self.scopes = [self.es]
        self.sems = {}
        self.count = {}
        self.is_dma_sem = {}
        self.seen = {e: {} for e in ENG}
        self.last_w = {}
        self.readers = {}
        self.engs = {"pe": nc.tensor, "act": nc.scalar, "dve": nc.vector, "pool": nc.gpsimd, "sp": nc.sync}
        self.uid = 0
        self.dmap = {}
        self.free_phys = []
        self.nphys = 0
        self.scope_names = [[]]
        for e in ENG[:4]:
            self._mksem("E_" + e, False)

    def _dsem(self, name, fresh=False):
        if name in self.dmap:
            return self.dmap[name]
        if self.free_phys and not fresh:
            ph = self.free_phys.pop()
        else:
            ph = self._mksem("D%d" % self.nphys, True)
            self.nphys += 1
        self.dmap[name] = ph
        self.scope_names[-1].append(name)
        return ph

    def _mksem(self, name, is_dma):
        if name not in self.sems:
            self.sems[name] = self.es.enter_context(self.nc.semaphore(name))
            self.count[name] = 0
            self.is_dma_sem[name] = is_dma
        return name

    def sb(self, name, shape, dt):
        self.uid += 1
        return self.scopes[-1].enter_context(self.nc.sbuf_tensor("%s_%d" % (name, self.uid), list(shape), dt))

    def ps(self, name, shape, dt):
        self.uid += 1
        return self.scopes[-1].enter_context(self.nc.psum_tensor("%s_%d" % (name, self.uid), list(shape), dt))

    @contextmanager
    def scope(self):
        es = ExitStack()
        self.scopes.append(es)
        self.scope_names.append([])
        try:
            yield
        finally:
            self.barrier()
            self.scopes.pop()
            for n in self.scope_names.pop():
                self.free_phys.append(self.dmap.pop(n))
            es.close()

    def barrier(self):
        for eng in ENG:
            e = self.engs[eng]
            for s, c in self.count.items():
                if self.seen[eng].get(s, 0) < c:
                    e.wait_ge(self.sems[s], c)
                    self.seen[eng][s] = c

    def op(self, eng, fn, reads=(), writes=(), dsem=None):
        deps = {}

        def add(tok):
            if tok is None:
                return
            s, v = tok
            if deps.get(s, 0) < v:
                deps[s] = v

        for k in reads:
            add(self.last_w.get(k))
        for k in writes:
            add(self.last_w.get(k))
            for s, v in self.readers.get(k, {}).items():
                add((s, v))
        e = self.engs[eng]
        for s, v in deps.items():
            if s == "E_pe" and eng == "pe":
                continue
            if self.seen[eng].get(s, 0) < v:
                if self.is_dma_sem[s]:
                    v = self.count[s]
                e.wait_ge(self.sems[s], v)
                self.seen[eng][s] = v
        if dsem is None:
            s = "E_" + eng
            inc = 1
        else:
            s = self._dsem(dsem, fresh=(eng == "pool"))
            inc = 16
        self.count[s] += inc
        tok = (s, self.count[s])
        ins = fn(e)
        ins.then_inc(self.sems[s], inc)
        for k in writes:
            self.last_w[k] = tok
            self.readers[k] = {}
        for k in reads:
            r = self.readers.setdefault(k, {})
            if r.get(s, 0) < tok[1]:
                r[s] = tok[1]
        return tok

    def dma(self, eng, out, in_, reads, writes, dsem, **kw):
        return self.op(eng, lambda e: e.dma_start(out=out, in_=in_, **kw), reads, writes, dsem)

    def finish(self):
        self.barrier()
        self.es.close()


def make_consts(LP):
    c = {}
    c["ident"] = np.eye(128, dtype=np.float32)
    blk = np.zeros((128, 128), np.float32)
    blk[:64, :64] = 1
    blk[64:, 64:] = 1
    c["blkones"] = blk
    bc = np.zeros((128, 2), np.float32)
    bc[:64, 0] = 1
    bc[64:, 1] = 1
    c["blkcols"] = bc
    s128 = np.zeros((128, 128), np.float32)
    for i in range(64):
        s128[i, i + 64] = 1
        s128[i + 64, i] = 1
    c["swap128"] = s128
    s64 = np.zeros((128, 128), np.float32)
    for b in range(2):
        for i in range(32):
            s64[b * 64 + i, b * 64 + i + 32] = 1
            s64[b * 64 + i + 32, b * 64 + i] = 1
    c["swap64"] = s64
    pos = np.arange(LP, dtype=np.float32)
    inv = (np.float32(10000.0) ** (-np.arange(0, 128, 2, dtype=np.float32) / np.float32(128))).astype(np.float32)
    ang = (pos[None, :] * inv[:, None]).astype(np.float32)
    cs, sn = np.cos(ang).astype(np.float32), np.sin(ang).astype(np.float32)
    c["ret_cos"] = np.concatenate([cs, cs], 0)
    c["ret_sin"] = np.concatenate([-sn, sn], 0)
    inv = (np.float32(10000.0) ** (-np.arange(0, 64, 2, dtype=np.float32) / np.float32(64))).astype(np.float32)
    ang = (pos[None, :] * inv[:, None]).astype(np.float32)
    cs, sn = np.cos(ang).astype(np.float32), np.sin(ang).astype(np.float32)
    c["mla_cos"] = np.concatenate([cs, cs, cs, cs], 0)
    c["mla_sin"] = np.concatenate([-sn, sn, -sn, sn], 0)
    c["rope_tab"] = np.ascontiguousarray(np.stack([c["ret_cos"], c["ret_sin"], c["mla_cos"], c["mla_sin"]], 1))
    k = np.arange(128)[:, None]
    q = np.arange(T)[None, :]
    mm = np.zeros((128, 3, T), np.float32)
    for m in range(3):
        mm[:, m, :] = (q >= 128 * m + k)
    c["mla_mask"] = mm
    rt = np.zeros((8, 128, 512), np.float32)
    lg = np.log1p(-np.exp2(-5.0 - np.arange(8, dtype=np.float64)))
    for h in range(8):
        rt[h, :, 0:T] = np.exp(lg[h] * np.arange(T, dtype=np.float64))[None, :]
        rt[h, :, T:T + 128] = np.exp(-lg[h] * np.arange(128, dtype=np.float64))[None, :]
    c["ret_tab"] = rt
    c["ret_lg"] = lg
    j = np.arange(128)[:, None]
    t = np.arange(128)[None, :]
    same = (j // 64) == (t // 64)
    rm = np.zeros((128, 5, 128), np.float32)
    rm[:, 0, :] = -1.0 * (same & (j < t))
    rm[:, 1, :] = -1.0 * (same & (j > t))
    rm[:, 2, :] = 1.0 * (same & (j < t))
    rm[:, 3, :] = 1.0 * (same & (j <= t))
    rm[:, 4, :] = -1.0 * (same & (j <= t))
    c["rw_masks"] = rm
    rs = np.ones((128, 128), np.float32)
    rs[:, 0] = 0
    rs[:, 64] = 0
    c["rw_reset"] = rs
    return c


CONST_NAMES = ["ident", "blkones", "blkcols", "swap128", "swap64", "rope_tab",
               "mla_mask", "ret_tab", "rw_masks", "rw_reset"]

VEC_LAYOUT = [("norm_mix", 16), ("norm_ffn", 16), ("mu", 28), ("w0", 8), ("a0", 8), ("k_k", 8), ("k_a", 8),
              ("r_k", 8), ("nq", 4), ("nkv", 2)]
VOFF = {}
_o = 0
for _n, _w in VEC_LAYOUT:
    VOFF[_n] = _o
    _o += _w
NV = _o


def pm(v, n):
    return np.ascontiguousarray(np.asarray(v, np.float32).reshape(n, 128).T)


def pack_vecs(inp, l):
    out = np.zeros((128, NV), np.float32)

    def put(name, arr):
        out[:, VOFF[name]:VOFF[name] + arr.shape[1]] = arr

    put("norm_mix", pm(inp["norm_mix"][l], 16))
    put("norm_ffn", pm(inp["norm_ffn"][l], 16))
    mu = np.asarray(inp["rw_mu"][l], np.float32)
    mu28 = np.zeros(28 * 128, np.float32)
    mu28[0:3072] = mu[0:3072]
    mu28[3072:3072 + 96] = mu[3072:3168]
    mu28[3200:3200 + 96] = mu[3168:3264]
    mu28[3328:3328 + 256] = mu[3264:3520]
    put("mu", pm(mu28, 28))
    put("w0", pm(inp["rw_w0"][l], 8))
    put("a0", pm(inp["rw_a0"][l], 8))
    put("k_k", pm(inp["rw_k_k"][l], 8))
    put("k_a", pm(inp["rw_k_a"][l], 8))
    put("r_k", pm(inp["rw_r_k"][l], 8))
    put("nq", pm(inp["mla_norm_q"][l], 4))
    put("nkv", pm(inp["mla_norm_kv"][l], 2))
    return out


IN_GROUPS = []
for _seg, _base in (("rw_r", 0), ("rw_k", 1024), ("rw_v", 2048)):
    for _i in range(2):
        IN_GROUPS.append((_seg, _base + 512 * _i, 512, _i))
IN_GROUPS.append(("rw_lora", 3072, 448, 0))
for _seg, _base in (("ret_q", 3520), ("ret_k", 4544), ("ret_v", 5568), ("ret_g", 6592)):
    for _i in range(2):
        IN_GROUPS.append((_seg, _base + 512 * _i, 512, _i))
IN_GROUPS.append(("mla_q", 7616, 512, 0))
IN_GROUPS.append(("mla_kv", 8128, 320, 0))
for _i in range(12):
    IN_GROUPS.append(("gate", 8448 + 512 * _i, 512, _i))


class WStream:
    def __init__(self, B, name, items, KC, width, nbuf=2):
        self.B = B
        self.P = B.P
        self.items = items
        self.name = name
        self.bufs = [B.P.sb(name, [128, KC, width], BF16) for _ in range(nbuf)]
        self.nbuf = nbuf
        self.next = 0

    def prefetch(self):
        if self.next >= len(self.items):
            return
        i = self.next
        self.next += 1
        ap, key, kc, w = self.items[i]
        b = i % self.nbuf
        bk = "%s%d" % (self.name, b)
        self.P.dma("sp", self.bufs[b][:, :kc, :w], ap, [key], [bk], bk)

    def get(self, i):
        while self.next <= i:
            self.prefetch()
        b = i % self.nbuf
        return self.bufs[b], "%s%d" % (self.name, b)


class Builder:
    def __init__(self, L, NL, debug=(), ext_in=()):
        self.ext_in = set(ext_in)
        self.L = L
        self.SEQ = L - NMETA
        self.LP = ((L + T - 1) // T) * T
        self.NT = self.LP // T
        self.N128 = self.LP // 128
        self.NL = NL
        self.debug = set(debug)
        nc = self.nc = bass.Bass("TRN2", target_bir_lowering=False)
        self.P = Prog(nc)
        LP, SEQ = self.LP, self.SEQ

        def di(n, s, dt=F32):
            return nc.dram_tensor(n, list(s), dt, kind="ExternalInput").ap()

        self.x = di("x", [SEQ, D])
        self.meta = di("meta", [NMETA, D])
        self.w_in = di("w_in", [NL, D, IN_COLS])
        self.w_up = di("rw_w_up", [NL, 96, 1024])
        self.a_up = di("rw_a_up", [NL, 96, 1024])
        self.g_up = di("rw_g_up", [NL, 256, 1024])
        self.w_uq = di("mla_w_uq", [NL, 512, 1536])
        self.w_ukv = di("mla_w_ukv", [NL, 256, 2048])
        self.w_bra = di("w_br_rwkv", [NL, 1024, D])
        self.w_brb = di("w_br_ret", [NL, 1024, D])
        self.w_brc = di("w_br_mla", [NL, 1024, D])
        self.w_out = di("w_out", [NL, D, D])
        self.w_gu = di("w_gate_up", [NL, D, 2 * FFN_H])
        self.w_dn = di("w_down", [NL, FFN_H, D])
        self.vecs = di("vecs", [NL, 128, NV])
        self.lnw = di("rw_ln_w", [NL, 1024])
        self.lnb = di("rw_ln_b", [NL, 1024])
        self.fng = di("final_norm_pm", [128, 16])
        self.cst = {}
        cs = make_consts(LP)
        for n in CONST_NAMES:
            self.cst[n] = di("c_" + n, cs[n].shape)
        self.out = nc.dram_tensor("out", [SEQ, D], F32, kind="ExternalOutput").ap()
        self.hT = self.ds("hT", [D, LP], F32)
        self.zrw = self.ds("zrw", [3584, 1 + LP], F32)
        self.retq = self.ds("retq", [1024, LP], BF16)
        self.retk = self.ds("retk", [1024, LP], BF16)
        self.retv = self.ds("retv", [LP, 1024], BF16)
        self.retg = self.ds("retg", [1024, LP], F32)
        self.qd = self.ds("qd", [512, LP], F32)
        self.kvd = self.ds("kvd", [256, LP], F32)
        self.kr = self.ds("kr", [128, LP], BF16)
        self.gates = self.ds("gates", [6144, LP], BF16)
        self.ya = self.ds("ya", [1024, LP], BF16)
        self.yb = self.ds("yb", [1024, LP], BF16)
        self.yc = self.ds("yc", [1024, LP], BF16)
        self.qn = self.ds("qn", [1024, LP], BF16)
        self.qr = self.ds("qr", [512, LP], BF16)
        self.kn = self.ds("kn", [1024, LP], BF16)
        self.vm = self.ds("vm", [LP, 1024], BF16)
        self.wt = {}

    def ds(self, n, s, dt):
        kind = "ExternalOutput" if n in self.debug else ("ExternalInput" if n in self.ext_in else "Internal")
        return self.nc.dram_tensor(n, list(s), dt, kind=kind).ap()

    def conv(self, l, tag, src2d, cs, width, KC):
        name = "wt%d_%s_%d" % (l, tag, cs)
        wt = self.nc.dram_tensor(name, [128, KC, width], BF16, kind="Internal").ap()
        self.P.dma("pool", wt, src2d.rearrange("(kc p) c -> p kc c", p=128)[:, :, cs:cs + width], [], [name],
                   "cv%d_%s" % (l, tag))
        return (wt, name, KC, width)

    def convert_layer(self, l):
        w = {}
        w["in"] = [self.conv(l, "in", self.w_in[l], cs, wd, 16) for (_, cs, wd, _) in IN_GROUPS]
        w["gup"] = [self.conv(l, "g", self.g_up[l], cs, 512, 2) for cs in (0, 512)]
        w["uq"] = [self.conv(l, "uq", self.w_uq[l], cs, 384, 4) for cs in range(0, 1536, 384)]
        w["ukv"] = [self.conv(l, "ukv", self.w_ukv[l], cs, 512, 2) for cs in range(0, 2048, 512)]
        w["bra"] = [self.conv(l, "bra", self.w_bra[l], cs, 256, 8) for cs in range(0, D, 256)]
        w["brb"] = [self.conv(l, "brb", self.w_brb[l], cs, 256, 8) for cs in range(0, D, 256)]
        w["brc"] = [self.conv(l, "brc", self.w_brc[l], cs, 256, 8) for cs in range(0, D, 256)]
        w["out"] = [self.conv(l, "out", self.w_out[l], cs, 256, 16) for cs in range(0, D, 256)]
        w["gate"] = [self.conv(l, "gu", self.w_gu[l], cs, 256, 16) for cs in range(0, FFN_H, 256)]
        w["up"] = [self.conv(l, "gu", self.w_gu[l], FFN_H + cs, 256, 16) for cs in range(0, FFN_H, 256)]
        w["dn"] = [self.conv(l, "dn", self.w_dn[l], cs, 256, 44) for cs in range(0, D, 256)]
        self.wt[l] = w

    def setup_globals(self):
        P = self.P
        self.psum = P.ps("psum", [128, 8, 512], F32)
        self.cs = {}
        for n in ["ident", "blkones", "blkcols", "swap128", "swap64"]:
            t = P.sb("c_" + n, list(self.cst[n].shape), F32)
            P.dma("sp", t[:], self.cst[n], [], ["c_" + n], "c_" + n)
            self.cs[n] = t
        self.ones_bf = P.sb("ones_bf", [128, 128], BF16)
        P.op("pool", lambda e: e.memset(self.ones_bf[:], 1.0), [], ["ones_bf"])
        self.ident_bf = P.sb("ident_bf", [128, 128], BF16)
        P.op("act", lambda e: e.activation(out=self.ident_bf[:], in_=self.cs["ident"][:], func=AF.Copy), ["c_ident"],
             ["ident_bf"])
        self.vec_sb = P.sb("vecs", [128, self.NL, NV], F32)
        P.dma("sp", self.vec_sb[:], self.vecs.rearrange("l p v -> p l v"), [], ["vecs"], "vecs")
        self.fng_sb = P.sb("fng", [128, 16], F32)
        P.dma("sp", self.fng_sb[:], self.fng, [], ["fng"], "fng")
        self.eps_sb = P.sb("eps", [128, 2], F32)
        P.op("pool", lambda e: e.memset(self.eps_sb[:, 0:1], EPS), [], ["eps"])
        P.op("pool", lambda e: e.memset(self.eps_sb[:, 1:2], GN_EPS), ["eps"], ["eps"])
        self.zero_sb = P.sb("zero", [128, 512], F32)
        P.op("pool", lambda e: e.memset(self.zero_sb[:], 0.0), [], ["zero"])
        P.dma("sp", self.zrw.rearrange("(b p) t -> p b t", p=128)[:, :, 0:1], self.zero_sb[:, 0:28].unsqueeze(2),
              ["zero"], ["zrw"], "zero", allow_slow_non_contiguous=True)
        for r0 in (3168, 3296):
            for c0 in range(0, self.LP + 1, 512):
                c1 = min(c0 + 512, self.LP + 1)
                P.dma("sp", self.zrw[r0:r0 + 32, c0:c1], self.zero_sb[0:32, 0:c1 - c0], ["zero"], ["zrw"], "zero")

    def bank(self, b):
        return self.psum[:, b, :], "pb%d" % b

    def vec(self, l, name, c0=0, n=1):
        o = VOFF[name] + c0
        return self.vec_sb[:, l, o:o + n]

    def fm_norm(self, hsb, hkey, KC, TT, gains, out, okey, nfeat, sq, sqkey, rstd, rkey, b):
        P = self.P
        pb, pk = self.bank(b)
        P.op("act", lambda e: e.activation(out=sq[:, :KC, :TT], in_=hsb[:, :KC, :TT], func=AF.Square), [hkey], [sqkey])
        for kc in range(KC):
            P.op("pe", lambda e, kc=kc: e.matmul(pb[:, :TT], self.ones_bf[:], sq[:, kc, :TT], start=(kc == 0),
                                                 stop=(kc == KC - 1)), [sqkey, "ones_bf"], [pk])
        P.op("act", lambda e: e.activation(out=rstd[:, :TT], in_=pb[:, :TT], func=AF.Sqrt, scale=1.0 / nfeat,
                                           bias=self.eps_sb[:, 0:1]), [pk, "eps"], [rkey])
        P.op("dve", lambda e: e.reciprocal(rstd[:, :TT], rstd[:, :TT]), [rkey], [rkey])
        for kc in range(KC):
            P.op("dve", lambda e, kc=kc: e.scalar_tensor_tensor(out=out[:, kc, :TT], in0=hsb[:, kc, :TT],
                                                                scalar=gains[:, kc:kc + 1], in1=rstd[:, :TT],
                                                                op0=ALU.mult, op1=ALU.mult), [hkey, rkey, "vecs", "fng"], [okey])

    def prologue(self):
        P = self.P
        L, LP = self.L, self.LP
        with P.scope():
            xt = [P.sb("xt", [128, D], F32) for _ in range(2)]
            hts = [P.sb("hts", [128, 16, 128], F32) for _ in range(2)]
            hTv = self.hT.rearrange("(kc p) t -> p kc t", p=128)
            for i in range(self.N128):
                r = i % 2
                xk = "xt%d" % r
                t0 = i * 128
                lo, hi = max(t0, NMETA), min(t0 + 128, L)
                if i == 0 or hi < t0 + 128:
                    P.op("pool", lambda e, r=r: e.memset(xt[r][:], 0.0), [], [xk])
                if i == 0:
                    P.dma("sp", xt[r][0:NMETA, :], self.meta, [], [xk], xk)
                if hi > lo:
                    P.dma("sp", xt[r][lo - t0:hi - t0, :], self.x[lo - NMETA:hi - NMETA, :], [], [xk], xk)
                for q in range(4):
                    pb, pk = self.bank(4 * r + q)
                    for j in range(4):
                        kc = 4 * q + j
                        P.op("pe", lambda e, pb=pb, j=j, kc=kc: e.transpose(pb[:, j * 128:(j + 1) * 128],
                                                                            xt[r][:, kc * 128:(kc + 1) * 128],
                                                                            self.cs["ident"][:]), [xk, "c_ident"], [pk])
                    eng = "act" if q % 2 else "dve"
                    if eng == "act":
                        P.op("act", lambda e, pb=pb, q=q: e.activation(out=hts[r][:, 4 * q:4 * q + 4, :],
                                                                       in_=pb.rearrange("p (j t) -> p j t", j=4),
                                                                       func=AF.Copy), [pk], ["hts%d" % r])
                    else:
                        P.op("dve", lambda e, pb=pb, q=q: e.tensor_copy(out=hts[r][:, 4 * q:4 * q + 4, :],
                                                                        in_=pb.rearrange("p (j t) -> p j t", j=4)),
                             [pk], ["hts%d" % r])
                P.dma("sp", hTv[:, :, t0:t0 + 128], hts[r][:], ["hts%d" % r], ["hT"], "hts%d" % r)

    def epilogue(self):
        P = self.P
        L = self.L
        with P.scope():
            hs = [P.sb("ehs", [128, 16, 128], F32) for _ in range(2)]
            sq = P.sb("esq", [128, 16, 128], BF16)
            rstd = P.sb("erstd", [128, 128], F32)
            un = [P.sb("eun", [128, 16, 128], F32) for _ in range(2)]
            osb = [P.sb("eosb", [128, D], F32) for _ in range(2)]
            hTv = self.hT.rearrange("(kc p) t -> p kc t", p=128)
            for i in range(self.N128):
                t0 = i * 128
                lo, hi = max(t0, NMETA), min(t0 + 128, L)
                if hi <= lo:
                    continue
                r = i % 2
                P.dma("sp", hs[r][:], hTv[:, :, t0:t0 + 128], ["hT"], ["ehs%d" % r], "ehs%d" % r)
                self.fm_norm(hs[r], "ehs%d" % r, 16, 128, self.fng_sb, un[r], "eun%d" % r, D, sq, "esq", rstd, "erstd", 4 * r)
                for q in range(4):
                    pb, pk = self.bank(4 * r + q)
                    for j in range(4):
                        kc = 4 * q + j
                        P.op("pe", lambda e, pb=pb, j=j, kc=kc: e.transpose(pb[:, j * 128:(j + 1) * 128], un[r][:, kc, :],
                                                                            self.cs["ident"][:]), ["eun%d" % r, "c_ident"], [pk])
                    if q % 2:
                        P.op("act", lambda e, pb=pb, q=q: e.activation(out=osb[r][:, q * 512:(q + 1) * 512], in_=pb,
                                                                       func=AF.Copy), [pk], ["eosb%d" % r])
                    else:
                        P.op("dve", lambda e, pb=pb, q=q: e.tensor_copy(out=osb[r][:, q * 512:(q + 1) * 512], in_=pb),
                             [pk], ["eosb%d" % r])
                P.dma("sp", self.out[lo - NMETA:hi - NMETA, :], osb[r][lo - t0:hi - t0, :], ["eosb%d" % r], ["out"],
                      "eosb%d" % r)

    def phase_inproj(self, l):
        P = self.P
        LP = self.LP
        w = self.wt[l]["in"]
        with P.scope():
            hs = [P.sb("ahs", [128, 16, T], F32) for _ in range(1)]
            sq = P.sb("asq", [128, 16, T], BF16)
            rstd = P.sb("arstd", [128, T], F32)
            u = P.sb("au", [128, 16, T], BF16)
            stg = [P.sb("astg", [128, 4, T], F32) for _ in range(3)]
            stgb = [P.sb("astgb", [128, 4, T], BF16) for _ in range(2)]
            stv = [P.sb("astv", [128, 3, 512], BF16) for _ in range(2)]
            zf = [P.sb("azf", [128, T], F32) for _ in range(2)]
            t1 = [P.sb("at1", [128, T], F32) for _ in range(2)]
            t2 = [P.sb("at2", [128, T], F32) for _ in range(2)]
            tabs = [P.sb("atab", [128, 4, T], F32) for _ in range(2)]
            ws = WStream(self, "aw", [], 16, 512, nbuf=3)
            ws.items = [w[gi] for _s in range(self.NT) for gi in range(len(IN_GROUPS))]
            hTv = self.hT.rearrange("(kc p) t -> p kc t", p=128)
            cnt = {"stg": 0, "stgb": 0, "stv": 0, "rot": 0, "bank": 0}
            gains = self.vec(l, "norm_mix", 0, 16)

            def nbank():
                b = cnt["bank"] % 6
                cnt["bank"] += 1
                return self.bank(b)

            def mm16(pb, pk, wb, wk, o, wd, out_rows=None):
                dst = pb[:wd, :T] if out_rows is None else pb[out_rows[0]:out_rows[1], :T]
                for kc in range(16):
                    P.op("pe", lambda e, kc=kc: e.matmul(dst, wb[:, kc, o:o + wd], u[:, kc, :], start=(kc == 0),
                                                         stop=(kc == 15)), [wk, "au"], [pk])

            def copy_evac(i, pb, pk, wd, dst, dk, func=AF.Copy):
                if func == AF.Copy and i % 2 == 0:
                    P.op("dve", lambda e: e.tensor_copy(out=dst, in_=pb[:wd, :T]), [pk], [dk])
                else:
                    P.op("act", lambda e: e.activation(out=dst, in_=pb[:wd, :T], func=func), [pk], [dk])

            for s in range(self.NT):
                r = 0
                hk = "ahs0"
                ts0 = s * T
                P.dma("sp", hs[0][:], hTv[:, :, ts0:ts0 + T], ["hT"], ["ahs0"], "ahs0")
                tb = tabs[s % 2]
                tk = "atab%d" % (s % 2)
                P.dma("sp", tb[:], self.cst["rope_tab"][:, :, ts0:ts0 + T], [], [tk], tk)
                self.fm_norm(hs[r], hk, 16, T, gains, u, "au", D, sq, "asq", rstd, "arstd", 7)
                for gi, (seg, cs, gw, gidx) in enumerate(IN_GROUPS):
                    wb, wk = ws.get(s * len(IN_GROUPS) + gi)
                    while ws.next <= s * len(IN_GROUPS) + gi + 2:
                        if ws.next >= len(ws.items):
                            break
                        ws.prefetch()
                    if seg == "gate":
                        bi = cnt["stgb"] % 2
                        cnt["stgb"] += 1
                        bk = "astgb%d" % bi
                        for j in range(4):
                            pb, pk = nbank()
                            mm16(pb, pk, wb, wk, j * 128, 128)
                            copy_evac(j, pb, pk, 128, stgb[bi][:, j, :], bk, AF.Sigmoid)
                        P.dma("sp", self.gates[512 * gidx:512 * gidx + 512, ts0:ts0 + T].rearrange("(j p) t -> p j t", p=128), stgb[bi][:],
                              [bk], ["gates"], bk)
                    elif seg in ("rw_r", "rw_k", "rw_v", "ret_g", "mla_q"):
                        si = cnt["stg"] % 3
                        cnt["stg"] += 1
                        sk = "astg%d" % si
                        func = {"ret_g": AF.Silu}.get(seg, AF.Copy)
                        for j in range(4):
                            pb, pk = nbank()
                            mm16(pb, pk, wb, wk, j * 128, 128)
                            copy_evac(j, pb, pk, 128, stg[si][:, j, :], sk, func)
                        if seg.startswith("rw_"):
                            row0 = {"rw_r": 0, "rw_k": 1024, "rw_v": 2048}[seg] + 512 * gidx
                            dst = self.zrw[row0:row0 + 512, 1 + ts0:1 + ts0 + T]
                            dk = "zrw"
                        elif seg == "ret_g":
                            dst = self.retg[512 * gidx:512 * gidx + 512, ts0:ts0 + T]
                            dk = "retg"
                        elif seg == "mla_q":
                            dst = self.qd[0:512, ts0:ts0 + T]
                            dk = "qd"
                        else:
                            dst = self.gates[512 * gidx:512 * gidx + 512, ts0:ts0 + T]
                            dk = "gates"
                        P.dma("sp", dst.rearrange("(j p) t -> p j t", p=128), stg[si][:], [sk], [dk], sk)
                    elif seg == "rw_lora":
                        si = cnt["stg"] % 3
                        cnt["stg"] += 1
                        sk = "astg%d" % si
                        for j, (o, wd, row0) in enumerate(((0, 96, 3072), (96, 96, 3200), (192, 128, 3328), (320, 128, 3456))):
                            pb, pk = nbank()
                            mm16(pb, pk, wb, wk, o, wd)
                            copy_evac(j, pb, pk, wd, stg[si][:wd, j, :], sk)
                            P.dma("sp", self.zrw[row0:row0 + wd, 1 + ts0:1 + ts0 + T], stg[si][:wd, j, :], [sk], ["zrw"], sk)
                    elif seg in ("ret_q", "ret_k", "mla_kv"):
                        bi = cnt["stgb"] % 2
                        cnt["stgb"] += 1
                        bk = "astgb%d" % bi
                        if seg == "mla_kv":
                            si = cnt["stg"] % 3
                            cnt["stg"] += 1
                            sk = "astg%d" % si
                            for j in range(2):
                                pb, pk = nbank()
                                mm16(pb, pk, wb, wk, j * 128, 128)
                                copy_evac(j, pb, pk, 128, stg[si][:, j, :], sk)
                            P.dma("sp", self.kvd[0:256, ts0:ts0 + T].rearrange("(j p) t -> p j t", p=128), stg[si][:, 0:2, :],
                                  [sk], ["kvd"], sk)
                            jobs = [(None, self.cs["swap64"], 2, 3, "c_swap64")]
                        else:
                            jobs = [(j, self.cs["swap128"], 0, 1, "c_swap128") for j in range(4)]
                        for (j, swp, ctab, stab, swk) in jobs:
                            rr = cnt["rot"] % 2
                            cnt["rot"] += 1
                            pb, pk = nbank()
                            if j is None:
                                mm16(pb, pk, wb, wk, 256, 64, out_rows=(0, 64))
                                mm16(pb, pk, wb, wk, 256, 64, out_rows=(64, 128))
                                jj = 0
                            else:
                                mm16(pb, pk, wb, wk, j * 128, 128)
                                jj = j
                            P.op("act", lambda e, pb=pb, rr=rr: e.activation(out=zf[rr][:], in_=pb[:, :T], func=AF.Copy),
                                 [pk], ["azf%d" % rr])
                            pb2, pk2 = nbank()
                            P.op("pe", lambda e, pb2=pb2, rr=rr, swp=swp: e.matmul(pb2[:, :T], swp[:], zf[rr][:], start=True,
                                                                                 stop=True), ["azf%d" % rr, swk], [pk2])
                            P.op("pool", lambda e, rr=rr, ctab=ctab: e.tensor_tensor(out=t1[rr][:], in0=zf[rr][:],
                                                                                   in1=tb[:, ctab, :], op=ALU.mult),
                                 ["azf%d" % rr, tk], ["at1%d" % rr])
                            P.op("dve", lambda e, pb2=pb2, rr=rr, stab=stab: e.tensor_tensor(out=t2[rr][:], in0=pb2[:, :T],
                                                                                           in1=tb[:, stab, :], op=ALU.mult),
                                 [pk2, tk], ["at2%d" % rr])
                            P.op("pool", lambda e, rr=rr, bi=bi, jj=jj: e.tensor_tensor(out=stgb[bi][:, jj, :], in0=t1[rr][:],
                                                                                      in1=t2[rr][:], op=ALU.add),
                                 ["at1%d" % rr, "at2%d" % rr], [bk])
                        if seg == "mla_kv":
                            P.dma("sp", self.kr[:, ts0:ts0 + T], stgb[bi][:, 0, :], [bk], ["kr"], bk)
                        else:
                            dt_ = self.retq if seg == "ret_q" else self.retk
                            P.dma("sp", dt_[512 * gidx:512 * gidx + 512, ts0:ts0 + T].rearrange("(j p) t -> p j t", p=128),
                                  stgb[bi][:], [bk], [seg], bk)
                    elif seg == "ret_v":
                        vi = cnt["stv"] % 2
                        cnt["stv"] += 1
                        vk = "astv%d" % vi
                        for tsub in range(3):
                            pb, pk = nbank()
                            for kc in range(16):
                                P.op("pe", lambda e, kc=kc, pb=pb, tsub=tsub: e.matmul(pb[:, :512], u[:, kc, tsub * 128:(tsub + 1) * 128],
                                                                                     wb[:, kc, 0:512], start=(kc == 0), stop=(kc == 15)),
                                     [wk, "au"], [pk])
                            if tsub % 2:
                                P.op("act", lambda e, pb=pb, tsub=tsub: e.activation(out=stv[vi][:, tsub, :], in_=pb[:, :512], func=AF.Copy),
                                     [pk], [vk])
                            else:
                                P.op("dve", lambda e, pb=pb, tsub=tsub: e.tensor_copy(out=stv[vi][:, tsub, :], in_=pb[:, :512]), [pk], [vk])
                        P.dma("sp", self.retv[ts0:ts0 + T, 512 * gidx:512 * gidx + 512].rearrange("(a p) c -> p a c", p=128),
                              stv[vi][:], [vk], ["retv"], vk)
                    else:
                        raise ValueError(seg)

    def phase_merge(self, l):
        P = self.P
        w = self.wt[l]
        with P.scope():
            hs = P.sb("ehs", [128, 16, T], F32)
            ys = [P.sb("eys", [128, 8, T], BF16) for _ in range(3)]
            mg = P.sb("emg", [128, 16, T], BF16)
            gt = [P.sb("egt", [128, 3, 2, T], BF16) for _ in range(2)]
            ta = [P.sb("eta", [128, T], F32) for _ in range(2)]
            tb = [P.sb("etb", [128, T], F32) for _ in range(2)]
            wsa = WStream(self, "ewa", [w["bra"][g] for _s in range(self.NT) for g in range(8)], 8, 256, 2)
            wsb = WStream(self, "ewb", [w["brb"][g] for _s in range(self.NT) for g in range(8)], 8, 256, 2)
            wsc = WStream(self, "ewc", [w["brc"][g] for _s in range(self.NT) for g in range(8)], 8, 256, 2)
            wso = WStream(self, "ewo", [w["out"][g] for _s in range(self.NT) for g in range(8)], 16, 256, 2)
            hTv = self.hT.rearrange("(kc p) t -> p kc t", p=128)
            gv = self.gates.rearrange("(b m p) t -> p b m t", b=3, p=128)
            nb = [0]

            def nbank():
                b = nb[0] % 8
                nb[0] += 1
                return self.bank(b)

            for s in range(self.NT):
                ts0 = s * T
                P.dma("sp", hs[:], hTv[:, :, ts0:ts0 + T], ["hT"], ["ehs"], "ehs")
                for bi, (src, nm) in enumerate(((self.ya, "ya"), (self.yb, "yb"), (self.yc, "yc"))):
                    P.dma("sp", ys[bi][:], src[:, ts0:ts0 + T].rearrange("(c p) t -> p c t", p=128), [nm], ["eys%d" % bi],
                          "eys%d" % bi)
                for g in range(8):
                    i = s * 8 + g
                    bufs = []
                    for ws in (wsa, wsb, wsc):
                        bufs.append(ws.get(i))
                        ws.prefetch()
                    gr = g % 2
                    gk = "egt%d" % gr
                    for b3 in range(3):
                        P.dma("sp", gt[gr][:, b3], gv[:, b3, 2 * g:2 * g + 2, ts0:ts0 + T], ["gates"], [gk], gk)
                    for j in range(2):
                        m = 2 * g + j
                        pbs = []
                        for bi in range(3):
                            pb, pk = nbank()
                            wb, wk = bufs[bi]
                            for kc in range(8):
                                P.op("pe", lambda e, pb=pb, wb=wb, kc=kc, bi=bi: e.matmul(pb[:, :T], wb[:, kc, j * 128:(j + 1) * 128],
                                                                                        ys[bi][:, kc, :], start=(kc == 0), stop=(kc == 7)),
                                     [wk, "eys%d" % bi], [pk])
                            pbs.append((pb, pk))
                        r = m % 2
                        P.op("dve", lambda e, r=r: e.tensor_tensor(out=ta[r][:], in0=pbs[0][0][:, :T], in1=gt[gr][:, 0, j, :], op=ALU.mult),
                             [pbs[0][1], gk], ["eta%d" % r])
                        P.op("dve", lambda e, r=r: e.tensor_tensor(out=tb[r][:], in0=pbs[1][0][:, :T], in1=gt[gr][:, 1, j, :], op=ALU.mult),
                             [pbs[1][1], gk], ["etb%d" % r])
                        P.op("pool", lambda e, r=r: e.tensor_tensor(out=ta[r][:], in0=ta[r][:], in1=tb[r][:], op=ALU.add),
                             ["eta%d" % r, "etb%d" % r], ["eta%d" % r])
                        P.op("dve", lambda e, r=r: e.tensor_tensor(out=tb[r][:], in0=pbs[2][0][:, :T], in1=gt[gr][:, 2, j, :], op=ALU.mult),
                             [pbs[2][1], gk, "eta%d" % r], ["etb%d" % r])
                        P.op("pool", lambda e, r=r, m=m: e.tensor_tensor(out=mg[:, m, :], in0=ta[r][:], in1=tb[r][:], op=ALU.add),
                             ["eta%d" % r, "etb%d" % r], ["emg"])
                for g in range(8):
                    wb, wk = wso.get(s * 8 + g)
                    wso.prefetch()
                    for j in range(2):
                        m = 2 * g + j
                        pb, pk = nbank()
                        for kc in range(16):
                            P.op("pe", lambda e, pb=pb, kc=kc: e.matmul(pb[:, :T], wb[:, kc, j * 128:(j + 1) * 128], mg[:, kc, :],
                                                                      start=(kc == 0), stop=(kc == 15)), [wk, "emg"], [pk])
                        P.op("dve", lambda e, pb=pb, m=m: e.tensor_tensor(out=hs[:, m, :], in0=pb[:, :T], in1=hs[:, m, :], op=ALU.add),
                             [pk, "ehs"], ["ehs"])
                P.dma("sp", hTv[:, :, ts0:ts0 + T], hs[:], ["ehs"], ["hT"], "ehs")

    def phase_ffn(self, l):
        P = self.P
        w = self.wt[l]
        NG = FFN_H // 256
        with P.scope():
            hs = P.sb("fhs", [128, 16, T], F32)
            sq = P.sb("fsq", [128, 16, T], BF16)
            rstd = P.sb("frstd", [128, T], F32)
            u = P.sb("fu", [128, 16, T], BF16)
            act = P.sb("fact", [128, 44, T], BF16)
            sg = [P.sb("fsg", [128, T], F32) for _ in range(2)]
            wsg = WStream(self, "fwg", [w["gate"][g] for _s in range(self.NT) for g in range(NG)], 16, 256, 2)
            wsu = WStream(self, "fwu", [w["up"][g] for _s in range(self.NT) for g in range(NG)], 16, 256, 2)
            wsd = WStream(self, "fwd", [w["dn"][g] for _s in range(self.NT) for g in range(8)], 44, 256, 2)
            hTv = self.hT.rearrange("(kc p) t -> p kc t", p=128)
            gains = self.vec(l, "norm_ffn", 0, 16)
            nb = [0]

            def nbank():
                b = nb[0] % 7
                nb[0] += 1
                return self.bank(b)

            for s in range(self.NT):
                ts0 = s * T
                P.dma("sp", hs[:], hTv[:, :, ts0:ts0 + T], ["hT"], ["fhs"], "fhs")
                self.fm_norm(hs, "fhs", 16, T, gains, u, "fu", D, sq, "fsq", rstd, "frstd", 7)
                for g in range(NG):
                    wg, wgk = wsg.get(s * NG + g)
                    wsg.prefetch()
                    wu, wuk = wsu.get(s * NG + g)
                    wsu.prefetch()
                    for j in range(2):
                        f = 2 * g + j
                        pg, pgk = nbank()
                        pu, puk = nbank()
                        for kc in range(16):
                            P.op("pe", lambda e, kc=kc, pg=pg: e.matmul(pg[:, :T], wg[:, kc, j * 128:(j + 1) * 128], u[:, kc, :],
                                                                      start=(kc == 0), stop=(kc == 15)), [wgk, "fu"], [pgk])
                        for kc in range(16):
                            P.op("pe", lambda e, kc=kc, pu=pu: e.matmul(pu[:, :T], wu[:, kc, j * 128:(j + 1) * 128], u[:, kc, :],
                                                                      start=(kc == 0), stop=(kc == 15)), [wuk, "fu"], [puk])
                        r = f % 2
                        P.op("act", lambda e, r=r, pg=pg: e.activation(out=sg[r][:], in_=pg[:, :T], func=AF.Silu), [pgk], ["fsg%d" % r])
                        P.op("dve", lambda e, r=r, pu=pu, f=f: e.tensor_tensor(out=act[:, f, :], in0=pu[:, :T], in1=sg[r][:], op=ALU.mult),
                             [puk, "fsg%d" % r], ["fact"])
                for g in range(8):
                    wd_, wdk = wsd.get(s * 8 + g)
                    wsd.prefetch()
                    for j in range(2):
                        m = 2 * g + j
                        pb, pk = nbank()
                        for f in range(44):
                            P.op("pe", lambda e, pb=pb, f=f: e.matmul(pb[:, :T], wd_[:, f, j * 128:(j + 1) * 128], act[:, f, :],
                                                                    start=(f == 0), stop=(f == 43)), [wdk, "fact"], [pk])
                        P.op("dve", lambda e, pb=pb, m=m: e.tensor_tensor(out=hs[:, m, :], in0=pb[:, :T], in1=hs[:, m, :], op=ALU.add),
                             [pk, "fhs"], ["fhs"])
                P.dma("sp", hTv[:, :, ts0:ts0 + T], hs[:], ["fhs"], ["hT"], "fhs")

    def phase_mla_prep(self, l):
        P = self.P
        w = self.wt[l]
        with P.scope():
            qd = P.sb("dqd", [128, 4, T], F32)
            kvd = P.sb("dkvd", [128, 2, T], F32)
            sq = P.sb("dsq", [128, 4, T], BF16)
            rstd = P.sb("drstd", [128, T], F32)
            cq = P.sb("dcq", [128, 4, T], BF16)
            ckv = P.sb("dckv", [128, 2, T], BF16)
            wq = [P.sb("dwq", [128, 4, 384], BF16) for _ in range(4)]
            wkv = [P.sb("dwkv", [128, 2, 512], BF16) for _ in range(4)]
            stb = [P.sb("dstb", [128, T], BF16) for _ in range(4)]
            stv = [P.sb("dstv", [128, 3, 256], BF16) for _ in range(2)]
            zf = [P.sb("dzf", [128, T], F32) for _ in range(2)]
            t1 = [P.sb("dt1", [128, T], F32) for _ in range(2)]
            t2 = [P.sb("dt2", [128, T], F32) for _ in range(2)]
            tabs = [P.sb("dtab", [128, 2, T], F32) for _ in range(2)]
            for i in range(4):
                P.dma("sp", wq[i][:], w["uq"][i][0], [w["uq"][i][1]], ["dwq%d" % i], "dwq%d" % i)
                P.dma("sp", wkv[i][:], w["ukv"][i][0], [w["ukv"][i][1]], ["dwkv%d" % i], "dwkv%d" % i)
            cnt = {"b": 0, "stb": 0, "stv": 0, "rot": 0}

            def nbank():
                b = cnt["b"] % 7
                cnt["b"] += 1
                return self.bank(b)

            def nstb():
                i = cnt["stb"] % 4
                cnt["stb"] += 1
                return stb[i], "dstb%d" % i

            for s in range(self.NT):
                ts0 = s * T
                P.dma("sp", qd[:], self.qd[:, ts0:ts0 + T].rearrange("(c p) t -> p c t", p=128), ["qd"], ["dqd"], "dqd")
                P.dma("sp", kvd[:], self.kvd[:, ts0:ts0 + T].rearrange("(c p) t -> p c t", p=128), ["kvd"], ["dkvd"], "dkvd")
                tb = tabs[s % 2]
                tk = "dtab%d" % (s % 2)
                P.dma("sp", tb[:], self.cst["rope_tab"][:, 2:4, ts0:ts0 + T], [], [tk], tk)
                self.fm_norm(qd, "dqd", 4, T, self.vec(l, "nq", 0, 4), cq, "dcq", 512, sq, "dsq", rstd, "drstd", 7)
                self.fm_norm(kvd, "dkvd", 2, T, self.vec(l, "nkv", 0, 2), ckv, "dckv", 256, sq, "dsq", rstd, "drstd", 7)
                for gq in range(4):
                    wk = "dwq%d" % gq
                    for j in range(2):
                        h = 2 * gq + j
                        pb, pk = nbank()
                        for kc in range(4):
                            P.op("pe", lambda e, kc=kc: e.matmul(pb[:, :T], wq[gq][:, kc, j * 192:j * 192 + 128], cq[:, kc, :],
                                                                 start=(kc == 0), stop=(kc == 3)), [wk, "dcq"], [pk])
                        sb_, sk = nstb()
                        P.op("act", lambda e: e.activation(out=sb_[:], in_=pb[:, :T], func=AF.Copy), [pk], [sk])
                        P.dma("sp", self.qn[h * 128:(h + 1) * 128, ts0:ts0 + T], sb_[:], [sk], ["qn"], sk)
                    pb, pk = nbank()
                    for j in range(2):
                        for kc in range(4):
                            P.op("pe", lambda e, kc=kc: e.matmul(pb[j * 64:(j + 1) * 64, :T], wq[gq][:, kc, j * 192 + 128:j * 192 + 192],
                                                                 cq[:, kc, :], start=(kc == 0), stop=(kc == 3)), [wk, "dcq"], [pk])
                    rr = cnt["rot"] % 2
                    cnt["rot"] += 1
                    P.op("act", lambda e: e.activation(out=zf[rr][:], in_=pb[:, :T], func=AF.Copy), [pk], ["dzf%d" % rr])
                    pb2, pk2 = nbank()
                    P.op("pe", lambda e: e.matmul(pb2[:, :T], self.cs["swap64"][:], zf[rr][:], start=True, stop=True),
                         ["dzf%d" % rr, "c_swap64"], [pk2])
                    P.op("pool", lambda e: e.tensor_tensor(out=t1[rr][:], in0=zf[rr][:], in1=tb[:, 0, :], op=ALU.mult),
                         ["dzf%d" % rr, tk], ["dt1%d" % rr])
                    P.op("dve", lambda e: e.tensor_tensor(out=t2[rr][:], in0=pb2[:, :T], in1=tb[:, 1, :], op=ALU.mult),
                         [pk2, tk], ["dt2%d" % rr])
                    sb_, sk = nstb()
                    P.op("pool", lambda e: e.tensor_tensor(out=sb_[:], in0=t1[rr][:], in1=t2[rr][:], op=ALU.add),
                         ["dt1%d" % rr, "dt2%d" % rr], [sk])
                    P.dma("sp", self.qr[gq * 128:(gq + 1) * 128, ts0:ts0 + T], sb_[:], [sk], ["qr"], sk)
                for gk in range(4):
                    wk = "dwkv%d" % gk
                    for j in range(2):
                        h = 2 * gk + j
                        pb, pk = nbank()
                        for kc in range(2):
                            P.op("pe", lambda e, kc=kc: e.matmul(pb[:, :T], wkv[gk][:, kc, j * 256:j * 256 + 128], ckv[:, kc, :],
                                                                 start=(kc == 0), stop=(kc == 1)), [wk, "dckv"], [pk])
                        sb_, sk = nstb()
                        P.op("act", lambda e: e.activation(out=sb_[:], in_=pb[:, :T], func=AF.Copy), [pk], [sk])
                        P.dma("sp", self.kn[h * 128:(h + 1) * 128, ts0:ts0 + T], sb_[:], [sk], ["kn"], sk)
                    vi = cnt["stv"] % 2
                    cnt["stv"] += 1
                    vk = "dstv%d" % vi
                    for tsub in range(3):
                        pb, pk = nbank()
                        for kc in range(2):
                            rhs = wkv[gk][:, kc, :].rearrange("p (j c) -> p j c", j=2)[:, :, 128:256]
                            P.op("pe", lambda e, kc=kc, rhs=rhs: e.matmul(pb[:, :256].rearrange("p (j c) -> p j c", j=2),
                                                                        ckv[:, kc, tsub * 128:(tsub + 1) * 128], rhs,
                                                                        start=(kc == 0), stop=(kc == 1)), [wk, "dckv"], [pk])
                        P.op("dve", lambda e: e.tensor_copy(out=stv[vi][:, tsub, :], in_=pb[:, :256]), [pk], [vk])
                    P.dma("sp", self.vm[ts0:ts0 + T, gk * 256:(gk + 1) * 256].rearrange("(a p) c -> p a c", p=128), stv[vi][:],
                          [vk], ["vm"], vk)

    def phase_mla_attn(self, l):
        P = self.P
        LP, N128 = self.LP, self.N128
        scale = 192.0 ** -0.5
        with P.scope():
            kn = [P.sb("mkn", [128, LP], BF16) for _ in range(2)]
            qn = [P.sb("mqn", [128, LP], BF16) for _ in range(2)]
            vm = [P.sb("mvm", [128, N128, 128], BF16) for _ in range(2)]
            qr = [P.sb("mqr", [128, LP], BF16) for _ in range(2)]
            kr = P.sb("mkr", [128, LP], BF16)
            mask = P.sb("mmask", [128, 3, T], BF16)
            maskf = P.sb("mmaskf", [128, 3, T], F32)
            pT = [P.sb("mpT", [128, T], BF16) for _ in range(4)]
            rl = [P.sb("mrl", [128, T], F32) for _ in range(2)]
            ob = [P.sb("mob", [128, T], BF16) for _ in range(2)]
            P.dma("sp", kr[:], self.kr, ["kr"], ["mkr"], "mkr")
            P.dma("sp", maskf[:], self.cst["mla_mask"], [], ["mmaskf"], "mmaskf")
            P.op("act", lambda e: e.activation(out=mask[:], in_=maskf[:], func=AF.Copy), ["mmaskf"], ["mmask"])
            npt = [0]
            ns = [0]
            pend_epi = [None]
            gcnt = [0]
            for h in range(8):
                r = h % 2
                b0 = r * 64
                P.dma("sp", kn[r][:], self.kn[h * 128:(h + 1) * 128, :], ["kn"], ["mkn%d" % r], "mkn%d" % r)
                P.dma("sp", qn[r][:], self.qn[h * 128:(h + 1) * 128, :], ["qn"], ["mqn%d" % r], "mqn%d" % r)
                P.dma("sp", vm[r][:], self.vm.rearrange("(n p) c -> p n c", p=128)[:, :, h * 128:(h + 1) * 128], ["vm"],
                      ["mvm%d" % r], "mvm%d" % r)
                pr = (h // 2) % 2
                if h % 2 == 0:
                    P.dma("sp", qr[pr][:], self.qr[(h // 2) * 128:(h // 2 + 1) * 128, :], ["qr"], ["mqr%d" % pr], "mqr%d" % pr)
                for g in range(self.NT):
                    q0 = g * T
                    gpar = gcnt[0] % 2
                    gcnt[0] += 1
                    pbO, pkO = self.bank(4 + gpar)
                    pbL, pkL = self.bank(6 + gpar)
                    nk = 3 * g + 3

                    def qk(kt):
                        pbS, pkS = self.bank(ns[0] % 4)
                        ns[0] += 1
                        P.op("pe", lambda e: e.matmul(pbS[:, :T], kn[r][:, kt * 128:(kt + 1) * 128], qn[r][:, q0:q0 + T], start=True,
                                                      stop=False), ["mkn%d" % r, "mqn%d" % r], [pkS])
                        P.op("pe", lambda e: e.matmul(pbS[:, :T], kr[b0:b0 + 64, kt * 128:(kt + 1) * 128], qr[pr][b0:b0 + 64, q0:q0 + T],
                                                      start=False, stop=True), ["mkr", "mqr%d" % pr], [pkS])
                        pi = npt[0] % 4
                        npt[0] += 1
                        pk_ = "mpT%d" % pi
                        P.op("act", lambda e: e.activation(out=pT[pi][:], in_=pbS[:, :T], func=AF.Exp, scale=scale), [pkS], [pk_])
                        if kt >= 3 * g:
                            m = kt - 3 * g
                            P.op("dve", lambda e: e.tensor_tensor(out=pT[pi][:], in0=pT[pi][:], in1=mask[:, m, :], op=ALU.mult),
                                 [pk_, "mmask"], [pk_])
                        return pi, pk_

                    def pv(kt, pi, pk_):
                        P.op("pe", lambda e: e.matmul(pbO[:, :T], vm[r][:, kt, :], pT[pi][:], start=(kt == 0), stop=(kt == nk - 1)),
                             ["mvm%d" % r, pk_], [pkO])
                        P.op("pe", lambda e: e.matmul(pbL[:, :T], self.ones_bf[:], pT[pi][:], start=(kt == 0), stop=(kt == nk - 1)),
                             ["ones_bf", pk_], [pkL])

                    pend = []
                    for kt in range(nk):
                        pend.append((kt,) + qk(kt))
                        if len(pend) > 2:
                            pv(*pend.pop(0))
                    while pend:
                        pv(*pend.pop(0))
                    def epi(h=h, g=g, q0=q0, pbO=pbO, pkO=pkO, pbL=pbL, pkL=pkL, gi=gpar):
                        P.op("dve", lambda e: e.reciprocal(rl[gi][:], pbL[:, :T]), [pkL], ["mrl%d" % gi])
                        P.op("dve", lambda e: e.tensor_tensor(out=ob[gi][:], in0=pbO[:, :T], in1=rl[gi][:], op=ALU.mult),
                             [pkO, "mrl%d" % gi], ["mob%d" % gi])
                        P.dma("sp", self.yc[h * 128:(h + 1) * 128, q0:q0 + T], ob[gi][:], ["mob%d" % gi], ["yc"], "mob%d" % gi)

                    if pend_epi[0] is not None:
                        pend_epi[0]()
                    pend_epi[0] = epi
            if pend_epi[0] is not None:
                pend_epi[0]()

    def phase_ret(self, l):
        P = self.P
        LP, N128 = self.LP, self.N128
        lg = make_consts(128)["ret_lg"] if False else np.log1p(-np.exp2(-5.0 - np.arange(8, dtype=np.float64)))
        sc = 128.0 ** -0.5
        with P.scope():
            kT = [P.sb("rkT", [128, LP], BF16) for _ in range(2)]
            qT = [P.sb("rqT", [128, LP], BF16) for _ in range(2)]
            vv = [P.sb("rvv", [128, N128, 128], BF16) for _ in range(2)]
            tab = [P.sb("rtab", [128, 512], F32) for _ in range(2)]
            maskf = P.sb("rmaskf", [128, 3, T], F32)
            gs = [P.sb("rgs", [128, T], F32) for _ in range(2)]
            pT = [P.sb("rpT", [128, T], BF16) for _ in range(4)]
            sq = [P.sb("rsq", [128, T], BF16) for _ in range(2)]
            rstd = [P.sb("rrstd", [128, T], F32) for _ in range(2)]
            tt = [P.sb("rtt", [128, T], F32) for _ in range(2)]
            ob = [P.sb("rob", [128, T], BF16) for _ in range(2)]
            npt = [0]
            ns = [0]
            pend_epi = [None]
            gcnt = [0]
            P.dma("sp", maskf[:], self.cst["mla_mask"], [], ["rmaskf"], "rmaskf")
            for h in range(8):
                r = h % 2
                P.dma("sp", kT[r][:], self.retk[h * 128:(h + 1) * 128, :], ["ret_k"], ["rkT%d" % r], "rkT%d" % r)
                P.dma("sp", qT[r][:], self.retq[h * 128:(h + 1) * 128, :], ["ret_q"], ["rqT%d" % r], "rqT%d" % r)
                P.dma("sp", vv[r][:], self.retv.rearrange("(n p) c -> p n c", p=128)[:, :, h * 128:(h + 1) * 128], ["retv"],
                      ["rvv%d" % r], "rvv%d" % r)
                P.dma("sp", tab[r][:], self.cst["ret_tab"][h], [], ["rtab%d" % r], "rtab%d" % r)
                P.op("pool", lambda e: e.tensor_tensor(out=qT[r][:].rearrange("p (g t) -> p g t", t=T), in0=qT[r][:].rearrange("p (g t) -> p g t", t=T),
                                                       in1=tab[r][:, 0:T].unsqueeze(1).to_broadcast([128, self.NT, T]), op=ALU.mult),
                     ["rqT%d" % r, "rtab%d" % r], ["rqT%d" % r])
                P.op("dve", lambda e: e.tensor_tensor(out=kT[r][:].rearrange("p (n t) -> p n t", t=128), in0=kT[r][:].rearrange("p (n t) -> p n t", t=128),
                                                      in1=tab[r][:, T:T + 128].unsqueeze(1).to_broadcast([128, N128, 128]), op=ALU.mult),
                     ["rkT%d" % r, "rtab%d" % r], ["rkT%d" % r])
                for g in range(self.NT):
                    q0 = g * T
                    gi = gcnt[0] % 2
                    gcnt[0] += 1
                    P.dma("sp", gs[gi][:], self.retg[h * 128:(h + 1) * 128, q0:q0 + T], ["retg"], ["rgs%d" % gi], "rgs%d" % gi)
                    pbO, pkO = self.bank(4 + gi)
                    pbL, pkL = self.bank(6 + gi)
                    nk = 3 * g + 3

                    def qk(kt):
                        pbS, pkS = self.bank(ns[0] % 4)
                        ns[0] += 1
                        P.op("pe", lambda e: e.matmul(pbS[:, :T], kT[r][:, kt * 128:(kt + 1) * 128], qT[r][:, q0:q0 + T], start=True,
                                                      stop=True), ["rkT%d" % r, "rqT%d" % r], [pkS])
                        pi = npt[0] % 4
                        npt[0] += 1
                        pk_ = "rpT%d" % pi
                        if kt >= 3 * g:
                            m = kt - 3 * g
                            c = float(np.float32(sc * math.exp(-lg[h] * 128 * m)))
                            P.op("dve", lambda e: e.scalar_tensor_tensor(out=pT[pi][:], in0=pbS[:, :T], scalar=c, in1=maskf[:, m, :],
                                                                         op0=ALU.mult, op1=ALU.mult), [pkS, "rmaskf"], [pk_])
                        else:
                            c = float(np.float32(sc * math.exp(lg[h] * (q0 - kt * 128))))
                            if kt % 2 == 0:
                                P.op("act", lambda e: e.activation(out=pT[pi][:], in_=pbS[:, :T], func=AF.Copy, scale=c), [pkS], [pk_])
                            else:
                                P.op("dve", lambda e: e.tensor_scalar(out=pT[pi][:], in0=pbS[:, :T], scalar1=c, scalar2=None, op0=ALU.mult),
                                     [pkS], [pk_])
                        return pi, pk_

                    def pv(kt, pi, pk_):
                        P.op("pe", lambda e: e.matmul(pbO[:, :T], vv[r][:, kt, :], pT[pi][:], start=(kt == 0), stop=(kt == nk - 1)),
                             ["rvv%d" % r, pk_], [pkO])

                    pend = []
                    for kt in range(nk):
                        pend.append((kt,) + qk(kt))
                        if len(pend) > 2:
                            pv(*pend.pop(0))
                    while pend:
                        pv(*pend.pop(0))
                    def epi(h=h, g=g, gi=gi, q0=q0, pbO=pbO, pkO=pkO, pbL=pbL, pkL=pkL):
                        P.op("act", lambda e: e.activation(out=sq[gi][:], in_=pbO[:, :T], func=AF.Square), [pkO], ["rsq%d" % gi])
                        P.op("pe", lambda e: e.matmul(pbL[:, :T], self.ones_bf[:], sq[gi][:], start=True, stop=True), ["ones_bf", "rsq%d" % gi],
                             [pkL])
                        P.op("act", lambda e: e.activation(out=rstd[gi][:], in_=pbL[:, :T], func=AF.Sqrt, scale=1.0 / 128,
                                                           bias=self.eps_sb[:, 0:1]), [pkL, "eps"], ["rrstd%d" % gi])
                        P.op("dve", lambda e: e.reciprocal(rstd[gi][:], rstd[gi][:]), ["rrstd%d" % gi], ["rrstd%d" % gi])
                        P.op("dve", lambda e: e.tensor_tensor(out=tt[gi][:], in0=pbO[:, :T], in1=rstd[gi][:], op=ALU.mult),
                             [pkO, "rrstd%d" % gi], ["rtt%d" % gi])
                        P.op("pool", lambda e: e.tensor_tensor(out=ob[gi][:], in0=tt[gi][:], in1=gs[gi][:], op=ALU.mult),
                             ["rtt%d" % gi, "rgs%d" % gi], ["rob%d" % gi])
                        P.dma("sp", self.yb[h * 128:(h + 1) * 128, q0:q0 + T], ob[gi][:], ["rob%d" % gi], ["yb"], "rob%d" % gi)

                    if pend_epi[0] is not None:
                        pend_epi[0]()
                    pend_epi[0] = epi
            if pend_epi[0] is not None:
                pend_epi[0]()

    def phase_rwkv(self, l):
        P = self.P
        w = self.wt[l]
        ident = self.cs["ident"]
        with P.scope():
            f = lambda n, s, dt=F32: P.sb(n, s, dt)
            wup = f("kwup", [96, 1024]); aup = f("kaup", [96, 1024]); gup = f("kgup", [128, 2, 1024], BF16)
            lnw = f("klnw", [128, 1024]); lnb = f("klnb", [128, 1024])
            rm = f("krm", [128, 5, 128]); rst = f("krst", [128, 128]); c1 = f("kc1", [128, 8])
            S = f("kS", [128, 8, 64])
            S2 = f("kS2", [128, 8, 2, 64])
            zt = f("kzt", [128, 28, 129]); zs = f("kzs", [128, 28, 128])
            tw = f("ktw", [96, 128]); sgd = f("ksgd", [128, 2, 128], BF16)
            sgw = f("ksgw", [128, 8, 128]); aa = f("kaa", [128, 8, 128])
            rt = f("krt", [128, 8, 128]); kapt = f("kkapt", [128, 8, 128]); kt_ = f("kkt", [128, 8, 128]); bt = f("kbt", [128, 8, 128])
            PC = f("kPC", [128, 8, 2])
            khat = f("kkhat", [128, 1024]); nbhat = f("knbhat", [128, 1024]); Vtok = f("kVtok", [128, 1024]); gtok = f("kgtok", [128, 1024])
            s16 = f("ks16", [128, 16])
            Mk = f("kMk", [128, 16, 128]); Akr = f("kAkr", [128, 16, 128]); nAbr = f("knAbr", [128, 16, 128]); Xall = f("kX", [128, 16, 128])
            A = [[f("kA", [128, 4, 128]) for _ in range(2)] for _ in range(2)]; AT = [[f("kAT", [128, 4, 128]) for _ in range(2)] for _ in range(2)]; XT = [f("kXT", [128, 4, 128]) for _ in range(2)]
            tmp = [[f("ktmp", [128, 128]) for _ in range(10)] for _ in range(4)]
            RHS = f("kRHS", [128, 1024]); UT = f("kUT", [128, 1024])
            m16 = f("km16", [128, 16]); v16 = f("kv16", [128, 16])
            ztf = zt[:].rearrange("p b t -> p (b t)")
            yc = ztf[:, 0:1024]; ysq = ztf[:, 1024:2048]
            yo = f("kyo", [128, 1024], BF16); yst = f("kyst", [128, 8, 128], BF16)
            P.dma("sp", wup[:], self.w_up[l], [], ["kwup"], "kwup")
            P.dma("sp", aup[:], self.a_up[l], [], ["kaup"], "kaup")
            for i in range(2):
                P.dma("sp", gup[:, :, i * 512:(i + 1) * 512], w["gup"][i][0], [w["gup"][i][1]], ["kgup"], "kgup")
            P.dma("sp", lnw[:], self.lnw[l].partition_broadcast(128), [], ["klnw"], "klnw")
            P.dma("sp", lnb[:], self.lnb[l].partition_broadcast(128), [], ["klnb"], "klnb")
            P.dma("sp", rm[:], self.cst["rw_masks"], [], ["krm"], "krm")
            P.dma("sp", rst[:], self.cst["rw_reset"], [], ["krst"], "krst")
            P.op("dve", lambda e: e.tensor_scalar(out=c1[:], in0=self.vec(l, "k_a", 0, 8), scalar1=-1.0, scalar2=1.0, op0=ALU.mult,
                                                  op1=ALU.add), ["vecs"], ["kc1"])
            P.op("pool", lambda e: e.memset(S[:], 0.0), [], ["kS"])
            P.op("pool", lambda e: e.memset(S2[:], 0.0), [], ["kS2"])
            P.op("pool", lambda e: e.memset(RHS[:], 0.0), [], ["kRHS"])
            P.op("pool", lambda e: e.memset(UT[:], 0.0), [], ["kUT"])
            zv = self.zrw.rearrange("(b p) t -> p b t", p=128)
            yav = self.ya.rearrange("(c p) t -> p c t", p=128)
            nb = [0]

            nb_pool = [list(range(8))]

            def nbank():
                pool = nb_pool[0]
                b = pool[nb[0] % len(pool)]
                nb[0] += 1
                return self.bank(b)

            def bc(ap, shape):
                return ap.to_broadcast(shape)

            for i in range(self.N128):
                t0 = i * 128
                P.dma("sp", zt[:], zv[:, :, t0:t0 + 129], ["zrw"], ["kzt", "kyc", "kysq"], "kzt")
                P.op("dve", lambda e: e.tensor_tensor(out=zs[:], in0=zt[:, :, 0:128], in1=zt[:, :, 1:129], op=ALU.subtract), ["kzt"], ["kzs"])
                P.op("pool", lambda e: e.tensor_tensor(out=zs[:], in0=zs[:], in1=bc(self.vec(l, "mu", 0, 28).unsqueeze(2), [128, 28, 128]),
                                                       op=ALU.mult), ["kzs", "vecs"], ["kzs"])
                P.op("dve", lambda e: e.tensor_tensor(out=zs[:], in0=zs[:], in1=zt[:, :, 1:129], op=ALU.add), ["kzs", "kzt"], ["kzs"])
                P.op("act", lambda e: e.activation(out=tw[:], in_=zs[0:96, 24, :], func=AF.Tanh), ["kzs"], ["ktw"])
                P.op("act", lambda e: e.activation(out=sgd[:], in_=zs[:, 26:28, :], func=AF.Sigmoid), ["kzs"], ["ksgd"])
                if getattr(self, "rw_stop", 99) <= 1:
                    continue
                pw = [nbank(), nbank()]
                pa = [nbank(), nbank()]
                for cb in range(8):
                    pb, pk = pw[cb // 4]
                    P.op("pe", lambda e: e.matmul(pb[:, (cb % 4) * 128:(cb % 4 + 1) * 128], wup[0:96, cb * 128:(cb + 1) * 128], tw[0:96, :],
                                                  start=True, stop=True), ["kwup", "ktw"], [pk])
                for cb in range(8):
                    pb, pk = pa[cb // 4]
                    P.op("pe", lambda e: e.matmul(pb[:, (cb % 4) * 128:(cb % 4 + 1) * 128], aup[0:96, cb * 128:(cb + 1) * 128], zs[0:96, 25, :],
                                                  start=True, stop=True), ["kaup", "kzs"], [pk])
                for cb in range(8):
                    pb, pk = pw[cb // 4]
                    P.op("act", lambda e: e.activation(out=sgw[:, cb, :], in_=pb[:, (cb % 4) * 128:(cb % 4 + 1) * 128], func=AF.Sigmoid,
                                                       bias=self.vec(l, "w0", cb, 1)), [pk, "vecs"], ["ksgw"])
                    pb, pk = pa[cb // 4]
                    P.op("act", lambda e: e.activation(out=aa[:, cb, :], in_=pb[:, (cb % 4) * 128:(cb % 4 + 1) * 128], func=AF.Sigmoid,
                                                       bias=self.vec(l, "a0", cb, 1)), [pk, "vecs"], ["kaa"])
                pg = [nbank(), nbank()]
                for hf in range(2):
                    pb, pk = pg[hf]
                    for kc in range(2):
                        P.op("pe", lambda e: e.matmul(pb[:, :], sgd[:, kc, :], gup[:, kc, hf * 512:(hf + 1) * 512], start=(kc == 0), stop=(kc == 1)),
                             ["ksgd", "kgup"], [pk])
                    P.op("act", lambda e: e.activation(out=gtok[:, hf * 512:(hf + 1) * 512], in_=pb[:, :], func=AF.Copy), [pk], ["kgtok"])
                if getattr(self, "rw_stop", 99) <= 2:
                    continue
                pkh = [self.bank(0), self.bank(0)]
                pbh = [self.bank(1), self.bank(1)]
                pvv = [self.bank(2), self.bank(2)]
                pq, pqk = self.bank(3)
                ps_, psk = self.bank(4)
                def TT(cb, j):
                    return tmp[cb % 4][j], "ktmp%d_%d" % (cb % 4, j)

                def SL(cb):
                    return slice((cb % 4) * 128, (cb % 4 + 1) * 128)

                CUM, ER, EINV, EK, KK, SQ, KAP, K2, BB, RK = range(10)
                rp = lambda cb: zs[:, cb, :]
                kp = lambda cb: zs[:, 8 + cb, :]
                vp = lambda cb: zs[:, 16 + cb, :]
                t_ = lambda cb, j: TT(cb, j)[0]
                k_ = lambda cb, j: TT(cb, j)[1]
                steps = [
                    lambda cb: P.op("dve", lambda e: e.tensor_tensor_scan(out=t_(cb, CUM)[:], data0=rst[:], data1=sgw[:, cb, :], initial=0.0,
                                                                          op0=ALU.mult, op1=ALU.add), ["krst", "ksgw"], [k_(cb, CUM)]),
                    lambda cb: P.op("pool", lambda e: e.tensor_scalar(out=t_(cb, KK)[:], in0=kp(cb), scalar1=self.vec(l, "k_k", cb, 1), scalar2=None,
                                                                      op0=ALU.mult), ["kzs", "vecs"], [k_(cb, KK)]),
                    lambda cb: P.op("act", lambda e: e.activation(out=t_(cb, ER)[:], in_=t_(cb, CUM)[:], func=AF.Exp, scale=-C0), [k_(cb, CUM)], [k_(cb, ER)]),
                    lambda cb: P.op("act", lambda e: e.activation(out=t_(cb, EINV)[:], in_=t_(cb, CUM)[:], func=AF.Exp, scale=C0), [k_(cb, CUM)], [k_(cb, EINV)]),
                    lambda cb: P.op("dve", lambda e: e.tensor_tensor(out=t_(cb, EK)[:], in0=t_(cb, CUM)[:], in1=sgw[:, cb, :], op=ALU.subtract),
                                    [k_(cb, CUM), "ksgw"], [k_(cb, EK)]),
                    lambda cb: P.op("act", lambda e: e.activation(out=t_(cb, SQ)[:], in_=t_(cb, KK)[:], func=AF.Square), [k_(cb, KK)], [k_(cb, SQ)]),
                    lambda cb: P.op("pe", lambda e: e.matmul(pq[:, SL(cb)], self.cs["blkones"][:], t_(cb, SQ)[:], start=True, stop=True),
                                    [k_(cb, SQ), "c_blkones"], [pqk]),
                    lambda cb: P.op("act", lambda e: e.activation(out=t_(cb, EK)[:], in_=t_(cb, EK)[:], func=AF.Exp, scale=-C0), [k_(cb, EK)], [k_(cb, EK)]),
                    lambda cb: P.op("dve", lambda e: e.tensor_scalar(out=t_(cb, K2)[:], in0=aa[:, cb, :], scalar1=self.vec(l, "k_a", cb, 1),
                                                                     scalar2=c1[:, cb:cb + 1], op0=ALU.mult, op1=ALU.add), ["kaa", "vecs", "kc1"], [k_(cb, K2)]),
                    lambda cb: P.op("pool", lambda e: e.tensor_tensor(out=rt[:, cb, :], in0=rp(cb), in1=t_(cb, ER)[:], op=ALU.mult), ["kzs", k_(cb, ER)], ["krt"]),
                    lambda cb: P.op("act", lambda e: e.activation(out=t_(cb, SQ)[:], in_=pq[:, SL(cb)], func=AF.Sqrt), [pqk], [k_(cb, SQ)]),
                    lambda cb: P.op("dve", lambda e: e.tensor_tensor(out=t_(cb, K2)[:], in0=t_(cb, K2)[:], in1=kp(cb), op=ALU.mult), [k_(cb, K2), "kzs"], [k_(cb, K2)]),
                    lambda cb: P.op("dve", lambda e: e.tensor_scalar(out=t_(cb, SQ)[:], in0=t_(cb, SQ)[:], scalar1=1e-12, scalar2=None, op0=ALU.max),
                                    [k_(cb, SQ)], [k_(cb, SQ)]),
                    lambda cb: P.op("act", lambda e: e.activation(out=PC[:, cb, :], in_=t_(cb, ER)[:].rearrange("p (c t) -> p c t", c=2)[:, :, 63],
                                                                  func=AF.Copy), [k_(cb, ER)], ["kPC"]),
                    lambda cb: P.op("dve", lambda e: e.reciprocal(t_(cb, SQ)[:], t_(cb, SQ)[:]), [k_(cb, SQ)], [k_(cb, SQ)]),
                    lambda cb: P.op("pool", lambda e: e.tensor_tensor(out=kt_[:, cb, :], in0=t_(cb, K2)[:], in1=t_(cb, EINV)[:], op=ALU.mult),
                                    [k_(cb, K2), k_(cb, EINV)], ["kkt"]),
                    lambda cb: P.op("dve", lambda e: e.tensor_tensor(out=t_(cb, KAP)[:], in0=t_(cb, KK)[:], in1=t_(cb, SQ)[:], op=ALU.mult),
                                    [k_(cb, KK), k_(cb, SQ)], [k_(cb, KAP)]),
                    lambda cb: P.op("dve", lambda e: e.scalar_tensor_tensor(out=t_(cb, RK)[:], in0=rp(cb), scalar=self.vec(l, "r_k", cb, 1), in1=t_(cb, K2)[:],
                                                                            op0=ALU.mult, op1=ALU.mult), ["kzs", "vecs", k_(cb, K2)], [k_(cb, RK)]),
                    lambda cb: P.op("pool", lambda e: e.tensor_tensor(out=t_(cb, BB)[:], in0=aa[:, cb, :], in1=t_(cb, KAP)[:], op=ALU.mult),
                                    ["kaa", k_(cb, KAP)], [k_(cb, BB)]),
                    lambda cb: P.op("dve", lambda e: e.tensor_tensor(out=kapt[:, cb, :], in0=t_(cb, KAP)[:], in1=t_(cb, EK)[:], op=ALU.mult),
                                    [k_(cb, KAP), k_(cb, EK)], ["kkapt"]),
                    lambda cb: P.op("pe", lambda e: e.matmul(ps_[:, 2 * cb:2 * cb + 2], t_(cb, RK)[:], self.cs["blkcols"][:], start=True, stop=True),
                                    [k_(cb, RK), "c_blkcols"], [psk]),
                    lambda cb: P.op("dve", lambda e: e.tensor_tensor(out=bt[:, cb, :], in0=t_(cb, BB)[:], in1=t_(cb, EINV)[:], op=ALU.mult),
                                    [k_(cb, BB), k_(cb, EINV)], ["kbt"]),
                    lambda cb: P.op("pe", lambda e: e.transpose(pvv[cb // 4][0][:, SL(cb)], vp(cb), ident[:]), ["kzs", "c_ident"], [pvv[cb // 4][1]]),
                    lambda cb: P.op("act", lambda e: e.activation(out=t_(cb, ER)[:, 0:64], in_=kt_[:, cb, 0:64], func=AF.Identity, scale=PC[:, cb, 0:1]),
                                    ["kkt", "kPC", k_(cb, ER)], [k_(cb, ER)]),
                    lambda cb: P.op("act", lambda e: e.activation(out=t_(cb, ER)[:, 64:128], in_=kt_[:, cb, 64:128], func=AF.Identity, scale=PC[:, cb, 1:2]),
                                    ["kkt", "kPC", k_(cb, ER)], [k_(cb, ER)]),
                    lambda cb: P.op("pool", lambda e: e.tensor_scalar(out=t_(cb, EINV)[:, 0:64], in0=bt[:, cb, 0:64], scalar1=PC[:, cb, 0:1], scalar2=-1.0,
                                                                      op0=ALU.mult, op1=ALU.mult), ["kbt", "kPC", k_(cb, EINV)], [k_(cb, EINV)]),
                    lambda cb: P.op("pool", lambda e: e.tensor_scalar(out=t_(cb, EINV)[:, 64:128], in0=bt[:, cb, 64:128], scalar1=PC[:, cb, 1:2], scalar2=-1.0,
                                                                      op0=ALU.mult, op1=ALU.mult), ["kbt", "kPC", k_(cb, EINV)], [k_(cb, EINV)]),
                    lambda cb: P.op("pe", lambda e: e.transpose(pkh[cb // 4][0][:, SL(cb)], t_(cb, ER)[:], ident[:]), [k_(cb, ER), "c_ident"], [pkh[cb // 4][1]]),
                    lambda cb: P.op("pe", lambda e: e.transpose(pbh[cb // 4][0][:, SL(cb)], t_(cb, EINV)[:], ident[:]), [k_(cb, EINV), "c_ident"], [pbh[cb // 4][1]]),
                ]
                def prep_gen(batch):
                    for st in steps:
                        for cb in batch:
                            st(cb)
                        yield

                def evac_hf(hf):
                    cs_ = slice(hf * 512, hf * 512 + 512)
                    P.op("act", lambda e: e.activation(out=khat[:, cs_], in_=pkh[hf][0][:, :], func=AF.Copy), [pkh[hf][1]], ["kkhat"])
                    P.op("dve", lambda e: e.tensor_copy(out=nbhat[:, cs_], in_=pbh[hf][0][:, :]), [pbh[hf][1]], ["knbhat"])
                    P.op("act", lambda e: e.activation(out=Vtok[:, cs_], in_=pvv[hf][0][:, :], func=AF.Copy), [pvv[hf][1]], ["kVtok"])

                for _ in prep_gen(range(0, 4)):
                    pass
                evac_hf(0)
                prep_b1 = prep_gen(range(4, 8))
                if getattr(self, "rw_stop", 99) <= 3:
                    for _ in prep_b1:
                        pass
                if getattr(self, "rw_stop", 99) <= 3:
                    continue
                v3 = lambda pb: pb.rearrange("p (h t) -> p h t", h=4)
                specs = [(bt, kapt, "kbt", "kkapt"), (kapt, bt, "kkapt", "kbt"), (kt_, kapt, "kkt", "kkapt"), (kt_, rt, "kkt", "krt"),
                         (bt, rt, "kbt", "krt")]
                def dbl_gen(pair):
                    grp = []
                    for gi2 in range(2):
                        hp_, q_ = gi2, pair
                        hg = 2 * hp_ + q_
                        hs_ = slice(hp_ * 8 + 4 * q_, hp_ * 8 + 4 * q_ + 4)
                        X = Xall[:, hs_, :]
                        xk = "kX%d" % hg
                        Ag, ATg, XTg = A[gi2], AT[gi2], XT[gi2]
                        ak = lambda c, gi2=gi2: "kA%d_%d" % (gi2, c)
                        atk = lambda c, gi2=gi2: "kAT%d_%d" % (gi2, c)
                        xtk = "kXT%d" % gi2
                        dsts = [(Ag[0][:], ak(0)), (ATg[0][:], atk(0)), (Mk[:, hs_, :], "kMk"), (Akr[:, hs_, :], "kAkr"), (nAbr[:, hs_, :], "knAbr")]
                        for mi, (la, ra, lk, rk_) in enumerate(specs):
                            pb, pk = nbank()
                            for hh in range(4):
                                cb, b0 = 4 * q_ + hh, hp_ * 64
                                P.op("pe", lambda e: e.matmul(pb[:, hh * 128:(hh + 1) * 128], la[b0:b0 + 64, cb, :], ra[b0:b0 + 64, cb, :], start=True,
                                                              stop=True), [lk, rk_], [pk])
                            dst, dk = dsts[mi]
                            P.op("dve", lambda e: e.tensor_tensor(out=dst, in0=v3(pb), in1=bc(rm[:, mi, :].unsqueeze(1), [128, 4, 128]), op=ALU.mult),
                                 [pk, "krm"], [dk])
                        P.op("pool", lambda e: e.tensor_tensor(out=X, in0=Ag[0][:], in1=bc(ident[:].unsqueeze(1), [128, 4, 128]), op=ALU.add),
                             [ak(0), "c_ident"], [xk])
                        P.op("pool", lambda e: e.tensor_tensor(out=XTg[:], in0=ATg[0][:], in1=bc(ident[:].unsqueeze(1), [128, 4, 128]), op=ALU.add),
                             [atk(0), "c_ident"], [xtk])
                        grp.append((X, xk, Ag, ATg, XTg, ak, atk, xtk))
                        yield
                    cur = 0
                    for lev in range(1, 6):
                        last = (lev == 5)
                        for (X, xk, Ag, ATg, XTg, ak, atk, xtk) in grp:
                            ca, cat = Ag[cur], ATg[cur]
                            p1, p1k = nbank()
                            for hh in range(4):
                                P.op("pe", lambda e: e.matmul(p1[:, hh * 128:(hh + 1) * 128], cat[:, hh, :], ca[:, hh, :], start=True, stop=True),
                                     [ak(cur), atk(cur)], [p1k])
                            if not last:
                                p2, p2k = nbank()
                                for hh in range(4):
                                    P.op("pe", lambda e: e.matmul(p2[:, hh * 128:(hh + 1) * 128], ca[:, hh, :], cat[:, hh, :], start=True, stop=True),
                                         [ak(cur), atk(cur)], [p2k])
                            P.op("act", lambda e: e.activation(out=Ag[1 - cur][:], in_=v3(p1), func=AF.Copy), [p1k], [ak(1 - cur)])
                            if not last:
                                P.op("act", lambda e: e.activation(out=ATg[1 - cur][:], in_=v3(p2), func=AF.Copy), [p2k], [atk(1 - cur)])
                        yield
                        for (X, xk, Ag, ATg, XTg, ak, atk, xtk) in grp:
                            na = Ag[1 - cur]
                            p3, p3k = nbank()
                            for hh in range(4):
                                P.op("pe", lambda e: e.matmul(p3[:, hh * 128:(hh + 1) * 128], XTg[:, hh, :], na[:, hh, :], start=True, stop=True),
                                     [xtk, ak(1 - cur)], [p3k])
                            if not last:
                                p4, p4k = nbank()
                                for hh in range(4):
                                    P.op("pe", lambda e: e.matmul(p4[:, hh * 128:(hh + 1) * 128], na[:, hh, :], XTg[:, hh, :], start=True, stop=True),
                                         [xtk, ak(1 - cur)], [p4k])
                            P.op("dve", lambda e: e.tensor_tensor(out=X, in0=v3(p3), in1=X, op=ALU.add), [p3k, xk], [xk])
                            if not last:
                                P.op("dve", lambda e: e.tensor_tensor(out=XTg[:], in0=v3(p4), in1=XTg[:], op=ALU.add), [p4k, xtk], [xtk])
                        cur = 1 - cur
                        yield

                nb_pool[0] = [5, 6, 7]
                g1, g2 = dbl_gen(0), prep_b1
                d1 = d2 = False
                while not (d1 and d2):
                    if not d1:
                        try:
                            next(g1)
                        except StopIteration:
                            d1 = True
                    if not d2:
                        try:
                            next(g2)
                        except StopIteration:
                            d2 = True
                evac_hf(1)
                P.op("dve", lambda e: e.tensor_copy(out=s16[:], in_=ps_[:, 0:16]), [psk], ["ks16"])
                nb_pool[0] = list(range(8))
                for _ in dbl_gen(1):
                    pass
                if getattr(self, "rw_stop", 99) <= 4:
                    continue
                pR = [nbank(), nbank()]
                pU = [nbank(), nbank()]
                pY = [nbank(), nbank()]
                pS, pSk = nbank()
                for cp in range(2):
                    c0 = cp * 64
                    cr = slice(c0, c0 + 64)
                    for h in range(16):
                        cb, hp = h // 2, h % 2
                        hsx = hp * 8 + cb
                        pb, pk = pR[h // 8]
                        hc = slice((h % 8) * 64, (h % 8) * 64 + 64)
                        hcf = slice(h * 64, h * 64 + 64)
                        P.op("pe", lambda e: e.matmul(pb[cr, hc], kapt[:, cb, cr], S2[:, cb, hp, :], start=True, stop=False),
                             ["kkapt", "kS2"], [pk])
                        P.op("pe", lambda e: e.matmul(pb[cr, hc], Mk[:, hsx, cr], Vtok[:, hcf], start=False, stop=True), ["kMk", "kVtok"], [pk])
                    P.op("act", lambda e: e.activation(out=RHS[cr, 0:512], in_=pR[0][0][cr, :], func=AF.Copy), [pR[0][1]], ["kRHS"])
                    P.op("dve", lambda e: e.tensor_copy(out=RHS[cr, 512:1024], in_=pR[1][0][cr, :]), [pR[1][1]], ["kRHS"])
                    for h in range(16):
                        hsx = (h % 2) * 8 + h // 2
                        pb, pk = pU[h // 8]
                        hc = slice((h % 8) * 64, (h % 8) * 64 + 64)
                        hcf = slice(h * 64, h * 64 + 64)
                        P.op("pe", lambda e: e.matmul(pb[cr, hc], Xall[:, hsx, cr], RHS[:, hcf], start=True, stop=True), ["kX0", "kX1", "kX2", "kX3", "kRHS"], [pk])
                    P.op("act", lambda e: e.activation(out=UT[cr, 0:512], in_=pU[0][0][cr, :], func=AF.Copy), [pU[0][1]], ["kUT"])
                    P.op("dve", lambda e: e.tensor_copy(out=UT[cr, 512:1024], in_=pU[1][0][cr, :]), [pU[1][1]], ["kUT"])
                    for h in range(16):
                        cb, hp = h // 2, h % 2
                        hsx = hp * 8 + cb
                        pb, pk = pY[h // 8]
                        hc = slice((h % 8) * 64, (h % 8) * 64 + 64)
                        hcf = slice(h * 64, h * 64 + 64)
                        P.op("pe", lambda e: e.matmul(pb[cr, hc], rt[:, cb, cr], S2[:, cb, hp, :], start=True, stop=False),
                             ["krt", "kS2"], [pk])
                        P.op("pe", lambda e: e.matmul(pb[cr, hc], Akr[:, hsx, cr], Vtok[:, hcf], start=False, stop=False), ["kAkr", "kVtok"], [pk])
                        P.op("pe", lambda e: e.matmul(pb[cr, hc], nAbr[:, hsx, cr], UT[:, hcf], start=False, stop=True), ["knAbr", "kUT"], [pk])
                    for h in range(16):
                        cb, b0 = h // 2, (h % 2) * 64
                        hcf = slice(h * 64, h * 64 + 64)
                        P.op("pe", lambda e: e.matmul(pS[b0:b0 + 64, cb * 64:(cb + 1) * 64], khat[cr, hcf], Vtok[cr, hcf], start=True, stop=False),
                             ["kkhat", "kVtok"], [pSk])
                        P.op("pe", lambda e: e.matmul(pS[b0:b0 + 64, cb * 64:(cb + 1) * 64], nbhat[cr, hcf], UT[cr, hcf], start=False, stop=True),
                             ["knbhat", "kUT"], [pSk])
                    for cb in range(8):
                        P.op("dve", lambda e: e.scalar_tensor_tensor(out=S[:, cb, :], in0=S[:, cb, :], scalar=PC[:, cb, cp:cp + 1],
                                                                     in1=pS[:, cb * 64:(cb + 1) * 64], op0=ALU.mult, op1=ALU.add),
                             ["kS", "kPC", pSk], ["kS"])
                    P.op("act", lambda e: e.activation(out=S2[0:64, :, 0, :], in_=S[0:64, :, :], func=AF.Copy), ["kS"], ["kS2"])
                    P.op("pool", lambda e: e.tensor_copy(out=S2[64:128, :, 1, :], in_=S[64:128, :, :]), ["kS"], ["kS2"])
                if getattr(self, "rw_stop", 99) <= 5:
                    continue
                for hf in range(2):
                    pb, pk = pY[hf]
                    P.op("dve", lambda e: e.tensor_reduce(out=m16[:, hf * 8:(hf + 1) * 8], in_=pb.rearrange("p (h v) -> p h v", h=8), axis=AX.X,
                                                          op=ALU.add), [pk], ["km16"])
                P.op("dve", lambda e: e.tensor_scalar(out=m16[:], in0=m16[:], scalar1=1.0 / 64, scalar2=None, op0=ALU.mult), ["km16"], ["km16"])
                for hf in range(2):
                    pb, pk = pY[hf]
                    P.op("dve", lambda e: e.tensor_tensor(out=yc[:, hf * 512:(hf + 1) * 512].rearrange("p (h v) -> p h v", h=8),
                                                          in0=pb.rearrange("p (h v) -> p h v", h=8),
                                                          in1=bc(m16[:, hf * 8:(hf + 1) * 8].unsqueeze(2), [128, 8, 64]), op=ALU.subtract),
                         [pk, "km16"], ["kyc"])
                P.op("act", lambda e: e.activation(out=ysq[:], in_=yc[:], func=AF.Square), ["kyc"], ["kysq"])
                P.op("dve", lambda e: e.tensor_reduce(out=v16[:], in_=ysq[:].rearrange("p (h v) -> p h v", h=16), axis=AX.X, op=ALU.add),
                     ["kysq"], ["kv16"])
                P.op("act", lambda e: e.activation(out=v16[:], in_=v16[:], func=AF.Sqrt, scale=1.0 / 64, bias=self.eps_sb[:, 1:2]),
                     ["kv16", "eps"], ["kv16"])
                P.op("dve", lambda e: e.reciprocal(v16[:], v16[:]), ["kv16"], ["kv16"])
                y3 = lambda t_: t_[:].rearrange("p (h v) -> p h v", h=16)
                P.op("dve", lambda e: e.tensor_tensor(out=y3(yc), in0=y3(yc), in1=bc(v16[:].unsqueeze(2), [128, 16, 64]), op=ALU.mult),
                     ["kyc", "kv16"], ["kyc"])
                P.op("pool", lambda e: e.tensor_tensor(out=yc[:], in0=yc[:], in1=lnw[:], op=ALU.mult), ["kyc", "klnw"], ["kyc"])
                P.op("pool", lambda e: e.tensor_tensor(out=yc[:], in0=yc[:], in1=lnb[:], op=ALU.add), ["kyc", "klnb"], ["kyc"])
                P.op("dve", lambda e: e.tensor_tensor(out=y3(ysq), in0=y3(Vtok), in1=bc(s16[:].unsqueeze(2), [128, 16, 64]), op=ALU.mult),
                     ["kVtok", "ks16", "kysq"], ["kysq"])
                P.op("pool", lambda e: e.tensor_tensor(out=yc[:], in0=yc[:], in1=ysq[:], op=ALU.add), ["kyc", "kysq"], ["kyc"])
                P.op("dve", lambda e: e.tensor_tensor(out=yo[:], in0=yc[:], in1=gtok[:], op=ALU.mult), ["kyc", "kgtok"], ["kyo"])
                pt_, ptk = nbank()
                ptb = pt_.bitcast(BF16)
                for cb in range(8):
                    P.op("pe", lambda e: e.transpose(ptb[:, cb * 128:(cb + 1) * 128], yo[:, cb * 128:(cb + 1) * 128], self.ident_bf[:]),
                         ["kyo", "ident_bf"], [ptk])
                P.op("act", lambda e: e.activation(out=yst[:], in_=ptb.rearrange("p (c t) -> p c t", c=8), func=AF.Copy), [ptk], ["kyst"])
                P.dma("sp", yav[:, :, t0:t0 + 128], yst[:], ["kyst"], ["ya"], "kyst")


def common_inputs(inp, NL, LP):
    m = {}
    for n in ["w_in", "rw_w_up", "rw_a_up", "rw_g_up", "mla_w_uq", "mla_w_ukv", "w_br_rwkv", "w_br_ret", "w_br_mla", "w_out",
              "w_gate_up", "w_down", "rw_ln_w", "rw_ln_b"]:
        m[n] = np.ascontiguousarray(np.asarray(inp[n], np.float32)[:NL])
    m["vecs"] = np.stack([pack_vecs(inp, l) for l in range(NL)], 0)
    m["final_norm_pm"] = pm(inp["final_norm"], 16)
    m["meta"] = np.ascontiguousarray(np.asarray(inp["meta_tokens"], np.float32))
    cs = make_consts(LP)
    for n in CONST_NAMES:
        m["c_" + n] = np.ascontiguousarray(cs[n].astype(np.float32))
    return m


def build_full(L, NL):
    B = Builder(L, NL)
    B.setup_globals()
    B.convert_layer(0)
    B.prologue()
    for l in range(NL):
        B.phase_inproj(l)
        B.phase_rwkv(l)
        B.phase_ret(l)
        B.phase_mla_prep(l)
        B.phase_mla_attn(l)
        B.phase_merge(l)
        if l + 1 < NL:
            B.convert_layer(l + 1)
        B.phase_ffn(l)
    B.epilogue()
    B.P.finish()
    return B


def kernel(**inputs):
    inp = {k: np.asarray(v) for k, v in inputs.items()}
    nb, seq, _ = inp["x"].shape
    L = seq + NMETA
    NL = inp["w_in"].shape[0]
    B = build_full(L, NL)
    cm = common_inputs(inp, NL, B.LP)
    n_cores = 8
    in_maps = []
    for c in range(n_cores):
        m = dict(cm)
        m["x"] = np.ascontiguousarray(inp["x"][c % nb], dtype=np.float32)
        in_maps.append(m)
    res = run_bass_kernel_spmd(B.nc, in_maps, core_ids=list(range(n_cores)))
    out = np.stack([np.asarray(res.results[b]["out"], np.float32) for b in range(nb)], 0)
    return out
```

```python
import math
from contextlib import ExitStack, contextmanager
import numpy as np
import ml_dtypes
import concourse.bass as bass
import concourse.mybir as mybir
from concourse.bass_utils import run_bass_kernel_spmd

F32 = mybir.dt.float32
BF16 = mybir.dt.bfloat16
AF = mybir.ActivationFunctionType
ALU = mybir.AluOpType
AX = mybir.AxisListType

D = 2048
NMETA = 16
RW_H, RW_N = 16, 64
IN_COLS = 14592
FFN_H = 5632
EPS = 1e-6
GN_EPS = 64 * 1e-5
C0 = math.exp(-0.5)
T = 384

ENG = ["pe", "act", "dve", "pool", "sp"]


class Prog:
    def __init__(self, nc):
        self.nc = nc
        self.es = ExitStack()
        self.scopes = [self.es]
        self.sems = {}
        self.count = {}
        self.is_dma_sem = {}
        self.seen = {e: {} for e in ENG}
        self.last_w = {}
        self.readers = {}
        self.engs = {"pe": nc.tensor, "act": nc.scalar, "dve": nc.vector, "pool": nc.gpsimd, "sp": nc.sync}
        self.uid = 0
        self.dmap = {}
        self.free_phys = []
        self.nphys = 0
        self.scope_names = [[]]
        for e in ENG[:4]:
            self._mksem("E_" + e, False)

    def _dsem(self, name, fresh=False):
        if name in self.dmap:
            return self.dmap[name]
        if self.free_phys and not fresh:
            ph = self.free_phys.pop()
        else:
            ph = self._mksem("D%d" % self.nphys, True)
            self.nphys += 1
        self.dmap[name] = ph
        self.scope_names[-1].append(name)
        return ph

    def _mksem(self, name, is_dma):
        if name not in self.sems:
            self.sems[name] = self.es.enter_context(self.nc.semaphore(name))
            self.count[name] = 0
            self.is_dma_sem[name] = is_dma
        return name

    def sb(self, name, shape, dt):
        self.uid += 1
        return self.scopes[-1].enter_context(self.nc.sbuf_tensor("%s_%d" % (name, self.uid), list(shape), dt))

    def ps(self, name, shape, dt):
        self.uid += 1
        return self.scopes[-1].enter_context(self.nc.psum_tensor("%s_%d" % (name, self.uid), list(shape), dt))

    @contextmanager
    def scope(self):
        es = ExitStack()
        self.scopes.append(es)
        self.scope_names.append([])
        try:
            yield
        finally:
            self.barrier()
            self.scopes.pop()
            for n in self.scope_names.pop():
                self.free_phys.append(self.dmap.pop(n))
            es.close()

    def barrier(self):
        for eng in ENG:
            e = self.engs[eng]
            for s, c in self.count.items():
                if self.seen[eng].get(s, 0) < c:
                    e.wait_ge(self.sems[s], c)
                    self.seen[eng][s] = c

    def op(self, eng, fn, reads=(), writes=(), dsem=None):
        deps = {}

        def add(tok):
            if tok is None:
                return
            s, v = tok
            if deps.get(s, 0) < v:
                deps[s] = v

        for k in reads:
            add(self.last_w.get(k))
        for k in writes:
            add(self.last_w.get(k))
            for s, v in self.readers.get(k, {}).items():
                add((s, v))
        e = self.engs[eng]
        for s, v in deps.items():
            if s == "E_pe" and eng == "pe":
                continue
            if self.seen[eng].get(s, 0) < v:
                if self.is_dma_sem[s]:
                    v = self.count[s]
                e.wait_ge(self.sems[s], v)
                self.seen[eng][s] = v
        if dsem is None:
            s = "E_" + eng
            inc = 1
        else:
            s = self._dsem(dsem, fresh=(eng == "pool"))
            inc = 16
        self.count[s] += inc
        tok = (s, self.count[s])
        ins = fn(e)
        ins.then_inc(self.sems[s], inc)
        for k in writes:
            self.last_w[k] = tok
            self.readers[k] = {}
        for k in reads:
            r = self.readers.setdefault(k, {})
            if r.get(s, 0) < tok[1]:
                r[s] = tok[1]
        return tok

    def dma(self, eng, out, in_, reads, writes, dsem, **kw):
        return self.op(eng, lambda e: e.dma_start(out=out, in_=in_, **kw), reads, writes, dsem)

    def finish(self):
        self.barrier()
        self.es.close()


def make_consts(LP):
    c = {}
    c["ident"] = np.eye(128, dtype=np.float32)
    blk = np.zeros((128, 128), np.float32)
    blk[:64, :64] = 1
    blk[64:, 64:] = 1
    c["blkones"] = blk
    bc = np.zeros((128, 2), np.float32)
    bc[:64, 0] = 1
    bc[64:, 1] = 1
    c["blkcols"] = bc
    s128 = np.zeros((128, 128), np.float32)
    for i in range(64):
        s128[i, i + 64] = 1
        s128[i + 64, i] = 1
    c["swap128"] = s128
    s64 = np.zeros((128, 128), np.float32)
    for b in range(2):
        for i in range(32):
            s64[b * 64 + i, b * 64 + i + 32] = 1
            s64[b * 64 + i + 32, b * 64 + i] = 1
    c["swap64"] = s64
    pos = np.arange(LP, dtype=np.float32)
    inv = (np.float32(10000.0) ** (-np.arange(0, 128, 2, dtype=np.float32) / np.float32(128))).astype(np.float32)
    ang = (pos[None, :] * inv[:, None]).astype(np.float32)
    cs, sn = np.cos(ang).astype(np.float32), np.sin(ang).astype(np.float32)
    c["ret_cos"] = np.concatenate([cs, cs], 0)
    c["ret_sin"] = np.concatenate([-sn, sn], 0)
    inv = (np.float32(10000.0) ** (-np.arange(0, 64, 2, dtype=np.float32) / np.float32(64))).astype(np.float32)
    ang = (pos[None, :] * inv[:, None]).astype(np.float32)
    cs, sn = np.cos(ang).astype(np.float32), np.sin(ang).astype(np.float32)
    c["mla_cos"] = np.concatenate([cs, cs, cs, cs], 0)
    c["mla_sin"] = np.concatenate([-sn, sn, -sn, sn], 0)
    c["rope_tab"] = np.ascontiguousarray(np.stack([c["ret_cos"], c["ret_sin"], c["mla_cos"], c["mla_sin"]], 1))
    k = np.arange(128)[:, None]
    q = np.arange(T)[None, :]
    mm = np.zeros((128, 3, T), np.float32)
    for m in range(3):
        mm[:, m, :] = (q >= 128 * m + k)
    c["mla_mask"] = mm
    rt = np.zeros((8, 128, 512), np.float32)
    lg = np.log1p(-np.exp2(-5.0 - np.arange(8, dtype=np.float64)))
    for h in range(8):
        rt[h, :, 0:T] = np.exp(lg[h] * np.arange(T, dtype=np.float64))[None, :]
        rt[h, :, T:T + 128] = np.exp(-lg[h] * np.arange(128, dtype=np.float64))[None, :]
    c["ret_tab"] = rt
    c["ret_lg"] = lg
    j = np.arange(128)[:, None]
    t = np.arange(128)[None, :]
    same = (j // 64) == (t // 64)
    rm = np.zeros((128, 5, 128), np.float32)
    rm[:, 0, :] = -1.0 * (same & (j < t))
    rm[:, 1, :] = -1.0 * (same & (j > t))
    rm[:, 2, :] = 1.0 * (same & (j < t))
    rm[:, 3, :] = 1.0 * (same & (j <= t))
    rm[:, 4, :] = -1.0 * (same & (j <= t))
    c["rw_masks"] = rm
    rs = np.ones((128, 128), np.float32)
    rs[:, 0] = 0
    rs[:, 64] = 0
    c["rw_reset"] = rs
    return c


CONST_NAMES = ["ident", "blkones", "blkcols", "swap128", "swap64", "rope_tab",
               "mla_mask", "ret_tab", "rw_masks", "rw_reset"]

VEC_LAYOUT = [("norm_mix", 16), ("norm_ffn", 16), ("mu", 28), ("w0", 8), ("a0", 8), ("k_k", 8), ("k_a", 8),
              ("r_k", 8), ("nq", 4), ("nkv", 2)]
VOFF = {}
_o = 0
for _n, _w in VEC_LAYOUT:
    VOFF[_n] = _o
    _o += _w
NV = _o


def pm(v, n):
    return np.ascontiguousarray(np.asarray(v, np.float32).reshape(n, 128).T)


def pack_vecs(inp, l):
    out = np.zeros((128, NV), np.float32)

    def put(name, arr):
        out[:, VOFF[name]:VOFF[name] + arr.shape[1]] = arr

    put("norm_mix", pm(inp["norm_mix"][l], 16))
    put("norm_ffn", pm(inp["norm_ffn"][l], 16))
    mu = np.asarray(inp["rw_mu"][l], np.float32)
    mu28 = np.zeros(28 * 128, np.float32)
    mu28[0:3072] = mu[0:3072]
    mu28[3072:3072 + 96] = mu[3072:3168]
    mu28[3200:3200 + 96] = mu[3168:3264]
    mu28[3328:3328 + 256] = mu[3264:3520]
    put("mu", pm(mu28, 28))
    put("w0", pm(inp["rw_w0"][l], 8))
    put("a0", pm(inp["rw_a0"][l], 8))
    put("k_k", pm(inp["rw_k_k"][l], 8))
    put("k_a", pm(inp["rw_k_a"][l], 8))
    put("r_k", pm(inp["rw_r_k"][l], 8))
    put("nq", pm(inp["mla_norm_q"][l], 4))
    put("nkv", pm(inp["mla_norm_kv"][l], 2))
    return out


IN_GROUPS = []
for _seg, _base in (("rw_r", 0), ("rw_k", 1024), ("rw_v", 2048)):
    for _i in range(2):
        IN_GROUPS.append((_seg, _base + 512 * _i, 512, _i))
IN_GROUPS.append(("rw_lora", 3072, 448, 0))
for _seg, _base in (("ret_q", 3520), ("ret_k", 4544), ("ret_v", 5568), ("ret_g", 6592)):
    for _i in range(2):
        IN_GROUPS.append((_seg, _base + 512 * _i, 512, _i))
IN_GROUPS.append(("mla_q", 7616, 512, 0))
IN_GROUPS.append(("mla_kv", 8128, 320, 0))
for _i in range(12):
    IN_GROUPS.append(("gate", 8448 + 512 * _i, 512, _i))


class WStream:
    def __init__(self, B, name, items, KC, width, nbuf=2):
        self.B = B
        self.P = B.P
        self.items = items
        self.name = name
        self.bufs = [B.P.sb(name, [128, KC, width], BF16) for _ in range(nbuf)]
        self.nbuf = nbuf
        self.next = 0

    def prefetch(self):
        if self.next >= len(self.items):
            return
        i = self.next
        self.next += 1
        ap, key, kc, w = self.items[i]
        b = i % self.nbuf
        bk = "%s%d" % (self.name, b)
        self.P.dma("sp", self.bufs[b][:, :kc, :w], ap, [key], [bk], bk)

    def get(self, i):
        while self.next <= i:
            self.prefetch()
        b = i % self.nbuf
        return self.bufs[b], "%s%d" % (self.name, b)


class Builder:
    def __init__(self, L, NL, debug=(), ext_in=()):
        self.ext_in = set(ext_in)
        self.L = L
        self.SEQ = L - NMETA
        self.LP = ((L + T - 1) // T) * T
        self.NT = self.LP // T
        self.N128 = self.LP // 128
        self.NL = NL
        self.debug = set(debug)
        nc = self.nc = bass.Bass("TRN2", target_bir_lowering=False)
        self.P = Prog(nc)
        LP, SEQ = self.LP, self.SEQ

        def di(n, s, dt=F32):
            return nc.dram_tensor(n, list(s), dt, kind="ExternalInput").ap()

        self.x = di("x", [SEQ, D])
        self.meta = di("meta", [NMETA, D])
        self.w_in = di("w_in", [NL, D, IN_COLS])
        self.w_up = di("rw_w_up", [NL, 96, 1024])
        self.a_up = di("rw_a_up", [NL, 96, 1024])
        self.g_up = di("rw_g_up", [NL, 256, 1024])
        self.w_uq = di("mla_w_uq", [NL, 512, 1536])
        self.w_ukv = di("mla_w_ukv", [NL, 256, 2048])
        self.w_bra = di("w_br_rwkv", [NL, 1024, D])
        self.w_brb = di("w_br_ret", [NL, 1024, D])
        self.w_brc = di("w_br_mla", [NL, 1024, D])
        self.w_out = di("w_out", [NL, D, D])
        self.w_gu = di("w_gate_up", [NL, D, 2 * FFN_H])
        self.w_dn = di("w_down", [NL, FFN_H, D])
        self.vecs = di("vecs", [NL, 128, NV])
        self.lnw = di("rw_ln_w", [NL, 1024])
        self.lnb = di("rw_ln_b", [NL, 1024])
        self.fng = di("final_norm_pm", [128, 16])
        self.cst = {}
        cs = make_consts(LP)
        for n in CONST_NAMES:
            self.cst[n] = di("c_" + n, cs[n].shape)
        self.out = nc.dram_tensor("out", [SEQ, D], F32, kind="ExternalOutput").ap()
        self.hT = self.ds("hT", [D, LP], F32)
        self.zrw = self.ds("zrw", [3584, 1 + LP], F32)
        self.retq = self.ds("retq", [1024, LP], BF16)
        self.retk = self.ds("retk", [1024, LP], BF16)
        self.retv = self.ds("retv", [LP, 1024], BF16)
        self.retg = self.ds("retg", [1024, LP], F32)
        self.qd = self.ds("qd", [512, LP], F32)
        self.kvd = self.ds("kvd", [256, LP], F32)
        self.kr = self.ds("kr", [128, LP], BF16)
        self.gates = self.ds("gates", [6144, LP], BF16)
        self.ya = self.ds("ya", [1024, LP], BF16)
        self.yb = self.ds("yb", [1024, LP], BF16)
        self.yc = self.ds("yc", [1024, LP], BF16)
        self.qn = self.ds("qn", [1024, LP], BF16)
        self.qr = self.ds("qr", [512, LP], BF16)
        self.kn = self.ds("kn", [1024, LP], BF16)
        self.vm = self.ds("vm", [LP, 1024], BF16)
        self.wt = {}

    def ds(self, n, s, dt):
        kind = "ExternalOutput" if n in self.debug else ("ExternalInput" if n in self.ext_in else "Internal")
        return self.nc.dram_tensor(n, list(s), dt, kind=kind).ap()

    def conv(self, l, tag, src2d, cs, width, KC):
        name = "wt%d_%s_%d" % (l, tag, cs)
        wt = self.nc.dram_tensor(name, [128, KC, width], BF16, kind="Internal").ap()
        self.P.dma("pool", wt, src2d.rearrange("(kc p) c -> p kc c", p=128)[:, :, cs:cs + width], [], [name],
                   "cv%d_%s" % (l, tag))
        return (wt, name, KC, width)

    def convert_layer(self, l):
        w = {}
        w["in"] = [self.conv(l, "in", self.w_in[l], cs, wd, 16) for (_, cs, wd, _) in IN_GROUPS]
        w["gup"] = [self.conv(l, "g", self.g_up[l], cs, 512, 2) for cs in (0, 512)]
        w["uq"] = [self.conv(l, "uq", self.w_uq[l], cs, 384, 4) for cs in range(0, 1536, 384)]
        w["ukv"] = [self.conv(l, "ukv", self.w_ukv[l], cs, 512, 2) for cs in range(0, 2048, 512)]
        w["bra"] = [self.conv(l, "bra", self.w_bra[l], cs, 256, 8) for cs in range(0, D, 256)]
        w["brb"] = [self.conv(l, "brb", self.w_brb[l], cs, 256, 8) for cs in range(0, D, 256)]
        w["brc"] = [self.conv(l, "brc", self.w_brc[l], cs, 256, 8) for cs in range(0, D, 256)]
        w["out"] = [self.conv(l, "out", self.w_out[l], cs, 256, 16) for cs in range(0, D, 256)]
        w["gate"] = [self.conv(l, "gu", self.w_gu[l], cs, 256, 16) for cs in range(0, FFN_H, 256)]
        w["up"] = [self.conv(l, "gu", self.w_gu[l], FFN_H + cs, 256, 16) for cs in range(0, FFN_H, 256)]
        w["dn"] = [self.conv(l, "dn", self.w_dn[l], cs, 256, 44) for cs in range(0, D, 256)]
        self.wt[l] = w

    def setup_globals(self):
        P = self.P
        self.psum = P.ps("psum", [128, 8, 512], F32)
        self.cs = {}
        for n in ["ident", "blkones", "blkcols", "swap128", "swap64"]:
            t = P.sb("c_" + n, list(self.cst[n].shape), F32)
            P.dma("sp", t[:], self.cst[n], [], ["c_" + n], "c_" + n)
            self.cs[n] = t
        self.ones_bf = P.sb("ones_bf", [128, 128], BF16)
        P.op("pool", lambda e: e.memset(self.ones_bf[:], 1.0), [], ["ones_bf"])
        self.ident_bf = P.sb("ident_bf", [128, 128], BF16)
        P.op("act", lambda e: e.activation(out=self.ident_bf[:], in_=self.cs["ident"][:], func=AF.Copy), ["c_ident"],
             ["ident_bf"])
        self.vec_sb = P.sb("vecs", [128, self.NL, NV], F32)
        P.dma("sp", self.vec_sb[:], self.vecs.rearrange("l p v -> p l v"), [], ["vecs"], "vecs")
        self.fng_sb = P.sb("fng", [128, 16], F32)
        P.dma("sp", self.fng_sb[:], self.fng, [], ["fng"], "fng")
        self.eps_sb = P.sb("eps", [128, 2], F32)
        P.op("pool", lambda e: e.memset(self.eps_sb[:, 0:1], EPS), [], ["eps"])
        P.op("pool", lambda e: e.memset(self.eps_sb[:, 1:2], GN_EPS), ["eps"], ["eps"])
        self.zero_sb = P.sb("zero", [128, 512], F32)
        P.op("pool", lambda e: e.memset(self.zero_sb[:], 0.0), [], ["zero"])
        P.dma("sp", self.zrw.rearrange("(b p) t -> p b t", p=128)[:, :, 0:1], self.zero_sb[:, 0:28].unsqueeze(2),
              ["zero"], ["zrw"], "zero", allow_slow_non_contiguous=True)
        for r0 in (3168, 3296):
            for c0 in range(0, self.LP + 1, 512):
                c1 = min(c0 + 512, self.LP + 1)
                P.dma("sp", self.zrw[r0:r0 + 32, c0:c1], self.zero_sb[0:32, 0:c1 - c0], ["zero"], ["zrw"], "zero")

    def bank(self, b):
        return self.psum[:, b, :], "pb%d" % b

    def vec(self, l, name, c0=0, n=1):
        o = VOFF[name] + c0
        return self.vec_sb[:, l, o:o + n]

    def fm_norm(self, hsb, hkey, KC, TT, gains, out, okey, nfeat, sq, sqkey, rstd, rkey, b):
        P = self.P
        pb, pk = self.bank(b)
        P.op("act", lambda e: e.activation(out=sq[:, :KC, :TT], in_=hsb[:, :KC, :TT], func=AF.Square), [hkey], [sqkey])
        for kc in range(KC):
            P.op("pe", lambda e, kc=kc: e.matmul(pb[:, :TT], self.ones_bf[:], sq[:, kc, :TT], start=(kc == 0),
                                                 stop=(kc == KC - 1)), [sqkey, "ones_bf"], [pk])
        P.op("act", lambda e: e.activation(out=rstd[:, :TT], in_=pb[:, :TT], func=AF.Sqrt, scale=1.0 / nfeat,
                                           bias=self.eps_sb[:, 0:1]), [pk, "eps"], [rkey])
        P.op("dve", lambda e: e.reciprocal(rstd[:, :TT], rstd[:, :TT]), [rkey], [rkey])
        for kc in range(KC):
            P.op("dve", lambda e, kc=kc: e.scalar_tensor_tensor(out=out[:, kc, :TT], in0=hsb[:, kc, :TT],
                                                                scalar=gains[:, kc:kc + 1], in1=rstd[:, :TT],
                                                                op0=ALU.mult, op1=ALU.mult), [hkey, rkey, "vecs", "fng"], [okey])

    def prologue(self):
        P = self.P
        L, LP = self.L, self.LP
        with P.scope():
            xt = [P.sb("xt", [128, D], F32) for _ in range(2)]
            hts = [P.sb("hts", [128, 16, 128], F32) for _ in range(2)]
            hTv = self.hT.rearrange("(kc p) t -> p kc t", p=128)
            for i in range(self.N128):
                r = i % 2
                xk = "xt%d" % r
                t0 = i * 128
                lo, hi = max(t0, NMETA), min(t0 + 128, L)
                if i == 0 or hi < t0 + 128:
                    P.op("pool", lambda e, r=r: e.memset(xt[r][:], 0.0), [], [xk])
                if i == 0:
                    P.dma("sp", xt[r][0:NMETA, :], self.meta, [], [xk], xk)
                if hi > lo:
                    P.dma("sp", xt[r][lo - t0:hi - t0, :], self.x[lo - NMETA:hi - NMETA, :], [], [xk], xk)
                for q in range(4):
                    pb, pk = self.bank(4 * r + q)
                    for j in range(4):
                        kc = 4 * q + j
                        P.op("pe", lambda e, pb=pb, j=j, kc=kc: e.transpose(pb[:, j * 128:(j + 1) * 128],
                                                                            xt[r][:, kc * 128:(kc + 1) * 128],
                                                                            self.cs["ident"][:]), [xk, "c_ident"], [pk])
                    eng = "act" if q % 2 else "dve"
                    if eng == "act":
                        P.op("act", lambda e, pb=pb, q=q: e.activation(out=hts[r][:, 4 * q:4 * q + 4, :],
                                                                       in_=pb.rearrange("p (j t) -> p j t", j=4),
                                                                       func=AF.Copy), [pk], ["hts%d" % r])
                    else:
                        P.op("dve", lambda e, pb=pb, q=q: e.tensor_copy(out=hts[r][:, 4 * q:4 * q + 4, :],
                                                                        in_=pb.rearrange("p (j t) -> p j t", j=4)),
                             [pk], ["hts%d" % r])
                P.dma("sp", hTv[:, :, t0:t0 + 128], hts[r][:], ["hts%d" % r], ["hT"], "hts%d" % r)

    def epilogue(self):
        P = self.P
        L = self.L
        with P.scope():
            hs = [P.sb("ehs", [128, 16, 128], F32) for _ in range(2)]
            sq = P.sb("esq", [128, 16, 128], BF16)
            rstd = P.sb("erstd", [128, 128], F32)
            un = [P.sb("eun", [128, 16, 128], F32) for _ in range(2)]
            osb = [P.sb("eosb", [128, D], F32) for _ in range(2)]
            hTv = self.hT.rearrange("(kc p) t -> p kc t", p=128)
            for i in range(self.N128):
                t0 = i * 128
                lo, hi = max(t0, NMETA), min(t0 + 128, L)
                if hi <= lo:
                    continue
                r = i % 2
                P.dma("sp", hs[r][:], hTv[:, :, t0:t0 + 128], ["hT"], ["ehs%d" % r], "ehs%d" % r)
                self.fm_norm(hs[r], "ehs%d" % r, 16, 128, self.fng_sb, un[r], "eun%d" % r, D, sq, "esq", rstd, "erstd", 4 * r)
                for q in range(4):
                    pb, pk = self.bank(4 * r + q)
                    for j in range(4):
                        kc = 4 * q + j
                        P.op("pe", lambda e, pb=pb, j=j, kc=kc: e.transpose(pb[:, j * 128:(j + 1) * 128], un[r][:, kc, :],
                                                                            self.cs["ident"][:]), ["eun%d" % r, "c_ident"], [pk])
                    if q % 2:
                        P.op("act", lambda e, pb=pb, q=q: e.activation(out=osb[r][:, q * 512:(q + 1) * 512], in_=pb,
                                                                       func=AF.Copy), [pk], ["eosb%d" % r])
                    else:
                        P.op("dve", lambda e, pb=pb, q=q: e.tensor_copy(out=osb[r][:, q * 512:(q + 1) * 512], in_=pb),
                             [pk], ["eosb%d" % r])
                P.dma("sp", self.out[lo - NMETA:hi - NMETA, :], osb[r][lo - t0:hi - t0, :], ["eosb%d" % r], ["out"],
                      "eosb%d" % r)

    def phase_inproj(self, l):
        P = self.P
        LP = self.LP
        w = self.wt[l]["in"]
        with P.scope():
            hs = [P.sb("ahs", [128, 16, T], F32) for _ in range(1)]
            sq = P.sb("asq", [128, 16, T], BF16)
            rstd = P.sb("arstd", [128, T], F32)
            u = P.sb("au", [128, 16, T], BF16)
            stg = [P.sb("astg", [128, 4, T], F32) for _ in range(3)]
            stgb = [P.sb("astgb", [128, 4, T], BF16) for _ in range(2)]
            stv = [P.sb("astv", [128, 3, 512], BF16) for _ in range(2)]
            zf = [P.sb("azf", [128, T], F32) for _ in range(2)]
            t1 = [P.sb("at1", [128, T], F32) for _ in range(2)]
            t2 = [P.sb("at2", [128, T], F32) for _ in range(2)]
            tabs = [P.sb("atab", [128, 4, T], F32) for _ in range(2)]
            ws = WStream(self, "aw", [], 16, 512, nbuf=3)
            ws.items = [w[gi] for _s in range(self.NT) for gi in range(len(IN_GROUPS))]
            hTv = self.hT.rearrange("(kc p) t -> p kc t", p=128)
            cnt = {"stg": 0, "stgb": 0, "stv": 0, "rot": 0, "bank": 0}
            gains = self.vec(l, "norm_mix", 0, 16)

            def nbank():
                b = cnt["bank"] % 6
                cnt["bank"] += 1
                return self.bank(b)

            def mm16(pb, pk, wb, wk, o, wd, out_rows=None):
                dst = pb[:wd, :T] if out_rows is None else pb[out_rows[0]:out_rows[1], :T]
                for kc in range(16):
                    P.op("pe", lambda e, kc=kc: e.matmul(dst, wb[:, kc, o:o + wd], u[:, kc, :], start=(kc == 0),
                                                         stop=(kc == 15)), [wk, "au"], [pk])

            def copy_evac(i, pb, pk, wd, dst, dk, func=AF.Copy):
                if func == AF.Copy and i % 2 == 0:
                    P.op("dve", lambda e: e.tensor_copy(out=dst, in_=pb[:wd, :T]), [pk], [dk])
                else:
                    P.op("act", lambda e: e.activation(out=dst, in_=pb[:wd, :T], func=func), [pk], [dk])

            for s in range(self.NT):
                r = 0
                hk = "ahs0"
                ts0 = s * T
                P.dma("sp", hs[0][:], hTv[:, :, ts0:ts0 + T], ["hT"], ["ahs0"], "ahs0")
                tb = tabs[s % 2]
                tk = "atab%d" % (s % 2)
                P.dma("sp", tb[:], self.cst["rope_tab"][:, :, ts0:ts0 + T], [], [tk], tk)
                self.fm_norm(hs[r], hk, 16, T, gains, u, "au", D, sq, "asq", rstd, "arstd", 7)
                for gi, (seg, cs, gw, gidx) in enumerate(IN_GROUPS):
                    wb, wk = ws.get(s * len(IN_GROUPS) + gi)
                    while ws.next <= s * len(IN_GROUPS) + gi + 2:
                        if ws.next >= len(ws.items):
                            break
                        ws.prefetch()
                    if seg == "gate":
                        bi = cnt["stgb"] % 2
                        cnt["stgb"] += 1
                        bk = "astgb%d" % bi
                        for j in range(4):
                            pb, pk = nbank()
                            mm16(pb, pk, wb, wk, j * 128, 128)
                            copy_evac(j, pb, pk, 128, stgb[bi][:, j, :], bk, AF.Sigmoid)
                        P.dma("sp", self.gates[512 * gidx:512 * gidx + 512, ts0:ts0 + T].rearrange("(j p) t -> p j t", p=128), stgb[bi][:],
                              [bk], ["gates"], bk)
                    elif seg in ("rw_r", "rw_k", "rw_v", "ret_g", "mla_q"):
                        si = cnt["stg"] % 3
                        cnt["stg"] += 1
                        sk = "astg%d" % si
                        func = {"ret_g": AF.Silu}.get(seg, AF.Copy)
                        for j in range(4):
                            pb, pk = nbank()
                            mm16(pb, pk, wb, wk, j * 128, 128)
                            copy_evac(j, pb, pk, 128, stg[si][:, j, :], sk, func)
                        if seg.startswith("rw_"):
                            row0 = {"rw_r": 0, "rw_k": 1024, "rw_v": 2048}[seg] + 512 * gidx
                            dst = self.zrw[row0:row0 + 512, 1 + ts0:1 + ts0 + T]
                            dk = "zrw"
                        elif seg == "ret_g":
                            dst = self.retg[512 * gidx:512 * gidx + 512, ts0:ts0 + T]
                            dk = "retg"
                        elif seg == "mla_q":
                            dst = self.qd[0:512, ts0:ts0 + T]
                            dk = "qd"
                        else:
                            dst = self.gates[512 * gidx:512 * gidx + 512, ts0:ts0 + T]
                            dk = "gates"
                        P.dma("sp", dst.rearrange("(j p) t -> p j t", p=128), stg[si][:], [sk], [dk], sk)
                    elif seg == "rw_lora":
                        si = cnt["stg"] % 3
                        cnt["stg"] += 1
                        sk = "astg%d" % si
                        for j, (o, wd, row0) in enumerate(((0, 96, 3072), (96, 96, 3200), (192, 128, 3328), (320, 128, 3456))):
                            pb, pk = nbank()
                            mm16(pb, pk, wb, wk, o, wd)
                            copy_evac(j, pb, pk, wd, stg[si][:wd, j, :], sk)
                            P.dma("sp", self.zrw[row0:row0 + wd, 1 + ts0:1 + ts0 + T], stg[si][:wd, j, :], [sk], ["zrw"], sk)
                    elif seg in ("ret_q", "ret_k", "mla_kv"):
                        bi = cnt["stgb"] % 2
                        cnt["stgb"] += 1
                        bk = "astgb%d" % bi
                        if seg == "mla_kv":
                            si = cnt["stg"] % 3
                            cnt["stg"] += 1
                            sk = "astg%d" % si
                            for j in range(2):
                                pb, pk = nbank()
                                mm16(pb, pk, wb, wk, j * 128, 128)
                                copy_evac(j, pb, pk, 128, stg[si][:, j, :], sk)
                            P.dma("sp", self.kvd[0:256, ts0:ts0 + T].rearrange("(j p) t -> p j t", p=128), stg[si][:, 0:2, :],
                                  [sk], ["kvd"], sk)
                            jobs = [(None, self.cs["swap64"], 2, 3, "c_swap64")]
                        else:
                            jobs = [(j, self.cs["swap128"], 0, 1, "c_swap128") for j in range(4)]
                        for (j, swp, ctab, stab, swk) in jobs:
                            rr = cnt["rot"] % 2
                            cnt["rot"] += 1
                            pb, pk = nbank()
                            if j is None:
                                mm16(pb, pk, wb, wk, 256, 64, out_rows=(0, 64))
                                mm16(pb, pk, wb, wk, 256, 64, out_rows=(64, 128))
                                jj = 0
                            else:
                                mm16(pb, pk, wb, wk, j * 128, 128)
                                jj = j
                            P.op("act", lambda e, pb=pb, rr=rr: e.activation(out=zf[rr][:], in_=pb[:, :T], func=AF.Copy),
                                 [pk], ["azf%d" % rr])
                            pb2, pk2 = nbank()
                            P.op("pe", lambda e, pb2=pb2, rr=rr, swp=swp: e.matmul(pb2[:, :T], swp[:], zf[rr][:], start=True,
                                                                                 stop=True), ["azf%d" % rr, swk], [pk2])
                            P.op("pool", lambda e, rr=rr, ctab=ctab: e.tensor_tensor(out=t1[rr][:], in0=zf[rr][:],
                                                                                   in1=tb[:, ctab, :], op=ALU.mult),
                                 ["azf%d" % rr, tk], ["at1%d" % rr])
                            P.op("dve", lambda e, pb2=pb2, rr=rr, stab=stab: e.tensor_tensor(out=t2[rr][:], in0=pb2[:, :T],
                                                                                           in1=tb[:, stab, :], op=ALU.mult),
                                 [pk2, tk], ["at2%d" % rr])
                            P.op("pool", lambda e, rr=rr, bi=bi, jj=jj: e.tensor_tensor(out=stgb[bi][:, jj, :], in0=t1[rr][:],
                                                                                      in1=t2[rr][:], op=ALU.add),
                                 ["at1%d" % rr, "at2%d" % rr], [bk])
                        if seg == "mla_kv":
                            P.dma("sp", self.kr[:, ts0:ts0 + T], stgb[bi][:, 0, :], [bk], ["kr"], bk)
                        else:
                            dt_ = self.retq if seg == "ret_q" else self.retk
                            P.dma("sp", dt_[512 * gidx:512 * gidx + 512, ts0:ts0 + T].rearrange("(j p) t -> p j t", p=128),
                                  stgb[bi][:], [bk], [seg], bk)
                    elif seg == "ret_v":
                        vi = cnt["stv"] % 2
                        cnt["stv"] += 1
                        vk = "astv%d" % vi
                        for tsub in range(3):
                            pb, pk = nbank()
                            for kc in range(16):
                                P.op("pe", lambda e, kc=kc, pb=pb, tsub=tsub: e.matmul(pb[:, :512], u[:, kc, tsub * 128:(tsub + 1) * 128],
                                                                                     wb[:, kc, 0:512], start=(kc == 0), stop=(kc == 15)),
                                     [wk, "au"], [pk])
                            if tsub % 2:
                                P.op("act", lambda e, pb=pb, tsub=tsub: e.activation(out=stv[vi][:, tsub, :], in_=pb[:, :512], func=AF.Copy),
                                     [pk], [vk])
                            else:
                                P.op("dve", lambda e, pb=pb, tsub=tsub: e.tensor_copy(out=stv[vi][:, tsub, :], in_=pb[:, :512]), [pk], [vk])
                        P.dma("sp", self.retv[ts0:ts0 + T, 512 * gidx:512 * gidx + 512].rearrange("(a p) c -> p a c", p=128),
                              stv[vi][:], [vk], ["retv"], vk)
                    else:
                        raise ValueError(seg)

    def phase_merge(self, l):
        P = self.P
        w = self.wt[l]
        with P.scope():
            hs = P.sb("ehs", [128, 16, T], F32)
            ys = [P.sb("eys", [128, 8, T], BF16) for _ in range(3)]
            mg = P.sb("emg", [128, 16, T], BF16)
            gt = [P.sb("egt", [128, 3, 2, T], BF16) for _ in range(2)]
            ta = [P.sb("eta", [128, T], F32) for _ in range(2)]
            tb = [P.sb("etb", [128, T], F32) for _ in range(2)]
            wsa = WStream(self, "ewa", [w["bra"][g] for _s in range(self.NT) for g in range(8)], 8, 256, 2)
            wsb = WStream(self, "ewb", [w["brb"][g] for _s in range(self.NT) for g in range(8)], 8, 256, 2)
            wsc = WStream(self, "ewc", [w["brc"][g] for _s in range(self.NT) for g in range(8)], 8, 256, 2)
            wso = WStream(self, "ewo", [w["out"][g] for _s in range(self.NT) for g in range(8)], 16, 256, 2)
            hTv = self.hT.rearrange("(kc p) t -> p kc t", p=128)
            gv = self.gates.rearrange("(b m p) t -> p b m t", b=3, p=128)
            nb = [0]

            def nbank():
                b = nb[0] % 8
                nb[0] += 1
                return self.bank(b)

            for s in range(self.NT):
                ts0 = s * T
                P.dma("sp", hs[:], hTv[:, :, ts0:ts0 + T], ["hT"], ["ehs"], "ehs")
                for bi, (src, nm) in enumerate(((self.ya, "ya"), (self.yb, "yb"), (self.yc, "yc"))):
                    P.dma("sp", ys[bi][:], src[:, ts0:ts0 + T].rearrange("(c p) t -> p c t", p=128), [nm], ["eys%d" % bi],
                          "eys%d" % bi)
                for g in range(8):
                    i = s * 8 + g
                    bufs = []
                    for ws in (wsa, wsb, wsc):
                        bufs.append(ws.get(i))
                        ws.prefetch()
                    gr = g % 2
                    gk = "egt%d" % gr
                    for b3 in range(3):
                        P.dma("sp", gt[gr][:, b3], gv[:, b3, 2 * g:2 * g + 2, ts0:ts0 + T], ["gates"], [gk], gk)
                    for j in range(2):
                        m = 2 * g + j
                        pbs = []
                        for bi in range(3):
                            pb, pk = nbank()
                            wb, wk = bufs[bi]
                            for kc in range(8):
                                P.op("pe", lambda e, pb=pb, wb=wb, kc=kc, bi=bi: e.matmul(pb[:, :T], wb[:, kc, j * 128:(j + 1) * 128],
                                                                                        ys[bi][:, kc, :], start=(kc == 0), stop=(kc == 7)),
                                     [wk, "eys%d" % bi], [pk])
                            pbs.append((pb, pk))
                        r = m % 2
                        P.op("dve", lambda e, r=r: e.tensor_tensor(out=ta[r][:], in0=pbs[0][0][:, :T], in1=gt[gr][:, 0, j, :], op=ALU.mult),
                             [pbs[0][1], gk], ["eta%d" % r])
                        P.op("dve", lambda e, r=r: e.tensor_tensor(out=tb[r][:], in0=pbs[1][0][:, :T], in1=gt[gr][:, 1, j, :], op=ALU.mult),
                             [pbs[1][1], gk], ["etb%d" % r])
                        P.op("pool", lambda e, r=r: e.tensor_tensor(out=ta[r][:], in0=ta[r][:], in1=tb[r][:], op=ALU.add),
                             ["eta%d" % r, "etb%d" % r], ["eta%d" % r])
                        P.op("dve", lambda e, r=r: e.tensor_tensor(out=tb[r][:], in0=pbs[2][0][:, :T], in1=gt[gr][:, 2, j, :], op=ALU.mult),
                             [pbs[2][1], gk, "eta%d" % r], ["etb%d" % r])
                        P.op("pool", lambda e, r=r, m=m: e.tensor_tensor(out=mg[:, m, :], in0=ta[r][:], in1=tb[r][:], op=ALU.add),
                             ["eta%d" % r, "etb%d" % r], ["emg"])
                for g in range(8):
                    wb, wk = wso.get(s * 8 + g)
                    wso.prefetch()
                    for j in range(2):
                        m = 2 * g + j
                        pb, pk = nbank()
                        for kc in range(16):
                            P.op("pe", lambda e, pb=pb, kc=kc: e.matmul(pb[:, :T], wb[:, kc, j * 128:(j + 1) * 128], mg[:, kc, :],
                                                                      start=(kc == 0), stop=(kc == 15)), [wk, "emg"], [pk])
                        P.op("dve", lambda e, pb=pb, m=m: e.tensor_tensor(out=hs[:, m, :], in0=pb[:, :T], in1=hs[:, m, :], op=ALU.add),
                             [pk, "ehs"], ["ehs"])
                P.dma("sp", hTv[:, :, ts0:ts0 + T], hs[:], ["ehs"], ["hT"], "ehs")

    def phase_ffn(self, l):
        P = self.P
        w = self.wt[l]
        NG = FFN_H // 256
        with P.scope():
            hs = P.sb("fhs", [128, 16, T], F32)
            sq = P.sb("fsq", [128, 16, T], BF16)
            rstd = P.sb("frstd", [128, T], F32)
            u = P.sb("fu", [128, 16, T], BF16)
            act = P.sb("fact", [128, 44, T], BF16)
            sg = [P.sb("fsg", [128, T], F32) for _ in range(2)]
            wsg = WStream(self, "fwg", [w["gate"][g] for _s in range(self.NT) for g in range(NG)], 16, 256, 2)
            wsu = WStream(self, "fwu", [w["up"][g] for _s in range(self.NT) for g in range(NG)], 16, 256, 2)
            wsd = WStream(self, "fwd", [w["dn"][g] for _s in range(self.NT) for g in range(8)], 44, 256, 2)
            hTv = self.hT.rearrange("(kc p) t -> p kc t", p=128)
            gains = self.vec(l, "norm_ffn", 0, 16)
            nb = [0]

            def nbank():
                b = nb[0] % 7
                nb[0] += 1
                return self.bank(b)

            for s in range(self.NT):
                ts0 = s * T
                P.dma("sp", hs[:], hTv[:, :, ts0:ts0 + T], ["hT"], ["fhs"], "fhs")
                self.fm_norm(hs, "fhs", 16, T, gains, u, "fu", D, sq, "fsq", rstd, "frstd", 7)
                for g in range(NG):
                    wg, wgk = wsg.get(s * NG + g)
                    wsg.prefetch()
                    wu, wuk = wsu.get(s * NG + g)
                    wsu.prefetch()
                    for j in range(2):
                        f = 2 * g + j
                        pg, pgk = nbank()
                        pu, puk = nbank()
                        for kc in range(16):
                            P.op("pe", lambda e, kc=kc, pg=pg: e.matmul(pg[:, :T], wg[:, kc, j * 128:(j + 1) * 128], u[:, kc, :],
                                                                      start=(kc == 0), stop=(kc == 15)), [wgk, "fu"], [pgk])
                        for kc in range(16):
                            P.op("pe", lambda e, kc=kc, pu=pu: e.matmul(pu[:, :T], wu[:, kc, j * 128:(j + 1) * 128], u[:, kc, :],
                                                                      start=(kc == 0), stop=(kc == 15)), [wuk, "fu"], [puk])
                        r = f % 2
                        P.op("act", lambda e, r=r, pg=pg: e.activation(out=sg[r][:], in_=pg[:, :T], func=AF.Silu), [pgk], ["fsg%d" % r])
                        P.op("dve", lambda e, r=r, pu=pu, f=f: e.tensor_tensor(out=act[:, f, :], in0=pu[:, :T], in1=sg[r][:], op=ALU.mult),
                             [puk, "fsg%d" % r], ["fact"])
                for g in range(8):
                    wd_, wdk = wsd.get(s * 8 + g)
                    wsd.prefetch()
                    for j in range(2):
                        m = 2 * g + j
                        pb, pk = nbank()
                        for f in range(44):
                            P.op("pe", lambda e, pb=pb, f=f: e.matmul(pb[:, :T], wd_[:, f, j * 128:(j + 1) * 128], act[:, f, :],
                                                                    start=(f == 0), stop=(f == 43)), [wdk, "fact"], [pk])
                        P.op("dve", lambda e, pb=pb, m=m: e.tensor_tensor(out=hs[:, m, :], in0=pb[:, :T], in1=hs[:, m, :], op=ALU.add),
                             [pk, "fhs"], ["fhs"])
                P.dma("sp", hTv[:, :, ts0:ts0 + T], hs[:], ["fhs"], ["hT"], "fhs")

    def phase_mla_prep(self, l):
        P = self.P
        w = self.wt[l]
        with P.scope():
            qd = P.sb("dqd", [128, 4, T], F32)
            kvd = P.sb("dkvd", [128, 2, T], F32)
            sq = P.sb("dsq", [128, 4, T], BF16)
            rstd = P.sb("drstd", [128, T], F32)
            cq = P.sb("dcq", [128, 4, T], BF16)
            ckv = P.sb("dckv", [128, 2, T], BF16)
            wq = [P.sb("dwq", [128, 4, 384], BF16) for _ in range(4)]
            wkv = [P.sb("dwkv", [128, 2, 512], BF16) for _ in range(4)]
            stb = [P.sb("dstb", [128, T], BF16) for _ in range(4)]
            stv = [P.sb("dstv", [128, 3, 256], BF16) for _ in range(2)]
            zf = [P.sb("dzf", [128, T], F32) for _ in range(2)]
            t1 = [P.sb("dt1", [128, T], F32) for _ in range(2)]
            t2 = [P.sb("dt2", [128, T], F32) for _ in range(2)]
            tabs = [P.sb("dtab", [128, 2, T], F32) for _ in range(2)]
            for i in range(4):
                P.dma("sp", wq[i][:], w["uq"][i][0], [w["uq"][i][1]], ["dwq%d" % i], "dwq%d" % i)
                P.dma("sp", wkv[i][:], w["ukv"][i][0], [w["ukv"][i][1]], ["dwkv%d" % i], "dwkv%d" % i)
            cnt = {"b": 0, "stb": 0, "stv": 0, "rot": 0}

            def nbank():
                b = cnt["b"] % 7
                cnt["b"] += 1
                return self.bank(b)

            def nstb():
                i = cnt["stb"] % 4
                cnt["stb"] += 1
                return stb[i], "dstb%d" % i

            for s in range(self.NT):
                ts0 = s * T
                P.dma("sp", qd[:], self.qd[:, ts0:ts0 + T].rearrange("(c p) t -> p c t", p=128), ["qd"], ["dqd"], "dqd")
                P.dma("sp", kvd[:], self.kvd[:, ts0:ts0 + T].rearrange("(c p) t -> p c t", p=128), ["kvd"], ["dkvd"], "dkvd")
                tb = tabs[s % 2]
                tk = "dtab%d" % (s % 2)
                P.dma("sp", tb[:], self.cst["rope_tab"][:, 2:4, ts0:ts0 + T], [], [tk], tk)
                self.fm_norm(qd, "dqd", 4, T, self.vec(l, "nq", 0, 4), cq, "dcq", 512, sq, "dsq", rstd, "drstd", 7)
                self.fm_norm(kvd, "dkvd", 2, T, self.vec(l, "nkv", 0, 2), ckv, "dckv", 256, sq, "dsq", rstd, "drstd", 7)
                for gq in range(4):
                    wk = "dwq%d" % gq
                    for j in range(2):
                        h = 2 * gq + j
                        pb, pk = nbank()
                        for kc in range(4):
                            P.op("pe", lambda e, kc=kc: e.matmul(pb[:, :T], wq[gq][:, kc, j * 192:j * 192 + 128], cq[:, kc, :],
                                                                 start=(kc == 0), stop=(kc == 3)), [wk, "dcq"], [pk])
                        sb_, sk = nstb()
                        P.op("act", lambda e: e.activation(out=sb_[:], in_=pb[:, :T], func=AF.Copy), [pk], [sk])
                        P.dma("sp", self.qn[h * 128:(h + 1) * 128, ts0:ts0 + T], sb_[:], [sk], ["qn"], sk)
                    pb, pk = nbank()
                    for j in range(2):
                        for kc in range(4):
                            P.op("pe", lambda e, kc=kc: e.matmul(pb[j * 64:(j + 1) * 64, :T], wq[gq][:, kc, j * 192 + 128:j * 192 + 192],
                                                                 cq[:, kc, :], start=(kc == 0), stop=(kc == 3)), [wk, "dcq"], [pk])
                    rr = cnt["rot"] % 2
                    cnt["rot"] += 1
                    P.op("act", lambda e: e.activation(out=zf[rr][:], in_=pb[:, :T], func=AF.Copy), [pk], ["dzf%d" % rr])
                    pb2, pk2 = nbank()
                    P.op("pe", lambda e: e.matmul(pb2[:, :T], self.cs["swap64"][:], zf[rr][:], start=True, stop=True),
                         ["dzf%d" % rr, "c_swap64"], [pk2])
                    P.op("pool", lambda e: e.tensor_tensor(out=t1[rr][:], in0=zf[rr][:], in1=tb[:, 0, :], op=ALU.mult),
                         ["dzf%d" % rr, tk], ["dt1%d" % rr])
                    P.op("dve", lambda e: e.tensor_tensor(out=t2[rr][:], in0=pb2[:, :T], in1=tb[:, 1, :], op=ALU.mult),
                         [pk2, tk], ["dt2%d" % rr])
                    sb_, sk = nstb()
                    P.op("pool", lambda e: e.tensor_tensor(out=sb_[:], in0=t1[rr][:], in1=t2[rr][:], op=ALU.add),
                         ["dt1%d" % rr, "dt2%d" % rr], [sk])
                    P.dma("sp", self.qr[gq * 128:(gq + 1) * 128, ts0:ts0 + T], sb_[:], [sk], ["qr"], sk)
                for gk in range(4):
                    wk = "dwkv%d" % gk
                    for j in range(2):
                        h = 2 * gk + j
                        pb, pk = nbank()
                        for kc in range(2):
                            P.op("pe", lambda e, kc=kc: e.matmul(pb[:, :T], wkv[gk][:, kc, j * 256:j * 256 + 128], ckv[:, kc, :],
                                                                 start=(kc == 0), stop=(kc == 1)), [wk, "dckv"], [pk])
                        sb_, sk = nstb()
                        P.op("act", lambda e: e.activation(out=sb_[:], in_=pb[:, :T], func=AF.Copy), [pk], [sk])
                        P.dma("sp", self.kn[h * 128:(h + 1) * 128, ts0:ts0 + T], sb_[:], [sk], ["kn"], sk)
                    vi = cnt["stv"] % 2
                    cnt["stv"] += 1
                    vk = "dstv%d" % vi
                    for tsub in range(3):
                        pb, pk = nbank()
                        for kc in range(2):
                            rhs = wkv[gk][:, kc, :].rearrange("p (j c) -> p j c", j=2)[:, :, 128:256]
                            P.op("pe", lambda e, kc=kc, rhs=rhs: e.matmul(pb[:, :256].rearrange("p (j c) -> p j c", j=2),
                                                                        ckv[:, kc, tsub * 128:(tsub + 1) * 128], rhs,
                                                                        start=(kc == 0), stop=(kc == 1)), [wk, "dckv"], [pk])
                        P.op("dve", lambda e: e.tensor_copy(out=stv[vi][:, tsub, :], in_=pb[:, :256]), [pk], [vk])
                    P.dma("sp", self.vm[ts0:ts0 + T, gk * 256:(gk + 1) * 256].rearrange("(a p) c -> p a c", p=128), stv[vi][:],
                          [vk], ["vm"], vk)

    def phase_mla_attn(self, l):
        P = self.P
        LP, N128 = self.LP, self.N128
        scale = 192.0 ** -0.5
        with P.scope():
            kn = [P.sb("mkn", [128, LP], BF16) for _ in range(2)]
            qn = [P.sb("mqn", [128, LP], BF16) for _ in range(2)]
            vm = [P.sb("mvm", [128, N128, 128], BF16) for _ in range(2)]
            qr = [P.sb("mqr", [128, LP], BF16) for _ in range(2)]
            kr = P.sb("mkr", [128, LP], BF16)
            mask = P.sb("mmask", [128, 3, T], BF16)
            maskf = P.sb("mmaskf", [128, 3, T], F32)
            pT = [P.sb("mpT", [128, T], BF16) for _ in range(5)]
            rl = [P.sb("mrl", [128, T], F32) for _ in range(2)]
            ob = [P.sb("mob", [128, T], BF16) for _ in range(2)]
            P.dma("sp", kr[:], self.kr, ["kr"], ["mkr"], "mkr")
            P.dma("sp", maskf[:], self.cst["mla_mask"], [], ["mmaskf"], "mmaskf")
            P.op("act", lambda e: e.activation(out=mask[:], in_=maskf[:], func=AF.Copy), ["mmaskf"], ["mmask"])
            npt = [0]
            ns = [0]
            for h in range(8):
                r = h % 2
                b0 = r * 64
                P.dma("sp", kn[r][:], self.kn[h * 128:(h + 1) * 128, :], ["kn"], ["mkn%d" % r], "mkn%d" % r)
                P.dma("sp", qn[r][:], self.qn[h * 128:(h + 1) * 128, :], ["qn"], ["mqn%d" % r], "mqn%d" % r)
                P.dma("sp", vm[r][:], self.vm.rearrange("(n p) c -> p n c", p=128)[:, :, h * 128:(h + 1) * 128], ["vm"],
                      ["mvm%d" % r], "mvm%d" % r)
                pr = (h // 2) % 2
                if h % 2 == 0:
                    P.dma("sp", qr[pr][:], self.qr[(h // 2) * 128:(h // 2 + 1) * 128, :], ["qr"], ["mqr%d" % pr], "mqr%d" % pr)
                for g in range(self.NT):
                    q0 = g * T
                    pbO, pkO = self.bank(4 + g % 2)
                    pbL, pkL = self.bank(6 + g % 2)
                    nk = 3 * g + 3

                    def qk(kt):
                        pbS, pkS = self.bank(ns[0] % 4)
                        ns[0] += 1
                        P.op("pe", lambda e: e.matmul(pbS[:, :T], kn[r][:, kt * 128:(kt + 1) * 128], qn[r][:, q0:q0 + T], start=True,
                                                      stop=False), ["mkn%d" % r, "mqn%d" % r], [pkS])
                        P.op("pe", lambda e: e.matmul(pbS[:, :T], kr[b0:b0 + 64, kt * 128:(kt + 1) * 128], qr[pr][b0:b0 + 64, q0:q0 + T],
                                                      start=False, stop=True), ["mkr", "mqr%d" % pr], [pkS])
                        pi = npt[0] % 5
                        npt[0] += 1
                        pk_ = "mpT%d" % pi
                        P.op("act", lambda e: e.activation(out=pT[pi][:], in_=pbS[:, :T], func=AF.Exp, scale=scale), [pkS], [pk_])
                        if kt >= 3 * g:
                            m = kt - 3 * g
                            P.op("dve", lambda e: e.tensor_tensor(out=pT[pi][:], in0=pT[pi][:], in1=mask[:, m, :], op=ALU.mult),
                                 [pk_, "mmask"], [pk_])
                        return pi, pk_

                    def pv(kt, pi, pk_):
                        P.op("pe", lambda e: e.matmul(pbO[:, :T], vm[r][:, kt, :], pT[pi][:], start=(kt == 0), stop=(kt == nk - 1)),
                             ["mvm%d" % r, pk_], [pkO])
                        P.op("pe", lambda e: e.matmul(pbL[:, :T], self.ones_bf[:], pT[pi][:], start=(kt == 0), stop=(kt == nk - 1)),
                             ["ones_bf", pk_], [pkL])

                    pend = []
                    for kt in range(nk):
                        pend.append((kt,) + qk(kt))
                        if len(pend) > 3:
                            pv(*pend.pop(0))
                    while pend:
                        pv(*pend.pop(0))
                    gi = g % 2
                    P.op("dve", lambda e: e.reciprocal(rl[gi][:], pbL[:, :T]), [pkL], ["mrl%d" % gi])
                    P.op("dve", lambda e: e.tensor_tensor(out=ob[gi][:], in0=pbO[:, :T], in1=rl[gi][:], op=ALU.mult),
                         [pkO, "mrl%d" % gi], ["mob%d" % gi])
                    P.dma("sp", self.yc[h * 128:(h + 1) * 128, q0:q0 + T], ob[gi][:], ["mob%d" % gi], ["yc"], "mob%d" % gi)

    def phase_ret(self, l):
        P = self.P
        LP, N128 = self.LP, self.N128
        lg = make_consts(128)["ret_lg"] if False else np.log1p(-np.exp2(-5.0 - np.arange(8, dtype=np.float64)))
        sc = 128.0 ** -0.5
        with P.scope():
            kT = [P.sb("rkT", [128, LP], BF16) for _ in range(2)]
            qT = [P.sb("rqT", [128, LP], BF16) for _ in range(2)]
            vv = [P.sb("rvv", [128, N128, 128], BF16) for _ in range(2)]
            tab = [P.sb("rtab", [128, 512], F32) for _ in range(2)]
            maskf = P.sb("rmaskf", [128, 3, T], F32)
            gs = [P.sb("rgs", [128, T], F32) for _ in range(2)]
            pT = [P.sb("rpT", [128, T], BF16) for _ in range(6)]
            sq = [P.sb("rsq", [128, T], BF16) for _ in range(2)]
            rstd = [P.sb("rrstd", [128, T], F32) for _ in range(2)]
            tt = [P.sb("rtt", [128, T], F32) for _ in range(2)]
            ob = [P.sb("rob", [128, T], BF16) for _ in range(2)]
            npt = [0]
            ns = [0]
            P.dma("sp", maskf[:], self.cst["mla_mask"], [], ["rmaskf"], "rmaskf")
            for h in range(8):
                r = h % 2
                P.dma("sp", kT[r][:], self.retk[h * 128:(h + 1) * 128, :], ["ret_k"], ["rkT%d" % r], "rkT%d" % r)
                P.dma("sp", qT[r][:], self.retq[h * 128:(h + 1) * 128, :], ["ret_q"], ["rqT%d" % r], "rqT%d" % r)
                P.dma("sp", vv[r][:], self.retv.rearrange("(n p) c -> p n c", p=128)[:, :, h * 128:(h + 1) * 128], ["retv"],
                      ["rvv%d" % r], "rvv%d" % r)
                P.dma("sp", tab[r][:], self.cst["ret_tab"][h], [], ["rtab%d" % r], "rtab%d" % r)
                P.op("pool", lambda e: e.tensor_tensor(out=qT[r][:].rearrange("p (g t) -> p g t", t=T), in0=qT[r][:].rearrange("p (g t) -> p g t", t=T),
                                                       in1=tab[r][:, 0:T].unsqueeze(1).to_broadcast([128, self.NT, T]), op=ALU.mult),
                     ["rqT%d" % r, "rtab%d" % r], ["rqT%d" % r])
                P.op("dve", lambda e: e.tensor_tensor(out=kT[r][:].rearrange("p (n t) -> p n t", t=128), in0=kT[r][:].rearrange("p (n t) -> p n t", t=128),
                                                      in1=tab[r][:, T:T + 128].unsqueeze(1).to_broadcast([128, N128, 128]), op=ALU.mult),
                     ["rkT%d" % r, "rtab%d" % r], ["rkT%d" % r])
                for g in range(self.NT):
                    q0 = g * T
                    gi = g % 2
                    P.dma("sp", gs[gi][:], self.retg[h * 128:(h + 1) * 128, q0:q0 + T], ["retg"], ["rgs%d" % gi], "rgs%d" % gi)
                    pbO, pkO = self.bank(5 + gi)
                    pbL, pkL = self.bank(7)
                    nk = 3 * g + 3

                    def qk(kt):
                        pbS, pkS = self.bank(ns[0] % 5)
                        ns[0] += 1
                        P.op("pe", lambda e: e.matmul(pbS[:, :T], kT[r][:, kt * 128:(kt + 1) * 128], qT[r][:, q0:q0 + T], start=True,
                                                      stop=True), ["rkT%d" % r, "rqT%d" % r], [pkS])
                        pi = npt[0] % 6
                        npt[0] += 1
                        pk_ = "rpT%d" % pi
                        if kt >= 3 * g:
                            m = kt - 3 * g
                            c = float(np.float32(sc * math.exp(-lg[h] * 128 * m)))
                            P.op("dve", lambda e: e.scalar_tensor_tensor(out=pT[pi][:], in0=pbS[:, :T], scalar=c, in1=maskf[:, m, :],
                                                                         op0=ALU.mult, op1=ALU.mult), [pkS, "rmaskf"], [pk_])
                        else:
                            c = float(np.float32(sc * math.exp(lg[h] * (q0 - kt * 128))))
                            if kt % 2 == 0:
                                P.op("act", lambda e: e.activation(out=pT[pi][:], in_=pbS[:, :T], func=AF.Copy, scale=c), [pkS], [pk_])
                            else:
                                P.op("dve", lambda e: e.tensor_scalar(out=pT[pi][:], in0=pbS[:, :T], scalar1=c, scalar2=None, op0=ALU.mult),
                                     [pkS], [pk_])
                        return pi, pk_

                    def pv(kt, pi, pk_):
                        P.op("pe", lambda e: e.matmul(pbO[:, :T], vv[r][:, kt, :], pT[pi][:], start=(kt == 0), stop=(kt == nk - 1)),
                             ["rvv%d" % r, pk_], [pkO])

                    pend = []
                    for kt in range(nk):
                        pend.append((kt,) + qk(kt))
                        if len(pend) > 4:
                            pv(*pend.pop(0))
                    while pend:
                        pv(*pend.pop(0))
                    P.op("act", lambda e: e.activation(out=sq[gi][:], in_=pbO[:, :T], func=AF.Square), [pkO], ["rsq%d" % gi])
                    P.op("pe", lambda e: e.matmul(pbL[:, :T], self.ones_bf[:], sq[gi][:], start=True, stop=True), ["ones_bf", "rsq%d" % gi],
                         [pkL])
                    P.op("act", lambda e: e.activation(out=rstd[gi][:], in_=pbL[:, :T], func=AF.Sqrt, scale=1.0 / 128,
                                                       bias=self.eps_sb[:, 0:1]), [pkL, "eps"], ["rrstd%d" % gi])
                    P.op("dve", lambda e: e.reciprocal(rstd[gi][:], rstd[gi][:]), ["rrstd%d" % gi], ["rrstd%d" % gi])
                    P.op("dve", lambda e: e.tensor_tensor(out=tt[gi][:], in0=pbO[:, :T], in1=rstd[gi][:], op=ALU.mult),
                         [pkO, "rrstd%d" % gi], ["rtt%d" % gi])
                    P.op("pool", lambda e: e.tensor_tensor(out=ob[gi][:], in0=tt[gi][:], in1=gs[gi][:], op=ALU.mult),
                         ["rtt%d" % gi, "rgs%d" % gi], ["rob%d" % gi])
                    P.dma("sp", self.yb[h * 128:(h + 1) * 128, q0:q0 + T], ob[gi][:], ["rob%d" % gi], ["yb"], "rob%d" % gi)

    def phase_rwkv(self, l):
        P = self.P
        w = self.wt[l]
        ident = self.cs["ident"]
        with P.scope():
            f = lambda n, s, dt=F32: P.sb(n, s, dt)
            wup = f("kwup", [96, 1024]); aup = f("kaup", [96, 1024]); gup = f("kgup", [128, 2, 1024], BF16)
            lnw = f("klnw", [128, 1024]); lnb = f("klnb", [128, 1024])
            rm = f("krm", [128, 5, 128]); rst = f("krst", [128, 128]); c1 = f("kc1", [128, 8])
            S = f("kS", [128, 8, 64])
            S2 = f("kS2", [128, 8, 2, 64])
            zt = f("kzt", [128, 28, 129]); zs = f("kzs", [128, 28, 128])
            tw = f("ktw", [96, 128]); sgd = f("ksgd", [128, 2, 128], BF16)
            sgw = f("ksgw", [128, 8, 128]); aa = f("kaa", [128, 8, 128])
            rt = f("krt", [128, 8, 128]); kapt = f("kkapt", [128, 8, 128]); kt_ = f("kkt", [128, 8, 128]); bt = f("kbt", [128, 8, 128])
            PC = f("kPC", [128, 8, 2])
            khat = f("kkhat", [128, 1024]); nbhat = f("knbhat", [128, 1024]); Vtok = f("kVtok", [128, 1024]); gtok = f("kgtok", [128, 1024])
            s16 = f("ks16", [128, 16])
            Mk = f("kMk", [128, 16, 128]); Akr = f("kAkr", [128, 16, 128]); nAbr = f("knAbr", [128, 16, 128]); Xall = f("kX", [128, 16, 128])
            A = [[f("kA", [128, 4, 128]) for _ in range(2)] for _ in range(2)]; AT = [[f("kAT", [128, 4, 128]) for _ in range(2)] for _ in range(2)]; XT = [f("kXT", [128, 4, 128]) for _ in range(2)]
            tmp = [[f("ktmp", [128, 128]) for _ in range(10)] for _ in range(4)]
            RHS = f("kRHS", [128, 1024]); UT = f("kUT", [128, 1024])
            m16 = f("km16", [128, 16]); v16 = f("kv16", [128, 16])
            ztf = zt[:].rearrange("p b t -> p (b t)")
            yc = ztf[:, 0:1024]; ysq = ztf[:, 1024:2048]
            yo = f("kyo", [128, 1024], BF16); yst = f("kyst", [128, 8, 128], BF16)
            P.dma("sp", wup[:], self.w_up[l], [], ["kwup"], "kwup")
            P.dma("sp", aup[:], self.a_up[l], [], ["kaup"], "kaup")
            for i in range(2):
                P.dma("sp", gup[:, :, i * 512:(i + 1) * 512], w["gup"][i][0], [w["gup"][i][1]], ["kgup"], "kgup")
            P.dma("sp", lnw[:], self.lnw[l].partition_broadcast(128), [], ["klnw"], "klnw")
            P.dma("sp", lnb[:], self.lnb[l].partition_broadcast(128), [], ["klnb"], "klnb")
            P.dma("sp", rm[:], self.cst["rw_masks"], [], ["krm"], "krm")
            P.dma("sp", rst[:], self.cst["rw_reset"], [], ["krst"], "krst")
            P.op("dve", lambda e: e.tensor_scalar(out=c1[:], in0=self.vec(l, "k_a", 0, 8), scalar1=-1.0, scalar2=1.0, op0=ALU.mult,
                                                  op1=ALU.add), ["vecs"], ["kc1"])
            P.op("pool", lambda e: e.memset(S[:], 0.0), [], ["kS"])
            P.op("pool", lambda e: e.memset(S2[:], 0.0), [], ["kS2"])
            P.op("pool", lambda e: e.memset(RHS[:], 0.0), [], ["kRHS"])
            P.op("pool", lambda e: e.memset(UT[:], 0.0), [], ["kUT"])
            zv = self.zrw.rearrange("(b p) t -> p b t", p=128)
            yav = self.ya.rearrange("(c p) t -> p c t", p=128)
            nb = [0]

            nb_pool = [list(range(8))]

            def nbank():
                pool = nb_pool[0]
                b = pool[nb[0] % len(pool)]
                nb[0] += 1
                return self.bank(b)

            def bc(ap, shape):
                return ap.to_broadcast(shape)

            for i in range(self.N128):
                t0 = i * 128
                P.dma("sp", zt[:], zv[:, :, t0:t0 + 129], ["zrw"], ["kzt", "kyc", "kysq"], "kzt")
                P.op("dve", lambda e: e.tensor_tensor(out=zs[:], in0=zt[:, :, 0:128], in1=zt[:, :, 1:129], op=ALU.subtract), ["kzt"], ["kzs"])
                P.op("pool", lambda e: e.tensor_tensor(out=zs[:], in0=zs[:], in1=bc(self.vec(l, "mu", 0, 28).unsqueeze(2), [128, 28, 128]),
                                                       op=ALU.mult), ["kzs", "vecs"], ["kzs"])
                P.op("dve", lambda e: e.tensor_tensor(out=zs[:], in0=zs[:], in1=zt[:, :, 1:129], op=ALU.add), ["kzs", "kzt"], ["kzs"])
                P.op("act", lambda e: e.activation(out=tw[:], in_=zs[0:96, 24, :], func=AF.Tanh), ["kzs"], ["ktw"])
                P.op("act", lambda e: e.activation(out=sgd[:], in_=zs[:, 26:28, :], func=AF.Sigmoid), ["kzs"], ["ksgd"])
                if getattr(self, "rw_stop", 99) <= 1:
                    continue
                pw = [nbank(), nbank()]
                pa = [nbank(), nbank()]
                for cb in range(8):
                    pb, pk = pw[cb // 4]
                    P.op("pe", lambda e: e.matmul(pb[:, (cb % 4) * 128:(cb % 4 + 1) * 128], wup[0:96, cb * 128:(cb + 1) * 128], tw[0:96, :],
                                                  start=True, stop=True), ["kwup", "ktw"], [pk])
                for cb in range(8):
                    pb, pk = pa[cb // 4]
                    P.op("pe", lambda e: e.matmul(pb[:, (cb % 4) * 128:(cb % 4 + 1) * 128], aup[0:96, cb * 128:(cb + 1) * 128], zs[0:96, 25, :],
                                                  start=True, stop=True), ["kaup", "kzs"], [pk])
                for cb in range(8):
                    pb, pk = pw[cb // 4]
                    P.op("act", lambda e: e.activation(out=sgw[:, cb, :], in_=pb[:, (cb % 4) * 128:(cb % 4 + 1) * 128], func=AF.Sigmoid,
                                                       bias=self.vec(l, "w0", cb, 1)), [pk, "vecs"], ["ksgw"])
                    pb, pk = pa[cb // 4]
                    P.op("act", lambda e: e.activation(out=aa[:, cb, :], in_=pb[:, (cb % 4) * 128:(cb % 4 + 1) * 128], func=AF.Sigmoid,
                                                       bias=self.vec(l, "a0", cb, 1)), [pk, "vecs"], ["kaa"])
                pg = [nbank(), nbank()]
                for hf in range(2):
                    pb, pk = pg[hf]
                    for kc in range(2):
                        P.op("pe", lambda e: e.matmul(pb[:, :], sgd[:, kc, :], gup[:, kc, hf * 512:(hf + 1) * 512], start=(kc == 0), stop=(kc == 1)),
                             ["ksgd", "kgup"], [pk])
                    P.op("act", lambda e: e.activation(out=gtok[:, hf * 512:(hf + 1) * 512], in_=pb[:, :], func=AF.Copy), [pk], ["kgtok"])
                if getattr(self, "rw_stop", 99) <= 2:
                    continue
                pkh = [self.bank(0), self.bank(0)]
                pbh = [self.bank(1), self.bank(1)]
                pvv = [self.bank(2), self.bank(2)]
                pq, pqk = self.bank(3)
                ps_, psk = self.bank(4)
                def TT(cb, j):
                    return tmp[cb % 4][j], "ktmp%d_%d" % (cb % 4, j)

                def SL(cb):
                    return slice((cb % 4) * 128, (cb % 4 + 1) * 128)

                CUM, ER, EINV, EK, KK, SQ, KAP, K2, BB, RK = range(10)
                rp = lambda cb: zs[:, cb, :]
                kp = lambda cb: zs[:, 8 + cb, :]
                vp = lambda cb: zs[:, 16 + cb, :]
                t_ = lambda cb, j: TT(cb, j)[0]
                k_ = lambda cb, j: TT(cb, j)[1]
                steps = [
                    lambda cb: P.op("dve", lambda e: e.tensor_tensor_scan(out=t_(cb, CUM)[:], data0=rst[:], data1=sgw[:, cb, :], initial=0.0,
                                                                          op0=ALU.mult, op1=ALU.add), ["krst", "ksgw"], [k_(cb, CUM)]),
                    lambda cb: P.op("pool", lambda e: e.tensor_scalar(out=t_(cb, KK)[:], in0=kp(cb), scalar1=self.vec(l, "k_k", cb, 1), scalar2=None,
                                                                      op0=ALU.mult), ["kzs", "vecs"], [k_(cb, KK)]),
                    lambda cb: P.op("act", lambda e: e.activation(out=t_(cb, ER)[:], in_=t_(cb, CUM)[:], func=AF.Exp, scale=-C0), [k_(cb, CUM)], [k_(cb, ER)]),
                    lambda cb: P.op("act", lambda e: e.activation(out=t_(cb, EINV)[:], in_=t_(cb, CUM)[:], func=AF.Exp, scale=C0), [k_(cb, CUM)], [k_(cb, EINV)]),
                    lambda cb: P.op("dve", lambda e: e.tensor_tensor(out=t_(cb, EK)[:], in0=t_(cb, CUM)[:], in1=sgw[:, cb, :], op=ALU.subtract),
                                    [k_(cb, CUM), "ksgw"], [k_(cb, EK)]),
                    lambda cb: P.op("act", lambda e: e.activation(out=t_(cb, SQ)[:], in_=t_(cb, KK)[:], func=AF.Square), [k_(cb, KK)], [k_(cb, SQ)]),
                    lambda cb: P.op("pe", lambda e: e.matmul(pq[:, SL(cb)], self.cs["blkones"][:], t_(cb, SQ)[:], start=True, stop=True),
                                    [k_(cb, SQ), "c_blkones"], [pqk]),
                    lambda cb: P.op("act", lambda e: e.activation(out=t_(cb, EK)[:], in_=t_(cb, EK)[:], func=AF.Exp, scale=-C0), [k_(cb, EK)], [k_(cb, EK)]),
                    lambda cb: P.op("dve", lambda e: e.tensor_scalar(out=t_(cb, K2)[:], in0=aa[:, cb, :], scalar1=self.vec(l, "k_a", cb, 1),
                                                                     scalar2=c1[:, cb:cb + 1], op0=ALU.mult, op1=ALU.add), ["kaa", "vecs", "kc1"], [k_(cb, K2)]),
                    lambda cb: P.op("pool", lambda e: e.tensor_tensor(out=rt[:, cb, :], in0=rp(cb), in1=t_(cb, ER)[:], op=ALU.mult), ["kzs", k_(cb, ER)], ["krt"]),
                    lambda cb: P.op("act", lambda e: e.activation(out=t_(cb, SQ)[:], in_=pq[:, SL(cb)], func=AF.Sqrt), [pqk], [k_(cb, SQ)]),
                    lambda cb: P.op("dve", lambda e: e.tensor_tensor(out=t_(cb, K2)[:], in0=t_(cb, K2)[:], in1=kp(cb), op=ALU.mult), [k_(cb, K2), "kzs"], [k_(cb, K2)]),
                    lambda cb: P.op("dve", lambda e: e.tensor_scalar(out=t_(cb, SQ)[:], in0=t_(cb, SQ)[:], scalar1=1e-12, scalar2=None, op0=ALU.max),
                                    [k_(cb, SQ)], [k_(cb, SQ)]),
                    lambda cb: P.op("act", lambda e: e.activation(out=PC[:, cb, :], in_=t_(cb, ER)[:].rearrange("p (c t) -> p c t", c=2)[:, :, 63],
                                                                  func=AF.Copy), [k_(cb, ER)], ["kPC"]),
                    lambda cb: P.op("dve", lambda e: e.reciprocal(t_(cb, SQ)[:], t_(cb, SQ)[:]), [k_(cb, SQ)], [k_(cb, SQ)]),
                    lambda cb: P.op("pool", lambda e: e.tensor_tensor(out=kt_[:, cb, :], in0=t_(cb, K2)[:], in1=t_(cb, EINV)[:], op=ALU.mult),
                                    [k_(cb, K2), k_(cb, EINV)], ["kkt"]),
                    lambda cb: P.op("dve", lambda e: e.tensor_tensor(out=t_(cb, KAP)[:], in0=t_(cb, KK)[:], in1=t_(cb, SQ)[:], op=ALU.mult),
                                    [k_(cb, KK), k_(cb, SQ)], [k_(cb, KAP)]),
                    lambda cb: P.op("dve", lambda e: e.scalar_tensor_tensor(out=t_(cb, RK)[:], in0=rp(cb), scalar=self.vec(l, "r_k", cb, 1), in1=t_(cb, K2)[:],
                                                                            op0=ALU.mult, op1=ALU.mult), ["kzs", "vecs", k_(cb, K2)], [k_(cb, RK)]),
                    lambda cb: P.op("pool", lambda e: e.tensor_tensor(out=t_(cb, BB)[:], in0=aa[:, cb, :], in1=t_(cb, KAP)[:], op=ALU.mult),
                                    ["kaa", k_(cb, KAP)], [k_(cb, BB)]),
                    lambda cb: P.op("dve", lambda e: e.tensor_tensor(out=kapt[:, cb, :], in0=t_(cb, KAP)[:], in1=t_(cb, EK)[:], op=ALU.mult),
                                    [k_(cb, KAP), k_(cb, EK)], ["kkapt"]),
                    lambda cb: P.op("pe", lambda e: e.matmul(ps_[:, 2 * cb:2 * cb + 2], t_(cb, RK)[:], self.cs["blkcols"][:], start=True, stop=True),
                                    [k_(cb, RK), "c_blkcols"], [psk]),
                    lambda cb: P.op("dve", lambda e: e.tensor_tensor(out=bt[:, cb, :], in0=t_(cb, BB)[:], in1=t_(cb, EINV)[:], op=ALU.mult),
                                    [k_(cb, BB), k_(cb, EINV)], ["kbt"]),
                    lambda cb: P.op("pe", lambda e: e.transpose(pvv[cb // 4][0][:, SL(cb)], vp(cb), ident[:]), ["kzs", "c_ident"], [pvv[cb // 4][1]]),
                    lambda cb: P.op("act", lambda e: e.activation(out=t_(cb, ER)[:, 0:64], in_=kt_[:, cb, 0:64], func=AF.Identity, scale=PC[:, cb, 0:1]),
                                    ["kkt", "kPC", k_(cb, ER)], [k_(cb, ER)]),
                    lambda cb: P.op("act", lambda e: e.activation(out=t_(cb, ER)[:, 64:128], in_=kt_[:, cb, 64:128], func=AF.Identity, scale=PC[:, cb, 1:2]),
                                    ["kkt", "kPC", k_(cb, ER)], [k_(cb, ER)]),
                    lambda cb: P.op("pool", lambda e: e.tensor_scalar(out=t_(cb, EINV)[:, 0:64], in0=bt[:, cb, 0:64], scalar1=PC[:, cb, 0:1], scalar2=-1.0,
                                                                      op0=ALU.mult, op1=ALU.mult), ["kbt", "kPC", k_(cb, EINV)], [k_(cb, EINV)]),
                    lambda cb: P.op("pool", lambda e: e.tensor_scalar(out=t_(cb, EINV)[:, 64:128], in0=bt[:, cb, 64:128], scalar1=PC[:, cb, 1:2], scalar2=-1.0,
                                                                      op0=ALU.mult, op1=ALU.mult), ["kbt", "kPC", k_(cb, EINV)], [k_(cb, EINV)]),
                    lambda cb: P.op("pe", lambda e: e.transpose(pkh[cb // 4][0][:, SL(cb)], t_(cb, ER)[:], ident[:]), [k_(cb, ER), "c_ident"], [pkh[cb // 4][1]]),
                    lambda cb: P.op("pe", lambda e: e.transpose(pbh[cb // 4][0][:, SL(cb)], t_(cb, EINV)[:], ident[:]), [k_(cb, EINV), "c_ident"], [pbh[cb // 4][1]]),
                ]
                def prep_gen(batch):
                    for st in steps:
                        for cb in batch:
                            st(cb)
                        yield

                def evac_hf(hf):
                    cs_ = slice(hf * 512, hf * 512 + 512)
                    P.op("act", lambda e: e.activation(out=khat[:, cs_], in_=pkh[hf][0][:, :], func=AF.Copy), [pkh[hf][1]], ["kkhat"])
                    P.op("dve", lambda e: e.tensor_copy(out=nbhat[:, cs_], in_=pbh[hf][0][:, :]), [pbh[hf][1]], ["knbhat"])
                    P.op("act", lambda e: e.activation(out=Vtok[:, cs_], in_=pvv[hf][0][:, :], func=AF.Copy), [pvv[hf][1]], ["kVtok"])

                for _ in prep_gen(range(0, 4)):
                    pass
                evac_hf(0)
                prep_b1 = prep_gen(range(4, 8))
                if getattr(self, "rw_stop", 99) <= 3:
                    for _ in prep_b1:
                        pass
                if getattr(self, "rw_stop", 99) <= 3:
                    continue
                v3 = lambda pb: pb.rearrange("p (h t) -> p h t", h=4)
                specs = [(bt, kapt, "kbt", "kkapt"), (kapt, bt, "kkapt", "kbt"), (kt_, kapt, "kkt", "kkapt"), (kt_, rt, "kkt", "krt"),
                         (bt, rt, "kbt", "krt")]
                def dbl_gen(pair):
                    grp = []
                    for gi2 in range(2):
                        hp_, q_ = gi2, pair
                        hg = 2 * hp_ + q_
                        hs_ = slice(hp_ * 8 + 4 * q_, hp_ * 8 + 4 * q_ + 4)
                        X = Xall[:, hs_, :]
                        xk = "kX%d" % hg
                        Ag, ATg, XTg = A[gi2], AT[gi2], XT[gi2]
                        ak = lambda c, gi2=gi2: "kA%d_%d" % (gi2, c)
                        atk = lambda c, gi2=gi2: "kAT%d_%d" % (gi2, c)
                        xtk = "kXT%d" % gi2
                        dsts = [(Ag[0][:], ak(0)), (ATg[0][:], atk(0)), (Mk[:, hs_, :], "kMk"), (Akr[:, hs_, :], "kAkr"), (nAbr[:, hs_, :], "knAbr")]
                        for mi, (la, ra, lk, rk_) in enumerate(specs):
                            pb, pk = nbank()
                            for hh in range(4):
                                cb, b0 = 4 * q_ + hh, hp_ * 64
                                P.op("pe", lambda e: e.matmul(pb[:, hh * 128:(hh + 1) * 128], la[b0:b0 + 64, cb, :], ra[b0:b0 + 64, cb, :], start=True,
                                                              stop=True), [lk, rk_], [pk])
                            dst, dk = dsts[mi]
                            P.op("dve", lambda e: e.tensor_tensor(out=dst, in0=v3(pb), in1=bc(rm[:, mi, :].unsqueeze(1), [128, 4, 128]), op=ALU.mult),
                                 [pk, "krm"], [dk])
                        P.op("pool", lambda e: e.tensor_tensor(out=X, in0=Ag[0][:], in1=bc(ident[:].unsqueeze(1), [128, 4, 128]), op=ALU.add),
                             [ak(0), "c_ident"], [xk])
                        P.op("pool", lambda e: e.tensor_tensor(out=XTg[:], in0=ATg[0][:], in1=bc(ident[:].unsqueeze(1), [128, 4, 128]), op=ALU.add),
                             [atk(0), "c_ident"], [xtk])
                        grp.append((X, xk, Ag, ATg, XTg, ak, atk, xtk))
                        yield
                    cur = 0
                    for lev in range(1, 6):
                        last = (lev == 5)
                        for (X, xk, Ag, ATg, XTg, ak, atk, xtk) in grp:
                            ca, cat = Ag[cur], ATg[cur]
                            p1, p1k = nbank()
                            for hh in range(4):
                                P.op("pe", lambda e: e.matmul(p1[:, hh * 128:(hh + 1) * 128], cat[:, hh, :], ca[:, hh, :], start=True, stop=True),
                                     [ak(cur), atk(cur)], [p1k])
                            if not last:
                                p2, p2k = nbank()
                                for hh in range(4):
                                    P.op("pe", lambda e: e.matmul(p2[:, hh * 128:(hh + 1) * 128], ca[:, hh, :], cat[:, hh, :], start=True, stop=True),
                                         [ak(cur), atk(cur)], [p2k])
                            P.op("act", lambda e: e.activation(out=Ag[1 - cur][:], in_=v3(p1), func=AF.Copy), [p1k], [ak(1 - cur)])
                            if not last:
                                P.op("act", lambda e: e.activation(out=ATg[1 - cur][:], in_=v3(p2), func=AF.Copy), [p2k], [atk(1 - cur)])
                        yield
                        for (X, xk, Ag, ATg, XTg, ak, atk, xtk) in grp:
                            na = Ag[1 - cur]
                            p3, p3k = nbank()
                            for hh in range(4):
                                P.op("pe", lambda e: e.matmul(p3[:, hh * 128:(hh + 1) * 128], XTg[:, hh, :], na[:, hh, :], start=True, stop=True),
                                     [xtk, ak(1 - cur)], [p3k])
                            if not last:
                                p4, p4k = nbank()
                                for hh in range(4):
                                    P.op("pe", lambda e: e.matmul(p4[:, hh * 128:(hh + 1) * 128], na[:, hh, :], XTg[:, hh, :], start=True, stop=True),
                                         [xtk, ak(1 - cur)], [p4k])
                            P.op("dve", lambda e: e.tensor_tensor(out=X, in0=v3(p3), in1=X, op=ALU.add), [p3k, xk], [xk])
                            if not last:
                                P.op("dve", lambda e: e.tensor_tensor(out=XTg[:], in0=v3(p4), in1=XTg[:], op=ALU.add), [p4k, xtk], [xtk])
                        cur = 1 - cur
                        yield

                nb_pool[0] = [5, 6, 7]
                g1, g2 = dbl_gen(0), prep_b1
                d1 = d2 = False
                while not (d1 and d2):
                    if not d1:
                        try:
                            next(g1)
                        except StopIteration:
                            d1 = True
                    if not d2:
                        try:
                            next(g2)
                        except StopIteration:
                            d2 = True
                evac_hf(1)
                P.op("dve", lambda e: e.tensor_copy(out=s16[:], in_=ps_[:, 0:16]), [psk], ["ks16"])
                nb_pool[0] = list(range(8))
                for _ in dbl_gen(1):
                    pass
                if getattr(self, "rw_stop", 99) <= 4:
                    continue
                pR = [nbank(), nbank()]
                pU = [nbank(), nbank()]
                pY = [nbank(), nbank()]
                pS, pSk = nbank()
                for cp in range(2):
                    c0 = cp * 64
                    cr = slice(c0, c0 + 64)
                    for h in range(16):
                        cb, hp = h // 2, h % 2
                        hsx = hp * 8 + cb
                        pb, pk = pR[h // 8]
                        hc = slice((h % 8) * 64, (h % 8) * 64 + 64)
                        hcf = slice(h * 64, h * 64 + 64)
                        P.op("pe", lambda e: e.matmul(pb[cr, hc], kapt[:, cb, cr], S2[:, cb, hp, :], start=True, stop=False),
                             ["kkapt", "kS2"], [pk])
                        P.op("pe", lambda e: e.matmul(pb[cr, hc], Mk[:, hsx, cr], Vtok[:, hcf], start=False, stop=True), ["kMk", "kVtok"], [pk])
                    P.op("act", lambda e: e.activation(out=RHS[cr, 0:512], in_=pR[0][0][cr, :], func=AF.Copy), [pR[0][1]], ["kRHS"])
                    P.op("dve", lambda e: e.tensor_copy(out=RHS[cr, 512:1024], in_=pR[1][0][cr, :]), [pR[1][1]], ["kRHS"])
                    for h in range(16):
                        hsx = (h % 2) * 8 + h // 2
                        pb, pk = pU[h // 8]
                        hc = slice((h % 8) * 64, (h % 8) * 64 + 64)
                        hcf = slice(h * 64, h * 64 + 64)
                        P.op("pe", lambda e: e.matmul(pb[cr, hc], Xall[:, hsx, cr], RHS[:, hcf], start=True, stop=True), ["kX0", "kX1", "kX2", "kX3", "kRHS"], [pk])
                    P.op("act", lambda e: e.activation(out=UT[cr, 0:512], in_=pU[0][0][cr, :], func=AF.Copy), [pU[0][1]], ["kUT"])
                    P.op("dve", lambda e: e.tensor_copy(out=UT[cr, 512:1024], in_=pU[1][0][cr, :]), [pU[1][1]], ["kUT"])
                    for h in range(16):
                        cb, hp = h // 2, h % 2
                        hsx = hp * 8 + cb
                        pb, pk = pY[h // 8]
                        hc = slice((h % 8) * 64, (h % 8) * 64 + 64)
                        hcf = slice(h * 64, h * 64 + 64)
                        P.op("pe", lambda e: e.matmul(pb[cr, hc], rt[:, cb, cr], S2[:, cb, hp, :], start=True, stop=False),
                             ["krt", "kS2"], [pk])
                        P.op("pe", lambda e: e.matmul(pb[cr, hc], Akr[:, hsx, cr], Vtok[:, hcf], start=False, stop=False), ["kAkr", "kVtok"], [pk])
                        P.op("pe", lambda e: e.matmul(pb[cr, hc], nAbr[:, hsx, cr], UT[:, hcf], start=False, stop=True), ["knAbr", "kUT"], [pk])
                    for h in range(16):
                        cb, b0 = h // 2, (h % 2) * 64
                        hcf = slice(h * 64, h * 64 + 64)
                        P.op("pe", lambda e: e.matmul(pS[b0:b0 + 64, cb * 64:(cb + 1) * 64], khat[cr, hcf], Vtok[cr, hcf], start=True, stop=False),
                             ["kkhat", "kVtok"], [pSk])
                        P.op("pe", lambda e: e.matmul(pS[b0:b0 + 64, cb * 64:(cb + 1) * 64], nbhat[cr, hcf], UT[cr, hcf], start=False, stop=True),
                             ["knbhat", "kUT"], [pSk])
                    for cb in range(8):
                        P.op("dve", lambda e: e.scalar_tensor_tensor(out=S[:, cb, :], in0=S[:, cb, :], scalar=PC[:, cb, cp:cp + 1],
                                                                     in1=pS[:, cb * 64:(cb + 1) * 64], op0=ALU.mult, op1=ALU.add),
                             ["kS", "kPC", pSk], ["kS"])
                    P.op("act", lambda e: e.activation(out=S2[0:64, :, 0, :], in_=S[0:64, :, :], func=AF.Copy), ["kS"], ["kS2"])
                    P.op("pool", lambda e: e.tensor_copy(out=S2[64:128, :, 1, :], in_=S[64:128, :, :]), ["kS"], ["kS2"])
                if getattr(self, "rw_stop", 99) <= 5:
                    continue
                for hf in range(2):
                    pb, pk = pY[hf]
                    P.op("dve", lambda e: e.tensor_reduce(out=m16[:, hf * 8:(hf + 1) * 8], in_=pb.rearrange("p (h v) -> p h v", h=8), axis=AX.X,
                                                          op=ALU.add), [pk], ["km16"])
                P.op("dve", lambda e: e.tensor_scalar(out=m16[:], in0=m16[:], scalar1=1.0 / 64, scalar2=None, op0=ALU.mult), ["km16"], ["km16"])
                for hf in range(2):
                    pb, pk = pY[hf]
                    P.op("dve", lambda e: e.tensor_tensor(out=yc[:, hf * 512:(hf + 1) * 512].rearrange("p (h v) -> p h v", h=8),
                                                          in0=pb.rearrange("p (h v) -> p h v", h=8),
                                                          in1=bc(m16[:, hf * 8:(hf + 1) * 8].unsqueeze(2), [128, 8, 64]), op=ALU.subtract),
                         [pk, "km16"], ["kyc"])
                P.op("act", lambda e: e.activation(out=ysq[:], in_=yc[:], func=AF.Square), ["kyc"], ["kysq"])
                P.op("dve", lambda e: e.tensor_reduce(out=v16[:], in_=ysq[:].rearrange("p (h v) -> p h v", h=16), axis=AX.X, op=ALU.add),
                     ["kysq"], ["kv16"])
                P.op("act", lambda e: e.activation(out=v16[:], in_=v16[:], func=AF.Sqrt, scale=1.0 / 64, bias=self.eps_sb[:, 1:2]),
                     ["kv16", "eps"], ["kv16"])
                P.op("dve", lambda e: e.reciprocal(v16[:], v16[:]), ["kv16"], ["kv16"])
                y3 = lambda t_: t_[:].rearrange("p (h v) -> p h v", h=16)
                P.op("dve", lambda e: e.tensor_tensor(out=y3(yc), in0=y3(yc), in1=bc(v16[:].unsqueeze(2), [128, 16, 64]), op=ALU.mult),
                     ["kyc", "kv16"], ["kyc"])
                P.op("pool", lambda e: e.tensor_tensor(out=yc[:], in0=yc[:], in1=lnw[:], op=ALU.mult), ["kyc", "klnw"], ["kyc"])
                P.op("pool", lambda e: e.tensor_tensor(out=yc[:], in0=yc[:], in1=lnb[:], op=ALU.add), ["kyc", "klnb"], ["kyc"])
                P.op("dve", lambda e: e.tensor_tensor(out=y3(ysq), in0=y3(Vtok), in1=bc(s16[:].unsqueeze(2), [128, 16, 64]), op=ALU.mult),
                     ["kVtok", "ks16", "kysq"], ["kysq"])
                P.op("pool", lambda e: e.tensor_tensor(out=yc[:], in0=yc[:], in1=ysq[:], op=ALU.add), ["kyc", "kysq"], ["kyc"])
                P.op("dve", lambda e: e.tensor_tensor(out=yo[:], in0=yc[:], in1=gtok[:], op=ALU.mult), ["kyc", "kgtok"], ["kyo"])
                pt_, ptk = nbank()
                ptb = pt_.bitcast(BF16)
                for cb in range(8):
                    P.op("pe", lambda e: e.transpose(ptb[:, cb * 128:(cb + 1) * 128], yo[:, cb * 128:(cb + 1) * 128], self.ident_bf[:]),
                         ["kyo", "ident_bf"], [ptk])
                P.op("act", lambda e: e.activation(out=yst[:], in_=ptb.rearrange("p (c t) -> p c t", c=8), func=AF.Copy), [ptk], ["kyst"])
                P.dma("sp", yav[:, :, t0:t0 + 128], yst[:], ["kyst"], ["ya"], "kyst")


def common_inputs(inp, NL, LP):
    m = {}
    for n in ["w_in", "rw_w_up", "rw_a_up", "rw_g_up", "mla_w_uq", "mla_w_ukv", "w_br_rwkv", "w_br_ret", "w_br_mla", "w_out",
              "w_gate_up", "w_down", "rw_ln_w", "rw_ln_b"]:
        m[n] = np.ascontiguousarray(np.asarray(inp[n], np.float32)[:NL])
    m["vecs"] = np.stack([pack_vecs(inp, l) for l in range(NL)], 0)
    m["final_norm_pm"] = pm(inp["final_norm"], 16)
    m["meta"] = np.ascontiguousarray(np.asarray(inp["meta_tokens"], np.float32))
    cs = make_consts(LP)
    for n in CONST_NAMES:
        m["c_" + n] = np.ascontiguousarray(cs[n].astype(np.float32))
    return m


def build_full(L, NL):
    B = Builder(L, NL)
    B.setup_globals()
    B.convert_layer(0)
    B.prologue()
    for l in range(NL):
        B.phase_inproj(l)
        B.phase_rwkv(l)
        B.phase_ret(l)
        B.phase_mla_prep(l)
        B.phase_mla_attn(l)
        B.phase_merge(l)
        if l + 1 < NL:
            B.convert_layer(l + 1)
        B.phase_ffn(l)
    B.epilogue()
    B.P.finish()
    return B


def kernel(**inputs):
    inp = {k: np.asarray(v) for k, v in inputs.items()}
    nb, seq, _ = inp["x"].shape
    L = seq + NMETA
    NL = inp["w_in"].shape[0]
    B = build_full(L, NL)
    cm = common_inputs(inp, NL, B.LP)
    n_cores = 8
    in_maps = []
    for c in range(n_cores):
        m = dict(cm)
        m["x"] = np.ascontiguousarray(inp["x"][c % nb], dtype=np.float32)
        in_maps.append(m)
    res = run_bass_kernel_spmd(B.nc, in_maps, core_ids=list(range(n_cores)))
    out = np.stack([np.asarray(res.results[b]["out"], np.float32) for b in range(nb)], 0)
    return out
```

```python
import math
from contextlib import ExitStack, contextmanager
import numpy as np
import ml_dtypes
import concourse.bass as bass
import concourse.mybir as mybir
from concourse.bass_utils import run_bass_kernel_spmd

F32 = mybir.dt.float32
BF16 = mybir.dt.bfloat16
AF = mybir.ActivationFunctionType
ALU = mybir.AluOpType
AX = mybir.AxisListType

D = 2048
NMETA = 16
RW_H, RW_N = 16, 64
IN_COLS = 14592
FFN_H = 5632
EPS = 1e-6
GN_EPS = 64 * 1e-5
C0 = math.exp(-0.5)
T = 384

ENG = ["pe", "act", "dve", "pool", "sp"]


class Prog:
    def __init__(self, nc):
        self.nc = nc
        self.es = ExitStack()
        self.scopes = [self.es]
        self.sems = {}
        self.count = {}
        self.is_dma_sem = {}
        self.seen = {e: {} for e in ENG}
        self.last_w = {}
        self.readers = {}
        self.engs = {"pe": nc.tensor, "act": nc.scalar, "dve": nc.vector, "pool": nc.gpsimd, "sp": nc.sync}
        self.uid = 0
        self.dmap = {}
        self.free_phys = []
        self.nphys = 0
        self.scope_names = [[]]
        for e in ENG[:4]:
            self._mksem("E_" + e, False)

    def _dsem(self, name, fresh=False):
        if name in self.dmap:
            return self.dmap[name]
        if self.free_phys and not fresh:
            ph = self.free_phys.pop()
        else:
            ph = self._mksem("D%d" % self.nphys, True)
            self.nphys += 1
        self.dmap[name] = ph
        self.scope_names[-1].append(name)
        return ph

    def _mksem(self, name, is_dma):
        if name not in self.sems:
            self.sems[name] = self.es.enter_context(self.nc.semaphore(name))
            self.count[name] = 0
            self.is_dma_sem[name] = is_dma
        return name

    def sb(self, name, shape, dt):
        self.uid += 1
        return self.scopes[-1].enter_context(self.nc.sbuf_tensor("%s_%d" % (name, self.uid), list(shape), dt))

    def ps(self, name, shape, dt):
        self.uid += 1
        return self.scopes[-1].enter_context(self.nc.psum_tensor("%s_%d" % (name, self.uid), list(shape), dt))

    @contextmanager
    def scope(self):
        es = ExitStack()
        self.scopes.append(es)
        self.scope_names.append([])
        try:
            yield
        finally:
            self.barrier()
            self.scopes.pop()
            for n in self.scope_names.pop():
                self.free_phys.append(self.dmap.pop(n))
            es.close()

    def barrier(self):
        for eng in ENG:
            e = self.engs[eng]
            for s, c in self.count.items():
                if self.seen[eng].get(s, 0) < c:
                    e.wait_ge(self.sems[s], c)
                    self.seen[eng][s] = c

    def op(self, eng, fn, reads=(), writes=(), dsem=None):
        deps = {}

        def add(tok):
            if tok is None:
                return
            s, v = tok
            if deps.get(s, 0) < v:
                deps[s] = v

        for k in reads:
            add(self.last_w.get(k))
        for k in writes:
            add(self.last_w.get(k))
            for s, v in self.readers.get(k, {}).items():
                add((s, v))
        e = self.engs[eng]
        for s, v in deps.items():
            if s == "E_pe" and eng == "pe":
                continue
            if self.seen[eng].get(s, 0) < v:
                if self.is_dma_sem[s]:
                    v = self.count[s]
                e.wait_ge(self.sems[s], v)
                self.seen[eng][s] = v
        if dsem is None:
            s = "E_" + eng
            inc = 1
        else:
            s = self._dsem(dsem, fresh=(eng == "pool"))
            inc = 16
        self.count[s] += inc
        tok = (s, self.count[s])
        ins = fn(e)
        ins.then_inc(self.sems[s], inc)
        for k in writes:
            self.last_w[k] = tok
            self.readers[k] = {}
        for k in reads:
            r = self.readers.setdefault(k, {})
            if r.get(s, 0) < tok[1]:
                r[s] = tok[1]
        return tok

    def dma(self, eng, out, in_, reads, writes, dsem, **kw):
        return self.op(eng, lambda e: e.dma_start(out=out, in_=in_, **kw), reads, writes, dsem)

    def finish(self):
        self.barrier()
        self.es.close()


def make_consts(LP):
    c = {}
    c["ident"] = np.eye(128, dtype=np.float32)
    blk = np.zeros((128, 128), np.float32)
    blk[:64, :64] = 1
    blk[64:, 64:] = 1
    c["blkones"] = blk
    bc = np.zeros((128, 2), np.float32)
    bc[:64, 0] = 1
    bc[64:, 1] = 1
    c["blkcols"] = bc
    s128 = np.zeros((128, 128), np.float32)
    for i in range(64):
        s128[i, i + 64] = 1
        s128[i + 64, i] = 1
    c["swap128"] = s128
    s64 = np.zeros((128, 128), np.float32)
    for b in range(2):
        for i in range(32):
            s64[b * 64 + i, b * 64 + i + 32] = 1
            s64[b * 64 + i + 32, b * 64 + i] = 1
    c["swap64"] = s64
    pos = np.arange(LP, dtype=np.float32)
    inv = (np.float32(10000.0) ** (-np.arange(0, 128, 2, dtype=np.float32) / np.float32(128))).astype(np.float32)
    ang = (pos[None, :] * inv[:, None]).astype(np.float32)
    cs, sn = np.cos(ang).astype(np.float32), np.sin(ang).astype(np.float32)
    c["ret_cos"] = np.concatenate([cs, cs], 0)
    c["ret_sin"] = np.concatenate([-sn, sn], 0)
    inv = (np.float32(10000.0) ** (-np.arange(0, 64, 2, dtype=np.float32) / np.float32(64))).astype(np.float32)
    ang = (pos[None, :] * inv[:, None]).astype(np.float32)
    cs, sn = np.cos(ang).astype(np.float32), np.sin(ang).astype(np.float32)
    c["mla_cos"] = np.concatenate([cs, cs, cs, cs], 0)
    c["mla_sin"] = np.concatenate([-sn, sn, -sn, sn], 0)
    c["rope_tab"] = np.ascontiguousarray(np.stack([c["ret_cos"], c["ret_sin"], c["mla_cos"], c["mla_sin"]], 1))
    k = np.arange(128)[:, None]
    q = np.arange(T)[None, :]
    mm = np.zeros((128, 3, T), np.float32)
    for m in range(3):
        mm[:, m, :] = (q >= 128 * m + k)
    c["mla_mask"] = mm
    rt = np.zeros((8, 128, 512), np.float32)
    lg = np.log1p(-np.exp2(-5.0 - np.arange(8, dtype=np.float64)))
    for h in range(8):
        rt[h, :, 0:T] = np.exp(lg[h] * np.arange(T, dtype=np.float64))[None, :]
        rt[h, :, T:T + 128] = np.exp(-lg[h] * np.arange(128, dtype=np.float64))[None, :]
    c["ret_tab"] = rt
    c["ret_lg"] = lg
    j = np.arange(128)[:, None]
    t = np.arange(128)[None, :]
    same = (j // 64) == (t // 64)
    rm = np.zeros((128, 5, 128), np.float32)
    rm[:, 0, :] = -1.0 * (same & (j < t))
    rm[:, 1, :] = -1.0 * (same & (j > t))
    rm[:, 2, :] = 1.0 * (same & (j < t))
    rm[:, 3, :] = 1.0 * (same & (j <= t))
    rm[:, 4, :] = -1.0 * (same & (j <= t))
    c["rw_masks"] = rm
    rs = np.ones((128, 128), np.float32)
    rs[:, 0] = 0
    rs[:, 64] = 0
    c["rw_reset"] = rs
    return c


CONST_NAMES = ["ident", "blkones", "blkcols", "swap128", "swap64", "rope_tab",
               "mla_mask", "ret_tab", "rw_masks", "rw_reset"]

VEC_LAYOUT = [("norm_mix", 16), ("norm_ffn", 16), ("mu", 28), ("w0", 8), ("a0", 8), ("k_k", 8), ("k_a", 8),
              ("r_k", 8), ("nq", 4), ("nkv", 2)]
VOFF = {}
_o = 0
for _n, _w in VEC_LAYOUT:
    VOFF[_n] = _o
    _o += _w
NV = _o


def pm(v, n):
    return np.ascontiguousarray(np.asarray(v, np.float32).reshape(n, 128).T)


def pack_vecs(inp, l):
    out = np.zeros((128, NV), np.float32)

    def put(name, arr):
        out[:, VOFF[name]:VOFF[name] + arr.shape[1]] = arr

    put("norm_mix", pm(inp["norm_mix"][l], 16))
    put("norm_ffn", pm(inp["norm_ffn"][l], 16))
    mu = np.asarray(inp["rw_mu"][l], np.float32)
    mu28 = np.zeros(28 * 128, np.float32)
    mu28[0:3072] = mu[0:3072]
    mu28[3072:3072 + 96] = mu[3072:3168]
    mu28[3200:3200 + 96] = mu[3168:3264]
    mu28[3328:3328 + 256] = mu[3264:3520]
    put("mu", pm(mu28, 28))
    put("w0", pm(inp["rw_w0"][l], 8))
    put("a0", pm(inp["rw_a0"][l], 8))
    put("k_k", pm(inp["rw_k_k"][l], 8))
    put("k_a", pm(inp["rw_k_a"][l], 8))
    put("r_k", pm(inp["rw_r_k"][l], 8))
    put("nq", pm(inp["mla_norm_q"][l], 4))
    put("nkv", pm(inp["mla_norm_kv"][l], 2))
    return out


IN_GROUPS = []
for _seg, _base in (("rw_r", 0), ("rw_k", 1024), ("rw_v", 2048)):
    for _i in range(2):
        IN_GROUPS.append((_seg, _base + 512 * _i, 512, _i))
IN_GROUPS.append(("rw_lora", 3072, 448, 0))
for _seg, _base in (("ret_q", 3520), ("ret_k", 4544), ("ret_v", 5568), ("ret_g", 6592)):
    for _i in range(2):
        IN_GROUPS.append((_seg, _base + 512 * _i, 512, _i))
IN_GROUPS.append(("mla_q", 7616, 512, 0))
IN_GROUPS.append(("mla_kv", 8128, 320, 0))
for _i in range(12):
    IN_GROUPS.append(("gate", 8448 + 512 * _i, 512, _i))


class WStream:
    def __init__(self, B, name, items, KC, width, nbuf=2):
        self.B = B
        self.P = B.P
        self.items = items
        self.name = name
        self.bufs = [B.P.sb(name, [128, KC, width], BF16) for _ in range(nbuf)]
        self.nbuf = nbuf
        self.next = 0

    def prefetch(self):
        if self.next >= len(self.items):
            return
        i = self.next
        self.next += 1
        ap, key, kc, w = self.items[i]
        b = i % self.nbuf
        bk = "%s%d" % (self.name, b)
        self.P.dma("sp", self.bufs[b][:, :kc, :w], ap, [key], [bk], bk)

    def get(self, i):
        while self.next <= i:
            self.prefetch()
        b = i % self.nbuf
        return self.bufs[b], "%s%d" % (self.name, b)


class Builder:
    def __init__(self, L, NL, debug=(), ext_in=()):
        self.ext_in = set(ext_in)
        self.L = L
        self.SEQ = L - NMETA
        self.LP = ((L + T - 1) // T) * T
        self.NT = self.LP // T
        self.N128 = self.LP // 128
        self.NL = NL
        self.debug = set(debug)
        nc = self.nc = bass.Bass("TRN2", target_bir_lowering=False)
        self.P = Prog(nc)
        LP, SEQ = self.LP, self.SEQ

        def di(n, s, dt=F32):
            return nc.dram_tensor(n, list(s), dt, kind="ExternalInput").ap()

        self.x = di("x", [SEQ, D])
        self.meta = di("meta", [NMETA, D])
        self.w_in = di("w_in", [NL, D, IN_COLS])
        self.w_up = di("rw_w_up", [NL, 96, 1024])
        self.a_up = di("rw_a_up", [NL, 96, 1024])
        self.g_up = di("rw_g_up", [NL, 256, 1024])
        self.w_uq = di("mla_w_uq", [NL, 512, 1536])
        self.w_ukv = di("mla_w_ukv", [NL, 256, 2048])
        self.w_bra = di("w_br_rwkv", [NL, 1024, D])
        self.w_brb = di("w_br_ret", [NL, 1024, D])
        self.w_brc = di("w_br_mla", [NL, 1024, D])
        self.w_out = di("w_out", [NL, D, D])
        self.w_gu = di("w_gate_up", [NL, D, 2 * FFN_H])
        self.w_dn = di("w_down", [NL, FFN_H, D])
        self.vecs = di("vecs", [NL, 128, NV])
        self.lnw = di("rw_ln_w", [NL, 1024])
        self.lnb = di("rw_ln_b", [NL, 1024])
        self.fng = di("final_norm_pm", [128, 16])
        self.cst = {}
        cs = make_consts(LP)
        for n in CONST_NAMES:
            self.cst[n] = di("c_" + n, cs[n].shape)
        self.out = nc.dram_tensor("out", [SEQ, D], F32, kind="ExternalOutput").ap()
        self.hT = self.ds("hT", [D, LP], F32)
        self.zrw = self.ds("zrw", [3584, 1 + LP], F32)
        self.retq = self.ds("retq", [1024, LP], BF16)
        self.retk = self.ds("retk", [1024, LP], BF16)
        self.retv = self.ds("retv", [LP, 1024], BF16)
        self.retg = self.ds("retg", [1024, LP], F32)
        self.qd = self.ds("qd", [512, LP], F32)
        self.kvd = self.ds("kvd", [256, LP], F32)
        self.kr = self.ds("kr", [128, LP], BF16)
        self.gates = self.ds("gates", [6144, LP], BF16)
        self.ya = self.ds("ya", [1024, LP], BF16)
        self.yb = self.ds("yb", [1024, LP], BF16)
        self.yc = self.ds("yc", [1024, LP], BF16)
        self.qn = self.ds("qn", [1024, LP], BF16)
        self.qr = self.ds("qr", [512, LP], BF16)
        self.kn = self.ds("kn", [1024, LP], BF16)
        self.vm = self.ds("vm", [LP, 1024], BF16)
        self.wt = {}

    def ds(self, n, s, dt):
        kind = "ExternalOutput" if n in self.debug else ("ExternalInput" if n in self.ext_in else "Internal")
        return self.nc.dram_tensor(n, list(s), dt, kind=kind).ap()

    def conv(self, l, tag, src2d, cs, width, KC):
        name = "wt%d_%s_%d" % (l, tag, cs)
        wt = self.nc.dram_tensor(name, [128, KC, width], BF16, kind="Internal").ap()
        self.P.dma("pool", wt, src2d.rearrange("(kc p) c -> p kc c", p=128)[:, :, cs:cs + width], [], [name],
                   "cv%d_%s" % (l, tag))
        return (wt, name, KC, width)

    def convert_layer(self, l):
        w = {}
        w["in"] = [self.conv(l, "in", self.w_in[l], cs, wd, 16) for (_, cs, wd, _) in IN_GROUPS]
        w["gup"] = [self.conv(l, "g", self.g_up[l], cs, 512, 2) for cs in (0, 512)]
        w["uq"] = [self.conv(l, "uq", self.w_uq[l], cs, 384, 4) for cs in range(0, 1536, 384)]
        w["ukv"] = [self.conv(l, "ukv", self.w_ukv[l], cs, 512, 2) for cs in range(0, 2048, 512)]
        w["bra"] = [self.conv(l, "bra", self.w_bra[l], cs, 256, 8) for cs in range(0, D, 256)]
        w["brb"] = [self.conv(l, "brb", self.w_brb[l], cs, 256, 8) for cs in range(0, D, 256)]
        w["brc"] = [self.conv(l, "brc", self.w_brc[l], cs, 256, 8) for cs in range(0, D, 256)]
        w["out"] = [self.conv(l, "out", self.w_out[l], cs, 256, 16) for cs in range(0, D, 256)]
        w["gate"] = [self.conv(l, "gu", self.w_gu[l], cs, 256, 16) for cs in range(0, FFN_H, 256)]
        w["up"] = [self.conv(l, "gu", self.w_gu[l], FFN_H + cs, 256, 16) for cs in range(0, FFN_H, 256)]
        w["dn"] = [self.conv(l, "dn", self.w_dn[l], cs, 256, 44) for cs in range(0, D, 256)]
        self.wt[l] = w

    def setup_globals(self):
        P = self.P
        self.psum = P.ps("psum", [128, 8, 512], F32)
        self.cs = {}
        for n in ["ident", "blkones", "blkcols", "swap128", "swap64"]:
            t = P.sb("c_" + n, list(self.cst[n].shape), F32)
            P.dma("sp", t[:], self.cst[n], [], ["c_" + n], "c_" + n)
            self.cs[n] = t
        self.ones_bf = P.sb("ones_bf", [128, 128], BF16)
        P.op("pool", lambda e: e.memset(self.ones_bf[:], 1.0), [], ["ones_bf"])
        self.ident_bf = P.sb("ident_bf", [128, 128], BF16)
        P.op("act", lambda e: e.activation(out=self.ident_bf[:], in_=self.cs["ident"][:], func=AF.Copy), ["c_ident"],
             ["ident_bf"])
        self.vec_sb = P.sb("vecs", [128, self.NL, NV], F32)
        P.dma("sp", self.vec_sb[:], self.vecs.rearrange("l p v -> p l v"), [], ["vecs"], "vecs")
        self.fng_sb = P.sb("fng", [128, 16], F32)
        P.dma("sp", self.fng_sb[:], self.fng, [], ["fng"], "fng")
        self.eps_sb = P.sb("eps", [128, 2], F32)
        P.op("pool", lambda e: e.memset(self.eps_sb[:, 0:1], EPS), [], ["eps"])
        P.op("pool", lambda e: e.memset(self.eps_sb[:, 1:2], GN_EPS), ["eps"], ["eps"])
        self.zero_sb = P.sb("zero", [128, 512], F32)
        P.op("pool", lambda e: e.memset(self.zero_sb[:], 0.0), [], ["zero"])
        P.dma("sp", self.zrw.rearrange("(b p) t -> p b t", p=128)[:, :, 0:1], self.zero_sb[:, 0:28].unsqueeze(2),
              ["zero"], ["zrw"], "zero", allow_slow_non_contiguous=True)
        for r0 in (3168, 3296):
            for c0 in range(0, self.LP + 1, 512):
                c1 = min(c0 + 512, self.LP + 1)
                P.dma("sp", self.zrw[r0:r0 + 32, c0:c1], self.zero_sb[0:32, 0:c1 - c0], ["zero"], ["zrw"], "zero")

    def bank(self, b):
        return self.psum[:, b, :], "pb%d" % b

    def vec(self, l, name, c0=0, n=1):
        o = VOFF[name] + c0
        return self.vec_sb[:, l, o:o + n]

    def fm_norm(self, hsb, hkey, KC, TT, gains, out, okey, nfeat, sq, sqkey, rstd, rkey, b):
        P = self.P
        pb, pk = self.bank(b)
        P.op("act", lambda e: e.activation(out=sq[:, :KC, :TT], in_=hsb[:, :KC, :TT], func=AF.Square), [hkey], [sqkey])
        for kc in range(KC):
            P.op("pe", lambda e, kc=kc: e.matmul(pb[:, :TT], self.ones_bf[:], sq[:, kc, :TT], start=(kc == 0),
                                                 stop=(kc == KC - 1)), [sqkey, "ones_bf"], [pk])
        P.op("act", lambda e: e.activation(out=rstd[:, :TT], in_=pb[:, :TT], func=AF.Sqrt, scale=1.0 / nfeat,
                                           bias=self.eps_sb[:, 0:1]), [pk, "eps"], [rkey])
        P.op("dve", lambda e: e.reciprocal(rstd[:, :TT], rstd[:, :TT]), [rkey], [rkey])
        for kc in range(KC):
            P.op("dve", lambda e, kc=kc: e.scalar_tensor_tensor(out=out[:, kc, :TT], in0=hsb[:, kc, :TT],
                                                                scalar=gains[:, kc:kc + 1], in1=rstd[:, :TT],
                                                                op0=ALU.mult, op1=ALU.mult), [hkey, rkey, "vecs", "fng"], [okey])

    def prologue(self):
        P = self.P
        L, LP = self.L, self.LP
        with P.scope():
            xt = [P.sb("xt", [128, D], F32) for _ in range(2)]
            hts = [P.sb("hts", [128, 16, 128], F32) for _ in range(2)]
            hTv = self.hT.rearrange("(kc p) t -> p kc t", p=128)
            for i in range(self.N128):
                r = i % 2
                xk = "xt%d" % r
                t0 = i * 128
                lo, hi = max(t0, NMETA), min(t0 + 128, L)
                if i == 0 or hi < t0 + 128:
                    P.op("pool", lambda e, r=r: e.memset(xt[r][:], 0.0), [], [xk])
                if i == 0:
                    P.dma("sp", xt[r][0:NMETA, :], self.meta, [], [xk], xk)
                if hi > lo:
                    P.dma("sp", xt[r][lo - t0:hi - t0, :], self.x[lo - NMETA:hi - NMETA, :], [], [xk], xk)
                for q in range(4):
                    pb, pk = self.bank(4 * r + q)
                    for j in range(4):
                        kc = 4 * q + j
                        P.op("pe", lambda e, pb=pb, j=j, kc=kc: e.transpose(pb[:, j * 128:(j + 1) * 128],
                                                                            xt[r][:, kc * 128:(kc + 1) * 128],
                                                                            self.cs["ident"][:]), [xk, "c_ident"], [pk])
                    eng = "act" if q % 2 else "dve"
                    if eng == "act":
                        P.op("act", lambda e, pb=pb, q=q: e.activation(out=hts[r][:, 4 * q:4 * q + 4, :],
                                                                       in_=pb.rearrange("p (j t) -> p j t", j=4),
                                                                       func=AF.Copy), [pk], ["hts%d" % r])
                    else:
                        P.op("dve", lambda e, pb=pb, q=q: e.tensor_copy(out=hts[r][:, 4 * q:4 * q + 4, :],
                                                                        in_=pb.rearrange("p (j t) -> p j t", j=4)),
                             [pk], ["hts%d" % r])
                P.dma("sp", hTv[:, :, t0:t0 + 128], hts[r][:], ["hts%d" % r], ["hT"], "hts%d" % r)

    def epilogue(self):
        P = self.P
        L = self.L
        with P.scope():
            hs = [P.sb("ehs", [128, 16, 128], F32) for _ in range(2)]
            sq = P.sb("esq", [128, 16, 128], BF16)
            rstd = P.sb("erstd", [128, 128], F32)
            un = [P.sb("eun", [128, 16, 128], F32) for _ in range(2)]
            osb = [P.sb("eosb", [128, D], F32) for _ in range(2)]
            hTv = self.hT.rearrange("(kc p) t -> p kc t", p=128)
            for i in range(self.N128):
                t0 = i * 128
                lo, hi = max(t0, NMETA), min(t0 + 128, L)
                if hi <= lo:
                    continue
                r = i % 2
                P.dma("sp", hs[r][:], hTv[:, :, t0:t0 + 128], ["hT"], ["ehs%d" % r], "ehs%d" % r)
                self.fm_norm(hs[r], "ehs%d" % r, 16, 128, self.fng_sb, un[r], "eun%d" % r, D, sq, "esq", rstd, "erstd", 4 * r)
                for q in range(4):
                    pb, pk = self.bank(4 * r + q)
                    for j in range(4):
                        kc = 4 * q + j
                        P.op("pe", lambda e, pb=pb, j=j, kc=kc: e.transpose(pb[:, j * 128:(j + 1) * 128], un[r][:, kc, :],
                                                                            self.cs["ident"][:]), ["eun%d" % r, "c_ident"], [pk])
                    if q % 2:
                        P.op("act", lambda e, pb=pb, q=q: e.activation(out=osb[r][:, q * 512:(q + 1) * 512], in_=pb,
                                                                       func=AF.Copy), [pk], ["eosb%d" % r])
                    else:
                        P.op("dve", lambda e, pb=pb, q=q: e.tensor_copy(out=osb[r][:, q * 512:(q + 1) * 512], in_=pb),
                             [pk], ["eosb%d" % r])
                P.dma("sp", self.out[lo - NMETA:hi - NMETA, :], osb[r][lo - t0:hi - t0, :], ["eosb%d" % r], ["out"],
                      "eosb%d" % r)

    def phase_inproj(self, l):
        P = self.P
        LP = self.LP
        w = self.wt[l]["in"]
        with P.scope():
            hs = [P.sb("ahs", [128, 16, T], F32) for _ in range(1)]
            sq = P.sb("asq", [128, 16, T], BF16)
            rstd = P.sb("arstd", [128, T], F32)
            u = P.sb("au", [128, 16, T], BF16)
            stg = [P.sb("astg", [128, 4, T], F32) for _ in range(3)]
            stgb = [P.sb("astgb", [128, 4, T], BF16) for _ in range(2)]
            stv = [P.sb("astv", [128, 3, 512], BF16) for _ in range(2)]
            zf = [P.sb("azf", [128, T], F32) for _ in range(2)]
            t1 = [P.sb("at1", [128, T], F32) for _ in range(2)]
            t2 = [P.sb("at2", [128, T], F32) for _ in range(2)]
            tabs = [P.sb("atab", [128, 4, T], F32) for _ in range(2)]
            ws = WStream(self, "aw", [], 16, 512, nbuf=3)
            ws.items = [w[gi] for _s in range(self.NT) for gi in range(len(IN_GROUPS))]
            hTv = self.hT.rearrange("(kc p) t -> p kc t", p=128)
            cnt = {"stg": 0, "stgb": 0, "stv": 0, "rot": 0, "bank": 0}
            gains = self.vec(l, "norm_mix", 0, 16)

            def nbank():
                b = cnt["bank"] % 6
                cnt["bank"] += 1
                return self.bank(b)

            def mm16(pb, pk, wb, wk, o, wd, out_rows=None):
                dst = pb[:wd, :T] if out_rows is None else pb[out_rows[0]:out_rows[1], :T]
                for kc in range(16):
                    P.op("pe", lambda e, kc=kc: e.matmul(dst, wb[:, kc, o:o + wd], u[:, kc, :], start=(kc == 0),
                                                         stop=(kc == 15)), [wk, "au"], [pk])

            def copy_evac(i, pb, pk, wd, dst, dk, func=AF.Copy):
                if func == AF.Copy and i % 2 == 0:
                    P.op("dve", lambda e: e.tensor_copy(out=dst, in_=pb[:wd, :T]), [pk], [dk])
                else:
                    P.op("act", lambda e: e.activation(out=dst, in_=pb[:wd, :T], func=func), [pk], [dk])

            for s in range(self.NT):
                r = 0
                hk = "ahs0"
                ts0 = s * T
                P.dma("sp", hs[0][:], hTv[:, :, ts0:ts0 + T], ["hT"], ["ahs0"], "ahs0")
                tb = tabs[s % 2]
                tk = "atab%d" % (s % 2)
                P.dma("sp", tb[:], self.cst["rope_tab"][:, :, ts0:ts0 + T], [], [tk], tk)
                self.fm_norm(hs[r], hk, 16, T, gains, u, "au", D, sq, "asq", rstd, "arstd", 7)
                for gi, (seg, cs, gw, gidx) in enumerate(IN_GROUPS):
                    wb, wk = ws.get(s * len(IN_GROUPS) + gi)
                    while ws.next <= s * len(IN_GROUPS) + gi + 2:
                        if ws.next >= len(ws.items):
                            break
                        ws.prefetch()
                    if seg == "gate":
                        bi = cnt["stgb"] % 2
                        cnt["stgb"] += 1
                        bk = "astgb%d" % bi
                        for j in range(4):
                            pb, pk = nbank()
                            mm16(pb, pk, wb, wk, j * 128, 128)
                            copy_evac(j, pb, pk, 128, stgb[bi][:, j, :], bk, AF.Sigmoid)
                        P.dma("sp", self.gates[512 * gidx:512 * gidx + 512, ts0:ts0 + T].rearrange("(j p) t -> p j t", p=128), stgb[bi][:],
                              [bk], ["gates"], bk)
                    elif seg in ("rw_r", "rw_k", "rw_v", "ret_g", "mla_q"):
                        si = cnt["stg"] % 3
                        cnt["stg"] += 1
                        sk = "astg%d" % si
                        func = {"ret_g": AF.Silu}.get(seg, AF.Copy)
                        for j in range(4):
                            pb, pk = nbank()
                            mm16(pb, pk, wb, wk, j * 128, 128)
                            copy_evac(j, pb, pk, 128, stg[si][:, j, :], sk, func)
                        if seg.startswith("rw_"):
                            row0 = {"rw_r": 0, "rw_k": 1024, "rw_v": 2048}[seg] + 512 * gidx
                            dst = self.zrw[row0:row0 + 512, 1 + ts0:1 + ts0 + T]
                            dk = "zrw"
                        elif seg == "ret_g":
                            dst = self.retg[512 * gidx:512 * gidx + 512, ts0:ts0 + T]
                            dk = "retg"
                        elif seg == "mla_q":
                            dst = self.qd[0:512, ts0:ts0 + T]
                            dk = "qd"
                        else:
                            dst = self.gates[512 * gidx:512 * gidx + 512, ts0:ts0 + T]
                            dk = "gates"
                        P.dma("sp", dst.rearrange("(j p) t -> p j t", p=128), stg[si][:], [sk], [dk], sk)
                    elif seg == "rw_lora":
                        si = cnt["stg"] % 3
                        cnt["stg"] += 1
                        sk = "astg%d" % si
                        for j, (o, wd, row0) in enumerate(((0, 96, 3072), (96, 96, 3200), (192, 128, 3328), (320, 128, 3456))):
                            pb, pk = nbank()
                            mm16(pb, pk, wb, wk, o, wd)
                            copy_evac(j, pb, pk, wd, stg[si][:wd, j, :], sk)
                            P.dma("sp", self.zrw[row0:row0 + wd, 1 + ts0:1 + ts0 + T], stg[si][:wd, j, :], [sk], ["zrw"], sk)
                    elif seg in ("ret_q", "ret_k", "mla_kv"):
                        bi = cnt["stgb"] % 2
                        cnt["stgb"] += 1
                        bk = "astgb%d" % bi
                        if seg == "mla_kv":
                            si = cnt["stg"] % 3
                            cnt["stg"] += 1
                            sk = "astg%d" % si
                            for j in range(2):
                                pb, pk = nbank()
                                mm16(pb, pk, wb, wk, j * 128, 128)
                                copy_evac(j, pb, pk, 128, stg[si][:, j, :], sk)
                            P.dma("sp", self.kvd[0:256, ts0:ts0 + T].rearrange("(j p) t -> p j t", p=128), stg[si][:, 0:2, :],
                                  [sk], ["kvd"], sk)
                            jobs = [(None, self.cs["swap64"], 2, 3, "c_swap64")]
                        else:
                            jobs = [(j, self.cs["swap128"], 0, 1, "c_swap128") for j in range(4)]
                        for (j, swp, ctab, stab, swk) in jobs:
                            rr = cnt["rot"] % 2
                            cnt["rot"] += 1
                            pb, pk = nbank()
                            if j is None:
                                mm16(pb, pk, wb, wk, 256, 64, out_rows=(0, 64))
                                mm16(pb, pk, wb, wk, 256, 64, out_rows=(64, 128))
                                jj = 0
                            else:
                                mm16(pb, pk, wb, wk, j * 128, 128)
                                jj = j
                            P.op("act", lambda e, pb=pb, rr=rr: e.activation(out=zf[rr][:], in_=pb[:, :T], func=AF.Copy),
                                 [pk], ["azf%d" % rr])
                            pb2, pk2 = nbank()
                            P.op("pe", lambda e, pb2=pb2, rr=rr, swp=swp: e.matmul(pb2[:, :T], swp[:], zf[rr][:], start=True,
                                                                                 stop=True), ["azf%d" % rr, swk], [pk2])
                            P.op("pool", lambda e, rr=rr, ctab=ctab: e.tensor_tensor(out=t1[rr][:], in0=zf[rr][:],
                                                                                   in1=tb[:, ctab, :], op=ALU.mult),
                                 ["azf%d" % rr, tk], ["at1%d" % rr])
                            P.op("dve", lambda e, pb2=pb2, rr=rr, stab=stab: e.tensor_tensor(out=t2[rr][:], in0=pb2[:, :T],
                                                                                           in1=tb[:, stab, :], op=ALU.mult),
                                 [pk2, tk], ["at2%d" % rr])
                            P.op("pool", lambda e, rr=rr, bi=bi, jj=jj: e.tensor_tensor(out=stgb[bi][:, jj, :], in0=t1[rr][:],
                                                                                      in1=t2[rr][:], op=ALU.add),
                                 ["at1%d" % rr, "at2%d" % rr], [bk])
                        if seg == "mla_kv":
                            P.dma("sp", self.kr[:, ts0:ts0 + T], stgb[bi][:, 0, :], [bk], ["kr"], bk)
                        else:
                            dt_ = self.retq if seg == "ret_q" else self.retk
                            P.dma("sp", dt_[512 * gidx:512 * gidx + 512, ts0:ts0 + T].rearrange("(j p) t -> p j t", p=128),
                                  stgb[bi][:], [bk], [seg], bk)
                    elif seg == "ret_v":
                        vi = cnt["stv"] % 2
                        cnt["stv"] += 1
                        vk = "astv%d" % vi
                        for tsub in range(3):
                            pb, pk = nbank()
                            for kc in range(16):
                                P.op("pe", lambda e, kc=kc, pb=pb, tsub=tsub: e.matmul(pb[:, :512], u[:, kc, tsub * 128:(tsub + 1) * 128],
                                                                                     wb[:, kc, 0:512], start=(kc == 0), stop=(kc == 15)),
                                     [wk, "au"], [pk])
                            if tsub % 2:
                                P.op("act", lambda e, pb=pb, tsub=tsub: e.activation(out=stv[vi][:, tsub, :], in_=pb[:, :512], func=AF.Copy),
                                     [pk], [vk])
                            else:
                                P.op("dve", lambda e, pb=pb, tsub=tsub: e.tensor_copy(out=stv[vi][:, tsub, :], in_=pb[:, :512]), [pk], [vk])
                        P.dma("sp", self.retv[ts0:ts0 + T, 512 * gidx:512 * gidx + 512].rearrange("(a p) c -> p a c", p=128),
                              stv[vi][:], [vk], ["retv"], vk)
                    else:
                        raise ValueError(seg)

    def phase_merge(self, l):
        P = self.P
        w = self.wt[l]
        with P.scope():
            hs = P.sb("ehs", [128, 16, T], F32)
            ys = [P.sb("eys", [128, 8, T], BF16) for _ in range(3)]
            mg = P.sb("emg", [128, 16, T], BF16)
            gt = [P.sb("egt", [128, 3, 2, T], BF16) for _ in range(2)]
            ta = [P.sb("eta", [128, T], F32) for _ in range(2)]
            tb = [P.sb("etb", [128, T], F32) for _ in range(2)]
            wsa = WStream(self, "ewa", [w["bra"][g] for _s in range(self.NT) for g in range(8)], 8, 256, 2)
            wsb = WStream(self, "ewb", [w["brb"][g] for _s in range(self.NT) for g in range(8)], 8, 256, 2)
            wsc = WStream(self, "ewc", [w["brc"][g] for _s in range(self.NT) for g in range(8)], 8, 256, 2)
            wso = WStream(self, "ewo", [w["out"][g] for _s in range(self.NT) for g in range(8)], 16, 256, 2)
            hTv = self.hT.rearrange("(kc p) t -> p kc t", p=128)
            gv = self.gates.rearrange("(b m p) t -> p b m t", b=3, p=128)
            nb = [0]

            def nbank():
                b = nb[0] % 8
                nb[0] += 1
                return self.bank(b)

            for s in range(self.NT):
                ts0 = s * T
                P.dma("sp", hs[:], hTv[:, :, ts0:ts0 + T], ["hT"], ["ehs"], "ehs")
                for bi, (src, nm) in enumerate(((self.ya, "ya"), (self.yb, "yb"), (self.yc, "yc"))):
                    P.dma("sp", ys[bi][:], src[:, ts0:ts0 + T].rearrange("(c p) t -> p c t", p=128), [nm], ["eys%d" % bi],
                          "eys%d" % bi)
                for g in range(8):
                    i = s * 8 + g
                    bufs = []
                    for ws in (wsa, wsb, wsc):
                        bufs.append(ws.get(i))
                        ws.prefetch()
                    gr = g % 2
                    gk = "egt%d" % gr
                    for b3 in range(3):
                        P.dma("sp", gt[gr][:, b3], gv[:, b3, 2 * g:2 * g + 2, ts0:ts0 + T], ["gates"], [gk], gk)
                    for j in range(2):
                        m = 2 * g + j
                        pbs = []
                        for bi in range(3):
                            pb, pk = nbank()
                            wb, wk = bufs[bi]
                            for kc in range(8):
                                P.op("pe", lambda e, pb=pb, wb=wb, kc=kc, bi=bi: e.matmul(pb[:, :T], wb[:, kc, j * 128:(j + 1) * 128],
                                                                                        ys[bi][:, kc, :], start=(kc == 0), stop=(kc == 7)),
                                     [wk, "eys%d" % bi], [pk])
                            pbs.append((pb, pk))
                        r = m % 2
                        P.op("dve", lambda e, r=r: e.tensor_tensor(out=ta[r][:], in0=pbs[0][0][:, :T], in1=gt[gr][:, 0, j, :], op=ALU.mult),
                             [pbs[0][1], gk], ["eta%d" % r])
                        P.op("dve", lambda e, r=r: e.tensor_tensor(out=tb[r][:], in0=pbs[1][0][:, :T], in1=gt[gr][:, 1, j, :], op=ALU.mult),
                             [pbs[1][1], gk], ["etb%d" % r])
                        P.op("pool", lambda e, r=r: e.tensor_tensor(out=ta[r][:], in0=ta[r][:], in1=tb[r][:], op=ALU.add),
                             ["eta%d" % r, "etb%d" % r], ["eta%d" % r])
                        P.op("dve", lambda e, r=r: e.tensor_tensor(out=tb[r][:], in0=pbs[2][0][:, :T], in1=gt[gr][:, 2, j, :], op=ALU.mult),
                             [pbs[2][1], gk, "eta%d" % r], ["etb%d" % r])
                        P.op("pool", lambda e, r=r, m=m: e.tensor_tensor(out=mg[:, m, :], in0=ta[r][:], in1=tb[r][:], op=ALU.add),
                             ["eta%d" % r, "etb%d" % r], ["emg"])
                for g in range(8):
                    wb, wk = wso.get(s * 8 + g)
                    wso.prefetch()
                    for j in range(2):
                        m = 2 * g + j
                        pb, pk = nbank()
                        for kc in range(16):
                            P.op("pe", lambda e, pb=pb, kc=kc: e.matmul(pb[:, :T], wb[:, kc, j * 128:(j + 1) * 128], mg[:, kc, :],
                                                                      start=(kc == 0), stop=(kc == 15)), [wk, "emg"], [pk])
                        P.op("dve", lambda e, pb=pb, m=m: e.tensor_tensor(out=hs[:, m, :], in0=pb[:, :T], in1=hs[:, m, :], op=ALU.add),
                             [pk, "ehs"], ["ehs"])
                P.dma("sp", hTv[:, :, ts0:ts0 + T], hs[:], ["ehs"], ["hT"], "ehs")

    def phase_ffn(self, l):
        P = self.P
        w = self.wt[l]
        NG = FFN_H // 256
        with P.scope():
            hs = P.sb("fhs", [128, 16, T], F32)
            sq = P.sb("fsq", [128, 16, T], BF16)
            rstd = P.sb("frstd", [128, T], F32)
            u = P.sb("fu", [128, 16, T], BF16)
            act = P.sb("fact", [128, 44, T], BF16)
            sg = [P.sb("fsg", [128, T], F32) for _ in range(2)]
            wsg = WStream(self, "fwg", [w["gate"][g] for _s in range(self.NT) for g in range(NG)], 16, 256, 2)
            wsu = WStream(self, "fwu", [w["up"][g] for _s in range(self.NT) for g in range(NG)], 16, 256, 2)
            wsd = WStream(self, "fwd", [w["dn"][g] for _s in range(self.NT) for g in range(8)], 44, 256, 2)
            hTv = self.hT.rearrange("(kc p) t -> p kc t", p=128)
            gains = self.vec(l, "norm_ffn", 0, 16)
            nb = [0]

            def nbank():
                b = nb[0] % 7
                nb[0] += 1
                return self.bank(b)

            for s in range(self.NT):
                ts0 = s * T
                P.dma("sp", hs[:], hTv[:, :, ts0:ts0 + T], ["hT"], ["fhs"], "fhs")
                self.fm_norm(hs, "fhs", 16, T, gains, u, "fu", D, sq, "fsq", rstd, "frstd", 7)
                for g in range(NG):
                    wg, wgk = wsg.get(s * NG + g)
                    wsg.prefetch()
                    wu, wuk = wsu.get(s * NG + g)
                    wsu.prefetch()
                    for j in range(2):
                        f = 2 * g + j
                        pg, pgk = nbank()
                        pu, puk = nbank()
                        for kc in range(16):
                            P.op("pe", lambda e, kc=kc, pg=pg: e.matmul(pg[:, :T], wg[:, kc, j * 128:(j + 1) * 128], u[:, kc, :],
                                                                      start=(kc == 0), stop=(kc == 15)), [wgk, "fu"], [pgk])
                        for kc in range(16):
                            P.op("pe", lambda e, kc=kc, pu=pu: e.matmul(pu[:, :T], wu[:, kc, j * 128:(j + 1) * 128], u[:, kc, :],
                                                                      start=(kc == 0), stop=(kc == 15)), [wuk, "fu"], [puk])
                        r = f % 2
                        P.op("act", lambda e, r=r, pg=pg: e.activation(out=sg[r][:], in_=pg[:, :T], func=AF.Silu), [pgk], ["fsg%d" % r])
                        P.op("dve", lambda e, r=r, pu=pu, f=f: e.tensor_tensor(out=act[:, f, :], in0=pu[:, :T], in1=sg[r][:], op=ALU.mult),
                             [puk, "fsg%d" % r], ["fact"])
                for g in range(8):
                    wd_, wdk = wsd.get(s * 8 + g)
                    wsd.prefetch()
                    for j in range(2):
                        m = 2 * g + j
                        pb, pk = nbank()
                        for f in range(44):
                            P.op("pe", lambda e, pb=pb, f=f: e.matmul(pb[:, :T], wd_[:, f, j * 128:(j + 1) * 128], act[:, f, :],
                                                                    start=(f == 0), stop=(f == 43)), [wdk, "fact"], [pk])
                        P.op("dve", lambda e, pb=pb, m=m: e.tensor_tensor(out=hs[:, m, :], in0=pb[:, :T], in1=hs[:, m, :], op=ALU.add),
                             [pk, "fhs"], ["fhs"])
                P.dma("sp", hTv[:, :, ts0:ts0 + T], hs[:], ["fhs"], ["hT"], "fhs")

    def phase_mla_prep(self, l):
        P = self.P
        w = self.wt[l]
        with P.scope():
            qd = P.sb("dqd", [128, 4, T], F32)
            kvd = P.sb("dkvd", [128, 2, T], F32)
            sq = P.sb("dsq", [128, 4, T], BF16)
            rstd = P.sb("drstd", [128, T], F32)
            cq = P.sb("dcq", [128, 4, T], BF16)
            ckv = P.sb("dckv", [128, 2, T], BF16)
            wq = [P.sb("dwq", [128, 4, 384], BF16) for _ in range(4)]
            wkv = [P.sb("dwkv", [128, 2, 512], BF16) for _ in range(4)]
            stb = [P.sb("dstb", [128, T], BF16) for _ in range(4)]
            stv = [P.sb("dstv", [128, 3, 256], BF16) for _ in range(2)]
            zf = [P.sb("dzf", [128, T], F32) for _ in range(2)]
            t1 = [P.sb("dt1", [128, T], F32) for _ in range(2)]
            t2 = [P.sb("dt2", [128, T], F32) for _ in range(2)]
            tabs = [P.sb("dtab", [128, 2, T], F32) for _ in range(2)]
            for i in range(4):
                P.dma("sp", wq[i][:], w["uq"][i][0], [w["uq"][i][1]], ["dwq%d" % i], "dwq%d" % i)
                P.dma("sp", wkv[i][:], w["ukv"][i][0], [w["ukv"][i][1]], ["dwkv%d" % i], "dwkv%d" % i)
            cnt = {"b": 0, "stb": 0, "stv": 0, "rot": 0}

            def nbank():
                b = cnt["b"] % 7
                cnt["b"] += 1
                return self.bank(b)

            def nstb():
                i = cnt["stb"] % 4
                cnt["stb"] += 1
                return stb[i], "dstb%d" % i

            for s in range(self.NT):
                ts0 = s * T
                P.dma("sp", qd[:], self.qd[:, ts0:ts0 + T].rearrange("(c p) t -> p c t", p=128), ["qd"], ["dqd"], "dqd")
                P.dma("sp", kvd[:], self.kvd[:, ts0:ts0 + T].rearrange("(c p) t -> p c t", p=128), ["kvd"], ["dkvd"], "dkvd")
                tb = tabs[s % 2]
                tk = "dtab%d" % (s % 2)
                P.dma("sp", tb[:], self.cst["rope_tab"][:, 2:4, ts0:ts0 + T], [], [tk], tk)
                self.fm_norm(qd, "dqd", 4, T, self.vec(l, "nq", 0, 4), cq, "dcq", 512, sq, "dsq", rstd, "drstd", 7)
                self.fm_norm(kvd, "dkvd", 2, T, self.vec(l, "nkv", 0, 2), ckv, "dckv", 256, sq, "dsq", rstd, "drstd", 7)
                for gq in range(4):
                    wk = "dwq%d" % gq
                    for j in range(2):
                        h = 2 * gq + j
                        pb, pk = nbank()
                        for kc in range(4):
                            P.op("pe", lambda e, kc=kc: e.matmul(pb[:, :T], wq[gq][:, kc, j * 192:j * 192 + 128], cq[:, kc, :],
                                                                 start=(kc == 0), stop=(kc == 3)), [wk, "dcq"], [pk])
                        sb_, sk = nstb()
                        P.op("act", lambda e: e.activation(out=sb_[:], in_=pb[:, :T], func=AF.Copy), [pk], [sk])
                        P.dma("sp", self.qn[h * 128:(h + 1) * 128, ts0:ts0 + T], sb_[:], [sk], ["qn"], sk)
                    pb, pk = nbank()
                    for j in range(2):
                        for kc in range(4):
                            P.op("pe", lambda e, kc=kc: e.matmul(pb[j * 64:(j + 1) * 64, :T], wq[gq][:, kc, j * 192 + 128:j * 192 + 192],
                                                                 cq[:, kc, :], start=(kc == 0), stop=(kc == 3)), [wk, "dcq"], [pk])
                    rr = cnt["rot"] % 2
                    cnt["rot"] += 1
                    P.op("act", lambda e: e.activation(out=zf[rr][:], in_=pb[:, :T], func=AF.Copy), [pk], ["dzf%d" % rr])
                    pb2, pk2 = nbank()
                    P.op("pe", lambda e: e.matmul(pb2[:, :T], self.cs["swap64"][:], zf[rr][:], start=True, stop=True),
                         ["dzf%d" % rr, "c_swap64"], [pk2])
                    P.op("pool", lambda e: e.tensor_tensor(out=t1[rr][:], in0=zf[rr][:], in1=tb[:, 0, :], op=ALU.mult),
                         ["dzf%d" % rr, tk], ["dt1%d" % rr])
                    P.op("dve", lambda e: e.tensor_tensor(out=t2[rr][:], in0=pb2[:, :T], in1=tb[:, 1, :], op=ALU.mult),
                         [pk2, tk], ["dt2%d" % rr])
                    sb_, sk = nstb()
                    P.op("pool", lambda e: e.tensor_tensor(out=sb_[:], in0=t1[rr][:], in1=t2[rr][:], op=ALU.add),
                         ["dt1%d" % rr, "dt2%d" % rr], [sk])
                    P.dma("sp", self.qr[gq * 128:(gq + 1) * 128, ts0:ts0 + T], sb_[:], [sk], ["qr"], sk)
                for gk in range(4):
                    wk = "dwkv%d" % gk
                    for j in range(2):
                        h = 2 * gk + j
                        pb, pk = nbank()
                        for kc in range(2):
                            P.op("pe", lambda e, kc=kc: e.matmul(pb[:, :T], wkv[gk][:, kc, j * 256:j * 256 + 128], ckv[:, kc, :],
                                                                 start=(kc == 0), stop=(kc == 1)), [wk, "dckv"], [pk])
                        sb_, sk = nstb()
                        P.op("act", lambda e: e.activation(out=sb_[:], in_=pb[:, :T], func=AF.Copy), [pk], [sk])
                        P.dma("sp", self.kn[h * 128:(h + 1) * 128, ts0:ts0 + T], sb_[:], [sk], ["kn"], sk)
                    vi = cnt["stv"] % 2
                    cnt["stv"] += 1
                    vk = "dstv%d" % vi
                    for tsub in range(3):
                        pb, pk = nbank()
                        for kc in range(2):
                            rhs = wkv[gk][:, kc, :].rearrange("p (j c) -> p j c", j=2)[:, :, 128:256]
                            P.op("pe", lambda e, kc=kc, rhs=rhs: e.matmul(pb[:, :256].rearrange("p (j c) -> p j c", j=2),
                                                                        ckv[:, kc, tsub * 128:(tsub + 1) * 128], rhs,
                                                                        start=(kc == 0), stop=(kc == 1)), [wk, "dckv"], [pk])
                        P.op("dve", lambda e: e.tensor_copy(out=stv[vi][:, tsub, :], in_=pb[:, :256]), [pk], [vk])
                    P.dma("sp", self.vm[ts0:ts0 + T, gk * 256:(gk + 1) * 256].rearrange("(a p) c -> p a c", p=128), stv[vi][:],
                          [vk], ["vm"], vk)

    def phase_mla_attn(self, l):
        P = self.P
        LP, N128 = self.LP, self.N128
        scale = 192.0 ** -0.5
        with P.scope():
            kn = [P.sb("mkn", [128, LP], BF16) for _ in range(2)]
            qn = [P.sb("mqn", [128, LP], BF16) for _ in range(2)]
            vm = [P.sb("mvm", [128, N128, 128], BF16) for _ in range(2)]
            qr = [P.sb("mqr", [128, LP], BF16) for _ in range(2)]
            kr = P.sb("mkr", [128, LP], BF16)
            mask = P.sb("mmask", [128, 3, T], BF16)
            maskf = P.sb("mmaskf", [128, 3, T], F32)
            pT = [P.sb("mpT", [128, T], BF16) for _ in range(5)]
            rl = [P.sb("mrl", [128, T], F32) for _ in range(2)]
            ob = [P.sb("mob", [128, T], BF16) for _ in range(2)]
            P.dma("sp", kr[:], self.kr, ["kr"], ["mkr"], "mkr")
            P.dma("sp", maskf[:], self.cst["mla_mask"], [], ["mmaskf"], "mmaskf")
            P.op("act", lambda e: e.activation(out=mask[:], in_=maskf[:], func=AF.Copy), ["mmaskf"], ["mmask"])
            npt = [0]
            ns = [0]
            def load_head(h):
                r = h % 2
                P.dma("sp", kn[r][:], self.kn[h * 128:(h + 1) * 128, :], ["kn"], ["mkn%d" % r], "mkn%d" % r)
                P.dma("sp", qn[r][:], self.qn[h * 128:(h + 1) * 128, :], ["qn"], ["mqn%d" % r], "mqn%d" % r)
                P.dma("sp", vm[r][:], self.vm.rearrange("(n p) c -> p n c", p=128)[:, :, h * 128:(h + 1) * 128], ["vm"],
                      ["mvm%d" % r], "mvm%d" % r)
                pr = (h // 2) % 2
                if h % 2 == 0:
                    P.dma("sp", qr[pr][:], self.qr[(h // 2) * 128:(h // 2 + 1) * 128, :], ["qr"], ["mqr%d" % pr], "mqr%d" % pr)

            load_head(0)
            for h in range(8):
                r = h % 2
                b0 = r * 64
                pr = (h // 2) % 2
                if h + 1 < 8:
                    load_head(h + 1)
                for g in range(self.NT):
                    q0 = g * T
                    pbO, pkO = self.bank(4 + g % 2)
                    pbL, pkL = self.bank(6 + g % 2)
                    nk = 3 * g + 3

                    def qk(kt):
                        pbS, pkS = self.bank(ns[0] % 4)
                        ns[0] += 1
                        P.op("pe", lambda e: e.matmul(pbS[:, :T], kn[r][:, kt * 128:(kt + 1) * 128], qn[r][:, q0:q0 + T], start=True,
                                                      stop=False), ["mkn%d" % r, "mqn%d" % r], [pkS])
                        P.op("pe", lambda e: e.matmul(pbS[:, :T], kr[b0:b0 + 64, kt * 128:(kt + 1) * 128], qr[pr][b0:b0 + 64, q0:q0 + T],
                                                      start=False, stop=True), ["mkr", "mqr%d" % pr], [pkS])
                        pi = npt[0] % 5
                        npt[0] += 1
                        pk_ = "mpT%d" % pi
                        P.op("act", lambda e: e.activation(out=pT[pi][:], in_=pbS[:, :T], func=AF.Exp, scale=scale), [pkS], [pk_])
                        if kt >= 3 * g:
                            m = kt - 3 * g
                            P.op("dve", lambda e: e.tensor_tensor(out=pT[pi][:], in0=pT[pi][:], in1=mask[:, m, :], op=ALU.mult),
                                 [pk_, "mmask"], [pk_])
                        return pi, pk_

                    def pv(kt, pi, pk_):
                        P.op("pe", lambda e: e.matmul(pbO[:, :T], vm[r][:, kt, :], pT[pi][:], start=(kt == 0), stop=(kt == nk - 1)),
                             ["mvm%d" % r, pk_], [pkO])
                        P.op("pe", lambda e: e.matmul(pbL[:, :T], self.ones_bf[:], pT[pi][:], start=(kt == 0), stop=(kt == nk - 1)),
                             ["ones_bf", pk_], [pkL])

                    pend = []
                    for kt in range(nk):
                        pend.append((kt,) + qk(kt))
                        if len(pend) > 3:
                            pv(*pend.pop(0))
                    while pend:
                        pv(*pend.pop(0))
                    gi = g % 2
                    P.op("dve", lambda e: e.reciprocal(rl[gi][:], pbL[:, :T]), [pkL], ["mrl%d" % gi])
                    P.op("dve", lambda e: e.tensor_tensor(out=ob[gi][:], in0=pbO[:, :T], in1=rl[gi][:], op=ALU.mult),
                         [pkO, "mrl%d" % gi], ["mob%d" % gi])
                    P.dma("sp", self.yc[h * 128:(h + 1) * 128, q0:q0 + T], ob[gi][:], ["mob%d" % gi], ["yc"], "mob%d" % gi)

    def phase_ret(self, l):
        P = self.P
        LP, N128 = self.LP, self.N128
        lg = make_consts(128)["ret_lg"] if False else np.log1p(-np.exp2(-5.0 - np.arange(8, dtype=np.float64)))
        sc = 128.0 ** -0.5
        with P.scope():
            kT = [P.sb("rkT", [128, LP], BF16) for _ in range(2)]
            qT = [P.sb("rqT", [128, LP], BF16) for _ in range(2)]
            vv = [P.sb("rvv", [128, N128, 128], BF16) for _ in range(2)]
            tab = [P.sb("rtab", [128, 512], F32) for _ in range(2)]
            maskf = P.sb("rmaskf", [128, 3, T], F32)
            gs = [P.sb("rgs", [128, T], F32) for _ in range(2)]
            pT = [P.sb("rpT", [128, T], BF16) for _ in range(6)]
            sq = [P.sb("rsq", [128, T], BF16) for _ in range(2)]
            rstd = [P.sb("rrstd", [128, T], F32) for _ in range(2)]
            tt = [P.sb("rtt", [128, T], F32) for _ in range(2)]
            ob = [P.sb("rob", [128, T], BF16) for _ in range(2)]
            npt = [0]
            ns = [0]
            P.dma("sp", maskf[:], self.cst["mla_mask"], [], ["rmaskf"], "rmaskf")
            def load_head(h):
                r = h % 2
                P.dma("sp", kT[r][:], self.retk[h * 128:(h + 1) * 128, :], ["ret_k"], ["rkT%d" % r], "rkT%d" % r)
                P.dma("sp", qT[r][:], self.retq[h * 128:(h + 1) * 128, :], ["ret_q"], ["rqT%d" % r], "rqT%d" % r)
                P.dma("sp", vv[r][:], self.retv.rearrange("(n p) c -> p n c", p=128)[:, :, h * 128:(h + 1) * 128], ["retv"],
                      ["rvv%d" % r], "rvv%d" % r)
                P.dma("sp", tab[r][:], self.cst["ret_tab"][h], [], ["rtab%d" % r], "rtab%d" % r)
                P.op("pool", lambda e: e.tensor_tensor(out=qT[r][:].rearrange("p (g t) -> p g t", t=T), in0=qT[r][:].rearrange("p (g t) -> p g t", t=T),
                                                       in1=tab[r][:, 0:T].unsqueeze(1).to_broadcast([128, self.NT, T]), op=ALU.mult),
                     ["rqT%d" % r, "rtab%d" % r], ["rqT%d" % r])
                P.op("dve", lambda e: e.tensor_tensor(out=kT[r][:].rearrange("p (n t) -> p n t", t=128), in0=kT[r][:].rearrange("p (n t) -> p n t", t=128),
                                                      in1=tab[r][:, T:T + 128].unsqueeze(1).to_broadcast([128, N128, 128]), op=ALU.mult),
                     ["rkT%d" % r, "rtab%d" % r], ["rkT%d" % r])

            load_head(0)
            for h in range(8):
                r = h % 2
                if h + 1 < 8:
                    load_head(h + 1)
                for g in range(self.NT):
                    q0 = g * T
                    gi = g % 2
                    P.dma("sp", gs[gi][:], self.retg[h * 128:(h + 1) * 128, q0:q0 + T], ["retg"], ["rgs%d" % gi], "rgs%d" % gi)
                    pbO, pkO = self.bank(5 + gi)
                    pbL, pkL = self.bank(7)
                    nk = 3 * g + 3

                    def qk(kt):
                        pbS, pkS = self.bank(ns[0] % 5)
                        ns[0] += 1
                        P.op("pe", lambda e: e.matmul(pbS[:, :T], kT[r][:, kt * 128:(kt + 1) * 128], qT[r][:, q0:q0 + T], start=True,
                                                      stop=True), ["rkT%d" % r, "rqT%d" % r], [pkS])
                        pi = npt[0] % 6
                        npt[0] += 1
                        pk_ = "rpT%d" % pi
                        if kt >= 3 * g:
                            m = kt - 3 * g
                            c = float(np.float32(sc * math.exp(-lg[h] * 128 * m)))
                            P.op("dve", lambda e: e.scalar_tensor_tensor(out=pT[pi][:], in0=pbS[:, :T], scalar=c, in1=maskf[:, m, :],
                                                                         op0=ALU.mult, op1=ALU.mult), [pkS, "rmaskf"], [pk_])
                        else:
                            c = float(np.float32(sc * math.exp(lg[h] * (q0 - kt * 128))))
                            if kt % 2 == 0:
                                P.op("act", lambda e: e.activation(out=pT[pi][:], in_=pbS[:, :T], func=AF.Copy, scale=c), [pkS], [pk_])
                            else:
                                P.op("dve", lambda e: e.tensor_scalar(out=pT[pi][:], in0=pbS[:, :T], scalar1=c, scalar2=None, op0=ALU.mult),
                                     [pkS], [pk_])
                        return pi, pk_

                    def pv(kt, pi, pk_):
                        P.op("pe", lambda e: e.matmul(pbO[:, :T], vv[r][:, kt, :], pT[pi][:], start=(kt == 0), stop=(kt == nk - 1)),
                             ["rvv%d" % r, pk_], [pkO])

                    pend = []
                    for kt in range(nk):
                        pend.append((kt,) + qk(kt))
                        if len(pend) > 4:
                            pv(*pend.pop(0))
                    while pend:
                        pv(*pend.pop(0))
                    P.op("act", lambda e: e.activation(out=sq[gi][:], in_=pbO[:, :T], func=AF.Square), [pkO], ["rsq%d" % gi])
                    P.op("pe", lambda e: e.matmul(pbL[:, :T], self.ones_bf[:], sq[gi][:], start=True, stop=True), ["ones_bf", "rsq%d" % gi],
                         [pkL])
                    P.op("act", lambda e: e.activation(out=rstd[gi][:], in_=pbL[:, :T], func=AF.Sqrt, scale=1.0 / 128,
                                                       bias=self.eps_sb[:, 0:1]), [pkL, "eps"], ["rrstd%d" % gi])
                    P.op("dve", lambda e: e.reciprocal(rstd[gi][:], rstd[gi][:]), ["rrstd%d" % gi], ["rrstd%d" % gi])
                    P.op("dve", lambda e: e.tensor_tensor(out=tt[gi][:], in0=pbO[:, :T], in1=rstd[gi][:], op=ALU.mult),
                         [pkO, "rrstd%d" % gi], ["rtt%d" % gi])
                    P.op("pool", lambda e: e.tensor_tensor(out=ob[gi][:], in0=tt[gi][:], in1=gs[gi][:], op=ALU.mult),
                         ["rtt%d" % gi, "rgs%d" % gi], ["rob%d" % gi])
                    P.dma("sp", self.yb[h * 128:(h + 1) * 128, q0:q0 + T], ob[gi][:], ["rob%d" % gi], ["yb"], "rob%d" % gi)

    def phase_rwkv(self, l):
        P = self.P
        w = self.wt[l]
        ident = self.cs["ident"]
        with P.scope():
            f = lambda n, s, dt=F32: P.sb(n, s, dt)
            wup = f("kwup", [96, 1024]); aup = f("kaup", [96, 1024]); gup = f("kgup", [128, 2, 1024], BF16)
            lnw = f("klnw", [128, 1024]); lnb = f("klnb", [128, 1024])
            rm = f("krm", [128, 5, 128]); rst = f("krst", [128, 128]); c1 = f("kc1", [128, 8])
            S = f("kS", [128, 8, 64])
            S2 = f("kS2", [128, 8, 2, 64])
            zt = f("kzt", [128, 28, 129]); zs = f("kzs", [128, 28, 128])
            tw = f("ktw", [96, 128]); sgd = f("ksgd", [128, 2, 128], BF16)
            sgw = f("ksgw", [128, 8, 128]); aa = f("kaa", [128, 8, 128])
            rt = f("krt", [128, 8, 128]); kapt = f("kkapt", [128, 8, 128]); kt_ = f("kkt", [128, 8, 128]); bt = f("kbt", [128, 8, 128])
            PC = f("kPC", [128, 8, 2])
            khat = f("kkhat", [128, 1024]); nbhat = f("knbhat", [128, 1024]); Vtok = f("kVtok", [128, 1024]); gtok = f("kgtok", [128, 1024])
            s16 = f("ks16", [128, 16])
            Mk = f("kMk", [128, 16, 128]); Akr = f("kAkr", [128, 16, 128]); nAbr = f("knAbr", [128, 16, 128]); Xall = f("kX", [128, 16, 128])
            A = [[f("kA", [128, 4, 128]) for _ in range(2)] for _ in range(2)]; AT = [[f("kAT", [128, 4, 128]) for _ in range(2)] for _ in range(2)]; XT = [f("kXT", [128, 4, 128]) for _ in range(2)]
            tmp = [[f("ktmp", [128, 128]) for _ in range(10)] for _ in range(4)]
            RHS = f("kRHS", [128, 1024]); UT = f("kUT", [128, 1024])
            m16 = f("km16", [128, 16]); v16 = f("kv16", [128, 16])
            ztf = zt[:].rearrange("p b t -> p (b t)")
            yc = ztf[:, 0:1024]; ysq = ztf[:, 1024:2048]
            yo = f("kyo", [128, 1024], BF16); yst = f("kyst", [128, 8, 128], BF16)
            P.dma("sp", wup[:], self.w_up[l], [], ["kwup"], "kwup")
            P.dma("sp", aup[:], self.a_up[l], [], ["kaup"], "kaup")
            for i in range(2):
                P.dma("sp", gup[:, :, i * 512:(i + 1) * 512], w["gup"][i][0], [w["gup"][i][1]], ["kgup"], "kgup")
            P.dma("sp", lnw[:], self.lnw[l].partition_broadcast(128), [], ["klnw"], "klnw")
            P.dma("sp", lnb[:], self.lnb[l].partition_broadcast(128), [], ["klnb"], "klnb")
            P.dma("sp", rm[:], self.cst["rw_masks"], [], ["krm"], "krm")
            P.dma("sp", rst[:], self.cst["rw_reset"], [], ["krst"], "krst")
            P.op("dve", lambda e: e.tensor_scalar(out=c1[:], in0=self.vec(l, "k_a", 0, 8), scalar1=-1.0, scalar2=1.0, op0=ALU.mult,
                                                  op1=ALU.add), ["vecs"], ["kc1"])
            P.op("pool", lambda e: e.memset(S[:], 0.0), [], ["kS"])
            P.op("pool", lambda e: e.memset(S2[:], 0.0), [], ["kS2"])
            P.op("pool", lambda e: e.memset(RHS[:], 0.0), [], ["kRHS"])
            P.op("pool", lambda e: e.memset(UT[:], 0.0), [], ["kUT"])
            zv = self.zrw.rearrange("(b p) t -> p b t", p=128)
            yav = self.ya.rearrange("(c p) t -> p c t", p=128)
            nb = [0]

            nb_pool = [list(range(8))]

            def nbank():
                pool = nb_pool[0]
                b = pool[nb[0] % len(pool)]
                nb[0] += 1
                return self.bank(b)

            def bc(ap, shape):
                return ap.to_broadcast(shape)

            for i in range(self.N128):
                t0 = i * 128
                P.dma("sp", zt[:], zv[:, :, t0:t0 + 129], ["zrw"], ["kzt", "kyc", "kysq"], "kzt")
                P.op("dve", lambda e: e.tensor_tensor(out=zs[:], in0=zt[:, :, 0:128], in1=zt[:, :, 1:129], op=ALU.subtract), ["kzt"], ["kzs"])
                P.op("pool", lambda e: e.tensor_tensor(out=zs[:], in0=zs[:], in1=bc(self.vec(l, "mu", 0, 28).unsqueeze(2), [128, 28, 128]),
                                                       op=ALU.mult), ["kzs", "vecs"], ["kzs"])
                P.op("dve", lambda e: e.tensor_tensor(out=zs[:], in0=zs[:], in1=zt[:, :, 1:129], op=ALU.add), ["kzs", "kzt"], ["kzs"])
                P.op("act", lambda e: e.activation(out=tw[:], in_=zs[0:96, 24, :], func=AF.Tanh), ["kzs"], ["ktw"])
                P.op("act", lambda e: e.activation(out=sgd[:], in_=zs[:, 26:28, :], func=AF.Sigmoid), ["kzs"], ["ksgd"])
                if getattr(self, "rw_stop", 99) <= 1:
                    continue
                pw = [nbank(), nbank()]
                pa = [nbank(), nbank()]
                for cb in range(8):
                    pb, pk = pw[cb // 4]
                    P.op("pe", lambda e: e.matmul(pb[:, (cb % 4) * 128:(cb % 4 + 1) * 128], wup[0:96, cb * 128:(cb + 1) * 128], tw[0:96, :],
                                                  start=True, stop=True), ["kwup", "ktw"], [pk])
                for cb in range(8):
                    pb, pk = pa[cb // 4]
                    P.op("pe", lambda e: e.matmul(pb[:, (cb % 4) * 128:(cb % 4 + 1) * 128], aup[0:96, cb * 128:(cb + 1) * 128], zs[0:96, 25, :],
                                                  start=True, stop=True), ["kaup", "kzs"], [pk])
                for cb in range(8):
                    pb, pk = pw[cb // 4]
                    P.op("act", lambda e: e.activation(out=sgw[:, cb, :], in_=pb[:, (cb % 4) * 128:(cb % 4 + 1) * 128], func=AF.Sigmoid,
                                                       bias=self.vec(l, "w0", cb, 1)), [pk, "vecs"], ["ksgw"])
                    pb, pk = pa[cb // 4]
                    P.op("act", lambda e: e.activation(out=aa[:, cb, :], in_=pb[:, (cb % 4) * 128:(cb % 4 + 1) * 128], func=AF.Sigmoid,
                                                       bias=self.vec(l, "a0", cb, 1)), [pk, "vecs"], ["kaa"])
                pg = [nbank(), nbank()]
                for hf in range(2):
                    pb, pk = pg[hf]
                    for kc in range(2):
                        P.op("pe", lambda e: e.matmul(pb[:, :], sgd[:, kc, :], gup[:, kc, hf * 512:(hf + 1) * 512], start=(kc == 0), stop=(kc == 1)),
                             ["ksgd", "kgup"], [pk])
                    P.op("act", lambda e: e.activation(out=gtok[:, hf * 512:(hf + 1) * 512], in_=pb[:, :], func=AF.Copy), [pk], ["kgtok"])
                if getattr(self, "rw_stop", 99) <= 2:
                    continue
                pkh = [self.bank(0), self.bank(0)]
                pbh = [self.bank(1), self.bank(1)]
                pvv = [self.bank(2), self.bank(2)]
                pq, pqk = self.bank(3)
                ps_, psk = self.bank(4)
                def TT(cb, j):
                    return tmp[cb % 4][j], "ktmp%d_%d" % (cb % 4, j)

                def SL(cb):
                    return slice((cb % 4) * 128, (cb % 4 + 1) * 128)

                CUM, ER, EINV, EK, KK, SQ, KAP, K2, BB, RK = range(10)
                rp = lambda cb: zs[:, cb, :]
                kp = lambda cb: zs[:, 8 + cb, :]
                vp = lambda cb: zs[:, 16 + cb, :]
                t_ = lambda cb, j: TT(cb, j)[0]
                k_ = lambda cb, j: TT(cb, j)[1]
                steps = [
                    lambda cb: P.op("dve", lambda e: e.tensor_tensor_scan(out=t_(cb, CUM)[:], data0=rst[:], data1=sgw[:, cb, :], initial=0.0,
                                                                          op0=ALU.mult, op1=ALU.add), ["krst", "ksgw"], [k_(cb, CUM)]),
                    lambda cb: P.op("pool", lambda e: e.tensor_scalar(out=t_(cb, KK)[:], in0=kp(cb), scalar1=self.vec(l, "k_k", cb, 1), scalar2=None,
                                                                      op0=ALU.mult), ["kzs", "vecs"], [k_(cb, KK)]),
                    lambda cb: P.op("act", lambda e: e.activation(out=t_(cb, ER)[:], in_=t_(cb, CUM)[:], func=AF.Exp, scale=-C0), [k_(cb, CUM)], [k_(cb, ER)]),
                    lambda cb: P.op("act", lambda e: e.activation(out=t_(cb, EINV)[:], in_=t_(cb, CUM)[:], func=AF.Exp, scale=C0), [k_(cb, CUM)], [k_(cb, EINV)]),
                    lambda cb: P.op("dve", lambda e: e.tensor_tensor(out=t_(cb, EK)[:], in0=t_(cb, CUM)[:], in1=sgw[:, cb, :], op=ALU.subtract),
                                    [k_(cb, CUM), "ksgw"], [k_(cb, EK)]),
                    lambda cb: P.op("act", lambda e: e.activation(out=t_(cb, SQ)[:], in_=t_(cb, KK)[:], func=AF.Square), [k_(cb, KK)], [k_(cb, SQ)]),
                    lambda cb: P.op("pe", lambda e: e.matmul(pq[:, SL(cb)], self.cs["blkones"][:], t_(cb, SQ)[:], start=True, stop=True),
                                    [k_(cb, SQ), "c_blkones"], [pqk]),
                    lambda cb: P.op("act", lambda e: e.activation(out=t_(cb, EK)[:], in_=t_(cb, EK)[:], func=AF.Exp, scale=-C0), [k_(cb, EK)], [k_(cb, EK)]),
                    lambda cb: P.op("dve", lambda e: e.tensor_scalar(out=t_(cb, K2)[:], in0=aa[:, cb, :], scalar1=self.vec(l, "k_a", cb, 1),
                                                                     scalar2=c1[:, cb:cb + 1], op0=ALU.mult, op1=ALU.add), ["kaa", "vecs", "kc1"], [k_(cb, K2)]),
                    lambda cb: P.op("pool", lambda e: e.tensor_tensor(out=rt[:, cb, :], in0=rp(cb), in1=t_(cb, ER)[:], op=ALU.mult), ["kzs", k_(cb, ER)], ["krt"]),
                    lambda cb: P.op("act", lambda e: e.activation(out=t_(cb, SQ)[:], in_=pq[:, SL(cb)], func=AF.Sqrt), [pqk], [k_(cb, SQ)]),
                    lambda cb: P.op("dve", lambda e: e.tensor_tensor(out=t_(cb, K2)[:], in0=t_(cb, K2)[:], in1=kp(cb), op=ALU.mult), [k_(cb, K2), "kzs"], [k_(cb, K2)]),
                    lambda cb: P.op("dve", lambda e: e.tensor_scalar(out=t_(cb, SQ)[:], in0=t_(cb, SQ)[:], scalar1=1e-12, scalar2=None, op0=ALU.max),
                                    [k_(cb, SQ)], [k_(cb, SQ)]),
                    lambda cb: P.op("act", lambda e: e.activation(out=PC[:, cb, :], in_=t_(cb, ER)[:].rearrange("p (c t) -> p c t", c=2)[:, :, 63],
                                                                  func=AF.Copy), [k_(cb, ER)], ["kPC"]),
                    lambda cb: P.op("dve", lambda e: e.reciprocal(t_(cb, SQ)[:], t_(cb, SQ)[:]), [k_(cb, SQ)], [k_(cb, SQ)]),
                    lambda cb: P.op("pool", lambda e: e.tensor_tensor(out=kt_[:, cb, :], in0=t_(cb, K2)[:], in1=t_(cb, EINV)[:], op=ALU.mult),
                                    [k_(cb, K2), k_(cb, EINV)], ["kkt"]),
                    lambda cb: P.op("dve", lambda e: e.tensor_tensor(out=t_(cb, KAP)[:], in0=t_(cb, KK)[:], in1=t_(cb, SQ)[:], op=ALU.mult),
                                    [k_(cb, KK), k_(cb, SQ)], [k_(cb, KAP)]),
                    lambda cb: P.op("dve", lambda e: e.scalar_tensor_tensor(out=t_(cb, RK)[:], in0=rp(cb), scalar=self.vec(l, "r_k", cb, 1), in1=t_(cb, K2)[:],
                                                                            op0=ALU.mult, op1=ALU.mult), ["kzs", "vecs", k_(cb, K2)], [k_(cb, RK)]),
                    lambda cb: P.op("pool", lambda e: e.tensor_tensor(out=t_(cb, BB)[:], in0=aa[:, cb, :], in1=t_(cb, KAP)[:], op=ALU.mult),
                                    ["kaa", k_(cb, KAP)], [k_(cb, BB)]),
                    lambda cb: P.op("dve", lambda e: e.tensor_tensor(out=kapt[:, cb, :], in0=t_(cb, KAP)[:], in1=t_(cb, EK)[:], op=ALU.mult),
                                    [k_(cb, KAP), k_(cb, EK)], ["kkapt"]),
                    lambda cb: P.op("pe", lambda e: e.matmul(ps_[:, 2 * cb:2 * cb + 2], t_(cb, RK)[:], self.cs["blkcols"][:], start=True, stop=True),
                                    [k_(cb, RK), "c_blkcols"], [psk]),
                    lambda cb: P.op("dve", lambda e: e.tensor_tensor(out=bt[:, cb, :], in0=t_(cb, BB)[:], in1=t_(cb, EINV)[:], op=ALU.mult),
                                    [k_(cb, BB), k_(cb, EINV)], ["kbt"]),
                    lambda cb: P.op("pe", lambda e: e.transpose(pvv[cb // 4][0][:, SL(cb)], vp(cb), ident[:]), ["kzs", "c_ident"], [pvv[cb // 4][1]]),
                    lambda cb: P.op("act", lambda e: e.activation(out=t_(cb, ER)[:, 0:64], in_=kt_[:, cb, 0:64], func=AF.Identity, scale=PC[:, cb, 0:1]),
                                    ["kkt", "kPC", k_(cb, ER)], [k_(cb, ER)]),
                    lambda cb: P.op("act", lambda e: e.activation(out=t_(cb, ER)[:, 64:128], in_=kt_[:, cb, 64:128], func=AF.Identity, scale=PC[:, cb, 1:2]),
                                    ["kkt", "kPC", k_(cb, ER)], [k_(cb, ER)]),
                    lambda cb: P.op("pool", lambda e: e.tensor_scalar(out=t_(cb, EINV)[:, 0:64], in0=bt[:, cb, 0:64], scalar1=PC[:, cb, 0:1], scalar2=-1.0,
                                                                      op0=ALU.mult, op1=ALU.mult), ["kbt", "kPC", k_(cb, EINV)], [k_(cb, EINV)]),
                    lambda cb: P.op("pool", lambda e: e.tensor_scalar(out=t_(cb, EINV)[:, 64:128], in0=bt[:, cb, 64:128], scalar1=PC[:, cb, 1:2], scalar2=-1.0,
                                                                      op0=ALU.mult, op1=ALU.mult), ["kbt", "kPC", k_(cb, EINV)], [k_(cb, EINV)]),
                    lambda cb: P.op("pe", lambda e: e.transpose(pkh[cb // 4][0][:, SL(cb)], t_(cb, ER)[:], ident[:]), [k_(cb, ER), "c_ident"], [pkh[cb // 4][1]]),
                    lambda cb: P.op("pe", lambda e: e.transpose(pbh[cb // 4][0][:, SL(cb)], t_(cb, EINV)[:], ident[:]), [k_(cb, EINV), "c_ident"], [pbh[cb // 4][1]]),
                ]
                def prep_gen(batch):
                    for st in steps:
                        for cb in batch:
                            st(cb)
                        yield

                def evac_hf(hf):
                    cs_ = slice(hf * 512, hf * 512 + 512)
                    P.op("act", lambda e: e.activation(out=khat[:, cs_], in_=pkh[hf][0][:, :], func=AF.Copy), [pkh[hf][1]], ["kkhat"])
                    P.op("dve", lambda e: e.tensor_copy(out=nbhat[:, cs_], in_=pbh[hf][0][:, :]), [pbh[hf][1]], ["knbhat"])
                    P.op("act", lambda e: e.activation(out=Vtok[:, cs_], in_=pvv[hf][0][:, :], func=AF.Copy), [pvv[hf][1]], ["kVtok"])

                for _ in prep_gen(range(0, 4)):
                    pass
                evac_hf(0)
                prep_b1 = prep_gen(range(4, 8))
                if getattr(self, "rw_stop", 99) <= 3:
                    for _ in prep_b1:
                        pass
                if getattr(self, "rw_stop", 99) <= 3:
                    continue
                v3 = lambda pb: pb.rearrange("p (h t) -> p h t", h=4)
                specs = [(bt, kapt, "kbt", "kkapt"), (kapt, bt, "kkapt", "kbt"), (kt_, kapt, "kkt", "kkapt"), (kt_, rt, "kkt", "krt"),
                         (bt, rt, "kbt", "krt")]
                def dbl_gen(pair):
                    grp = []
                    for gi2 in range(2):
                        hp_, q_ = gi2, pair
                        hg = 2 * hp_ + q_
                        hs_ = slice(hp_ * 8 + 4 * q_, hp_ * 8 + 4 * q_ + 4)
                        X = Xall[:, hs_, :]
                        xk = "kX%d" % hg
                        Ag, ATg, XTg = A[gi2], AT[gi2], XT[gi2]
                        ak = lambda c, gi2=gi2: "kA%d_%d" % (gi2, c)
                        atk = lambda c, gi2=gi2: "kAT%d_%d" % (gi2, c)
                        xtk = "kXT%d" % gi2
                        dsts = [(Ag[0][:], ak(0)), (ATg[0][:], atk(0)), (Mk[:, hs_, :], "kMk"), (Akr[:, hs_, :], "kAkr"), (nAbr[:, hs_, :], "knAbr")]
                        for mi, (la, ra, lk, rk_) in enumerate(specs):
                            pb, pk = nbank()
                            for hh in range(4):
                                cb, b0 = 4 * q_ + hh, hp_ * 64
                                P.op("pe", lambda e: e.matmul(pb[:, hh * 128:(hh + 1) * 128], la[b0:b0 + 64, cb, :], ra[b0:b0 + 64, cb, :], start=True,
                                                              stop=True), [lk, rk_], [pk])
                            dst, dk = dsts[mi]
                            P.op("dve", lambda e: e.tensor_tensor(out=dst, in0=v3(pb), in1=bc(rm[:, mi, :].unsqueeze(1), [128, 4, 128]), op=ALU.mult),
                                 [pk, "krm"], [dk])
                        P.op("pool", lambda e: e.tensor_tensor(out=X, in0=Ag[0][:], in1=bc(ident[:].unsqueeze(1), [128, 4, 128]), op=ALU.add),
                             [ak(0), "c_ident"], [xk])
                        P.op("pool", lambda e: e.tensor_tensor(out=XTg[:], in0=ATg[0][:], in1=bc(ident[:].unsqueeze(1), [128, 4, 128]), op=ALU.add),
                             [atk(0), "c_ident"], [xtk])
                        grp.append((X, xk, Ag, ATg, XTg, ak, atk, xtk))
                        yield
                    cur = 0
                    for lev in range(1, 6):
                        last = (lev == 5)
                        for (X, xk, Ag, ATg, XTg, ak, atk, xtk) in grp:
                            ca, cat = Ag[cur], ATg[cur]
                            p1, p1k = nbank()
                            for hh in range(4):
                                P.op("pe", lambda e: e.matmul(p1[:, hh * 128:(hh + 1) * 128], cat[:, hh, :], ca[:, hh, :], start=True, stop=True),
                                     [ak(cur), atk(cur)], [p1k])
                            if not last:
                                p2, p2k = nbank()
                                for hh in range(4):
                                    P.op("pe", lambda e: e.matmul(p2[:, hh * 128:(hh + 1) * 128], ca[:, hh, :], cat[:, hh, :], start=True, stop=True),
                                         [ak(cur), atk(cur)], [p2k])
                            P.op("act", lambda e: e.activation(out=Ag[1 - cur][:], in_=v3(p1), func=AF.Copy), [p1k], [ak(1 - cur)])
                            if not last:
                                P.op("act", lambda e: e.activation(out=ATg[1 - cur][:], in_=v3(p2), func=AF.Copy), [p2k], [atk(1 - cur)])
                        yield
                        for (X, xk, Ag, ATg, XTg, ak, atk, xtk) in grp:
                            na = Ag[1 - cur]
                            p3, p3k = nbank()
                            for hh in range(4):
                                P.op("pe", lambda e: e.matmul(p3[:, hh * 128:(hh + 1) * 128], XTg[:, hh, :], na[:, hh, :], start=True, stop=True),
                                     [xtk, ak(1 - cur)], [p3k])
                            if not last:
                                p4, p4k = nbank()
                                for hh in range(4):
                                    P.op("pe", lambda e: e.matmul(p4[:, hh * 128:(hh + 1) * 128], na[:, hh, :], XTg[:, hh, :], start=True, stop=True),
                                         [xtk, ak(1 - cur)], [p4k])
                            P.op("dve", lambda e: e.tensor_tensor(out=X, in0=v3(p3), in1=X, op=ALU.add), [p3k, xk], [xk])
                            if not last:
                                P.op("dve", lambda e: e.tensor_tensor(out=XTg[:], in0=v3(p4), in1=XTg[:], op=ALU.add), [p4k, xtk], [xtk])
                        cur = 1 - cur
                        yield

                nb_pool[0] = [5, 6, 7]
                g1, g2 = dbl_gen(0), prep_b1
                d1 = d2 = False
                while not (d1 and d2):
                    if not d1:
                        try:
                            next(g1)
                        except StopIteration:
                            d1 = True
                    if not d2:
                        try:
                            next(g2)
                        except StopIteration:
                            d2 = True
                evac_hf(1)
                P.op("dve", lambda e: e.tensor_copy(out=s16[:], in_=ps_[:, 0:16]), [psk], ["ks16"])
                nb_pool[0] = list(range(8))
                for _ in dbl_gen(1):
                    pass
                if getattr(self, "rw_stop", 99) <= 4:
                    continue
                pR = [nbank(), nbank()]
                pU = [nbank(), nbank()]
                pY = [nbank(), nbank()]
                pS, pSk = nbank()
                for cp in range(2):
                    c0 = cp * 64
                    cr = slice(c0, c0 + 64)
                    for h in range(16):
                        cb, hp = h // 2, h % 2
                        hsx = hp * 8 + cb
                        pb, pk = pR[h // 8]
                        hc = slice((h % 8) * 64, (h % 8) * 64 + 64)
                        hcf = slice(h * 64, h * 64 + 64)
                        P.op("pe", lambda e: e.matmul(pb[cr, hc], kapt[:, cb, cr], S2[:, cb, hp, :], start=True, stop=False),
                             ["kkapt", "kS2"], [pk])
                        P.op("pe", lambda e: e.matmul(pb[cr, hc], Mk[:, hsx, cr], Vtok[:, hcf], start=False, stop=True), ["kMk", "kVtok"], [pk])
                    P.op("act", lambda e: e.activation(out=RHS[cr, 0:512], in_=pR[0][0][cr, :], func=AF.Copy), [pR[0][1]], ["kRHS"])
                    P.op("dve", lambda e: e.tensor_copy(out=RHS[cr, 512:1024], in_=pR[1][0][cr, :]), [pR[1][1]], ["kRHS"])
                    for h in range(16):
                        hsx = (h % 2) * 8 + h // 2
                        pb, pk = pU[h // 8]
                        hc = slice((h % 8) * 64, (h % 8) * 64 + 64)
                        hcf = slice(h * 64, h * 64 + 64)
                        P.op("pe", lambda e: e.matmul(pb[cr, hc], Xall[:, hsx, cr], RHS[:, hcf], start=True, stop=True), ["kX0", "kX1", "kX2", "kX3", "kRHS"], [pk])
                    P.op("act", lambda e: e.activation(out=UT[cr, 0:512], in_=pU[0][0][cr, :], func=AF.Copy), [pU[0][1]], ["kUT"])
                    P.op("dve", lambda e: e.tensor_copy(out=UT[cr, 512:1024], in_=pU[1][0][cr, :]), [pU[1][1]], ["kUT"])
                    for h in range(16):
                        cb, hp = h // 2, h % 2
                        hsx = hp * 8 + cb
                        pb, pk = pY[h // 8]
                        hc = slice((h % 8) * 64, (h % 8) * 64 + 64)
                        hcf = slice(h * 64, h * 64 + 64)
                        P.op("pe", lambda e: e.matmul(pb[cr, hc], rt[:, cb, cr], S2[:, cb, hp, :], start=True, stop=False),
                             ["krt", "kS2"], [pk])
                        P.op("pe", lambda e: e.matmul(pb[cr, hc], Akr[:, hsx, cr], Vtok[:, hcf], start=False, stop=False), ["kAkr", "kVtok"], [pk])
                        P.op("pe", lambda e: e.matmul(pb[cr, hc], nAbr[:, hsx, cr], UT[:, hcf], start=False, stop=True), ["knAbr", "kUT"], [pk])
                    for h in range(16):
                        cb, b0 = h // 2, (h % 2) * 64
                        hcf = slice(h * 64, h * 64 + 64)
                        P.op("pe", lambda e: e.matmul(pS[b0:b0 + 64, cb * 64:(cb + 1) * 64], khat[cr, hcf], Vtok[cr, hcf], start=True, stop=False),
                             ["kkhat", "kVtok"], [pSk])
                        P.op("pe", lambda e: e.matmul(pS[b0:b0 + 64, cb * 64:(cb + 1) * 64], nbhat[cr, hcf], UT[cr, hcf], start=False, stop=True),
                             ["knbhat", "kUT"], [pSk])
                    for cb in range(8):
                        P.op("dve", lambda e: e.scalar_tensor_tensor(out=S[:, cb, :], in0=S[:, cb, :], scalar=PC[:, cb, cp:cp + 1],
                                                                     in1=pS[:, cb * 64:(cb + 1) * 64], op0=ALU.mult, op1=ALU.add),
                             ["kS", "kPC", pSk], ["kS"])
                    P.op("act", lambda e: e.activation(out=S2[0:64, :, 0, :], in_=S[0:64, :, :], func=AF.Copy), ["kS"], ["kS2"])
                    P.op("pool", lambda e: e.tensor_copy(out=S2[64:128, :, 1, :], in_=S[64:128, :, :]), ["kS"], ["kS2"])
                if getattr(self, "rw_stop", 99) <= 5:
                    continue
                for hf in range(2):
                    pb, pk = pY[hf]
                    P.op("dve", lambda e: e.tensor_reduce(out=m16[:, hf * 8:(hf + 1) * 8], in_=pb.rearrange("p (h v) -> p h v", h=8), axis=AX.X,
                                                          op=ALU.add), [pk], ["km16"])
                P.op("dve", lambda e: e.tensor_scalar(out=m16[:], in0=m16[:], scalar1=1.0 / 64, scalar2=None, op0=ALU.mult), ["km16"], ["km16"])
                for hf in range(2):
                    pb, pk = pY[hf]
                    P.op("dve", lambda e: e.tensor_tensor(out=yc[:, hf * 512:(hf + 1) * 512].rearrange("p (h v) -> p h v", h=8),
                                                          in0=pb.rearrange("p (h v) -> p h v", h=8),
                                                          in1=bc(m16[:, hf * 8:(hf + 1) * 8].unsqueeze(2), [128, 8, 64]), op=ALU.subtract),
                         [pk, "km16"], ["kyc"])
                P.op("act", lambda e: e.activation(out=ysq[:], in_=yc[:], func=AF.Square), ["kyc"], ["kysq"])
                P.op("dve", lambda e: e.tensor_reduce(out=v16[:], in_=ysq[:].rearrange("p (h v) -> p h v", h=16), axis=AX.X, op=ALU.add),
                     ["kysq"], ["kv16"])
                P.op("act", lambda e: e.activation(out=v16[:], in_=v16[:], func=AF.Sqrt, scale=1.0 / 64, bias=self.eps_sb[:, 1:2]),
                     ["kv16", "eps"], ["kv16"])
                P.op("dve", lambda e: e.reciprocal(v16[:], v16[:]), ["kv16"], ["kv16"])
                y3 = lambda t_: t_[:].rearrange("p (h v) -> p h v", h=16)
                P.op("dve", lambda e: e.tensor_tensor(out=y3(yc), in0=y3(yc), in1=bc(v16[:].unsqueeze(2), [128, 16, 64]), op=ALU.mult),
                     ["kyc", "kv16"], ["kyc"])
                P.op("pool", lambda e: e.tensor_tensor(out=yc[:], in0=yc[:], in1=lnw[:], op=ALU.mult), ["kyc", "klnw"], ["kyc"])
                P.op("pool", lambda e: e.tensor_tensor(out=yc[:], in0=yc[:], in1=lnb[:], op=ALU.add), ["kyc", "klnb"], ["kyc"])
                P.op("dve", lambda e: e.tensor_tensor(out=y3(ysq), in0=y3(Vtok), in1=bc(s16[:].unsqueeze(2), [128, 16, 64]), op=ALU.mult),
                     ["kVtok", "ks16", "kysq"], ["kysq"])
                P.op("pool", lambda e: e.tensor_tensor(out=yc[:], in0=yc[:], in1=ysq[:], op=ALU.add), ["kyc", "kysq"], ["kyc"])
                P.op("dve", lambda e: e.tensor_tensor(out=yo[:], in0=yc[:], in1=gtok[:], op=ALU.mult), ["kyc", "kgtok"], ["kyo"])
                pt_, ptk = nbank()
                ptb = pt_.bitcast(BF16)
                for cb in range(8):
                    P.op("pe", lambda e: e.transpose(ptb[:, cb * 128:(cb + 1) * 128], yo[:, cb * 128:(cb + 1) * 128], self.ident_bf[:]),
                         ["kyo", "ident_bf"], [ptk])
                P.op("act", lambda e: e.activation(out=yst[:], in_=ptb.rearrange("p (c t) -> p c t", c=8), func=AF.Copy), [ptk], ["kyst"])
                P.dma("sp", yav[:, :, t0:t0 + 128], yst[:], ["kyst"], ["ya"], "kyst")


def common_inputs(inp, NL, LP):
    m = {}
    for n in ["w_in", "rw_w_up", "rw_a_up", "rw_g_up", "mla_w_uq", "mla_w_ukv", "w_br_rwkv", "w_br_ret", "w_br_mla", "w_out",
              "w_gate_up", "w_down", "rw_ln_w", "rw_ln_b"]:
        m[n] = np.ascontiguousarray(np.asarray(inp[n], np.float32)[:NL])
    m["vecs"] = np.stack([pack_vecs(inp, l) for l in range(NL)], 0)
    m["final_norm_pm"] = pm(inp["final_norm"], 16)
    m["meta"] = np.ascontiguousarray(np.asarray(inp["meta_tokens"], np.float32))
    cs = make_consts(LP)
    for n in CONST_NAMES:
        m["c_" + n] = np.ascontiguousarray(cs[n].astype(np.float32))
    return m


def build_full(L, NL):
    B = Builder(L, NL)
    B.setup_globals()
    B.convert_layer(0)
    B.prologue()
    for l in range(NL):
        B.phase_inproj(l)
        B.phase_rwkv(l)
        B.phase_ret(l)
        B.phase_mla_prep(l)
        B.phase_mla_attn(l)
        B.phase_merge(l)
        if l + 1 < NL:
            B.convert_layer(l + 1)
        B.phase_ffn(l)
    B.epilogue()
    B.P.finish()
    return B


def kernel(**inputs):
    inp = {k: np.asarray(v) for k, v in inputs.items()}
    nb, seq, _ = inp["x"].shape
    L = seq + NMETA
    NL = inp["w_in"].shape[0]
    B = build_full(L, NL)
    cm = common_inputs(inp, NL, B.LP)
    n_cores = 8
    in_maps = []
    for c in range(n_cores):
        m = dict(cm)
        m["x"] = np.ascontiguousarray(inp["x"][c % nb], dtype=np.float32)
        in_maps.append(m)
    res = run_bass_kernel_spmd(B.nc, in_maps, core_ids=list(range(n_cores)))
    out = np.stack([np.asarray(res.results[b]["out"], np.float32) for b in range(nb)], 0)
    return out
```
